# Optimizing a Trainium2 kernel written in Bass

```python
import math
import jax, jax.numpy as jnp
from jax import lax
import numpy as np

D_MODEL = 1024
BATCH = 4
SEQ = 8192
DEPTH = 1

D_MIX = D_MODEL
RET_HEADS = 4
RET_HEAD_DIM = (D_MIX // 2) // RET_HEADS
RET_WIDTH = RET_HEADS * RET_HEAD_DIM
RET_CHUNK = 128
ATT_HEADS = 8
ATT_HEAD_DIM = (D_MIX - RET_WIDTH) // ATT_HEADS
ATT_WIDTH = ATT_HEADS * ATT_HEAD_DIM
DILATED_PATTERN = ((128, 1), (512, 4), (2048, 16))
BAND_BLOCK = 128
D_FF = 2816
ROPE_BASE = 10000.0
NORM_EPS = 1e-6
GN_EPS = 1e-6
IN_COLS = 4 * RET_WIDTH + 3 * ATT_WIDTH

kernel_name = "hybrid_retention_dilated_macaron"


def rms_norm(x, g):
    xf = x.astype(jnp.float32)
    y = xf * lax.rsqrt(jnp.mean(xf * xf, axis=-1, keepdims=True) + NORM_EPS)
    return (y * g.astype(jnp.float32)).astype(x.dtype)


def swiglu(x, w_gate, w_up, w_down):
    return (jax.nn.silu(x @ w_gate) * (x @ w_up)) @ w_down


def split_heads(t, n_heads):
    b, s, _ = t.shape
    return t.reshape(b, s, n_heads, -1).transpose(0, 2, 1, 3)


def rotate_every_two(t):
    t1 = t[..., ::2]
    t2 = t[..., 1::2]
    return jnp.stack((-t2, t1), axis=-1).reshape(t.shape)


def apply_rotary(t):
    s, d = t.shape[-2], t.shape[-1]
    pos = jnp.arange(s, dtype=jnp.float32)
    inv_freq = ROPE_BASE ** (-jnp.arange(0, d, 2, dtype=jnp.float32) / d)
    ang = jnp.repeat(pos[:, None] * inv_freq[None, :], 2, axis=-1)
    cos = jnp.cos(ang).astype(t.dtype)
    sin = jnp.sin(ang).astype(t.dtype)
    return t * cos + rotate_every_two(t) * sin


def chunkwise_retention(q, k, v):
    b, h, s, dk = q.shape
    dv = v.shape[-1]
    c = RET_CHUNK
    n = s // c
    dt = q.dtype
    log_g = jnp.log(1.0 - 2.0 ** (-5.0 - jnp.arange(h, dtype=jnp.float32)))
    idx = jnp.arange(c, dtype=jnp.float32)
    rel = idx[:, None] - idx[None, :]
    decay_in = jnp.where(rel >= 0, jnp.exp(log_g[:, None, None] * jnp.maximum(rel, 0.0)), 0.0)
    zeta = jnp.exp(log_g[:, None] * (c - 1 - idx)[None, :])
    xi = jnp.exp(log_g[:, None] * (idx + 1)[None, :])
    chunk_decay = jnp.exp(log_g * c)
    qc = q.reshape(b, h, n, c, dk)
    kc = k.reshape(b, h, n, c, dk)
    vc = v.reshape(b, h, n, c, dv)
    scores = jnp.einsum('bhncd,bhnmd->bhncm', qc, kc) * decay_in[None, :, None].astype(dt)
    intra = jnp.einsum('bhncm,bhnme->bhnce', scores, vc)
    kv = jnp.einsum('bhncd,bhnce->nbhde', kc * zeta[None, :, None, :, None].astype(dt), vc)
    gamma_c = chunk_decay[None, :, None, None].astype(dt)

    def step(state, kv_n):
        return state * gamma_c + kv_n, state

    _, prev_states = lax.scan(step, jnp.zeros((b, h, dk, dv), dt), kv)
    inter = jnp.einsum('bhncd,nbhde->bhnce', qc, prev_states) * xi[None, :, None, :, None].astype(dt)
    return (intra + inter).reshape(b, h, s, dv)


def dilated_window_branch(q, k, v, window, dilation):
    b, h, s, hd = q.shape
    L = s // dilation
    span = window // dilation
    blk = BAND_BLOCK
    nb = -(-L // blk)
    lp = nb * blk

    def to_blocks(t):
        t = t.reshape(b, h, L, dilation, hd).transpose(0, 1, 3, 2, 4)
        t = jnp.pad(t, ((0, 0), (0, 0), (0, 0), (0, lp - L), (0, 0)))
        return t.reshape(b, h, dilation, nb, blk, hd)

    def with_prev(t):
        prev = jnp.pad(t[:, :, :, :-1], ((0, 0), (0, 0), (0, 0), (1, 0), (0, 0), (0, 0)))
        return jnp.concatenate([prev, t], axis=4)

    qb = to_blocks(q)
    kb = with_prev(to_blocks(k))
    vb = with_prev(to_blocks(v))
    sc = jnp.einsum('bhgnqd,bhgnkd->bhgnqk', qb, kb).astype(jnp.float32)
    qi = jnp.arange(blk)[:, None]
    kj = jnp.arange(2 * blk)[None, :]
    dist = qi + blk - kj
    bidx = jnp.arange(nb)[:, None, None]
    mask = (dist >= 0) & (dist <= span) & (bidx * blk + kj - blk >= 0)
    sc = jnp.where(mask, sc, -jnp.inf)
    m = jnp.max(sc, axis=-1, keepdims=True)
    e = jnp.exp(sc - m)
    denom = jnp.sum(e, axis=-1, keepdims=True)
    lse = (m + jnp.log(denom))[..., 0]
    o = jnp.einsum('bhgnqk,bhgnkd->bhgnqd', (e / denom).astype(v.dtype), vb)
    o = o.reshape(b, h, dilation, lp, hd)[:, :, :, :L].transpose(0, 1, 3, 2, 4).reshape(b, h, s, hd)
    lse = lse.reshape(b, h, dilation, lp)[..., :L].transpose(0, 1, 3, 2).reshape(b, h, s)
    return o, lse


def dilated_attention(q, k, v):
    outs, lses = [], []
    for window, dilation in DILATED_PATTERN:
        o, l = dilated_window_branch(q, k, v, window, dilation)
        outs.append(o)
        lses.append(l)
    wts = jax.nn.softmax(jnp.stack(lses, axis=0), axis=0)
    return jnp.einsum('pbhs,pbhsd->bhsd', wts.astype(q.dtype), jnp.stack(outs, axis=0))


def setup_inputs(seed: int = 0) -> dict:
    key = jax.random.key(seed)
    ks = jax.random.split(key, 16)
    f32 = jnp.float32

    def w(k_, shape, fan_in):
        return jax.random.normal(k_, shape, f32) * (fan_in ** -0.5)

    def gain(k_, shape):
        return 1.0 + 0.02 * jax.random.normal(k_, shape, f32)

    return {
        "x": jax.random.normal(ks[0], (BATCH, SEQ, D_MODEL), f32),
        "norm_ffn1": gain(ks[1], (DEPTH, D_MODEL)),
        "ffn1_w_gate": w(ks[2], (DEPTH, D_MODEL, D_FF), D_MODEL),
        "ffn1_w_up": w(ks[3], (DEPTH, D_MODEL, D_FF), D_MODEL),
        "ffn1_w_down": w(ks[4], (DEPTH, D_FF, D_MODEL), D_FF),
        "norm_mix": gain(ks[5], (DEPTH, D_MODEL)),
        "w_in": w(ks[6], (DEPTH, D_MODEL, IN_COLS), D_MODEL),
        "ret_norm_gain": gain(ks[7], (DEPTH, RET_WIDTH)),
        "w_out": w(ks[8], (DEPTH, D_MIX, D_MODEL), D_MIX),
        "norm_ffn2": gain(ks[9], (DEPTH, D_MODEL)),
        "ffn2_w_gate": w(ks[10], (DEPTH, D_MODEL, D_FF), D_MODEL),
        "ffn2_w_up": w(ks[11], (DEPTH, D_MODEL, D_FF), D_MODEL),
        "ffn2_w_down": w(ks[12], (DEPTH, D_FF, D_MODEL), D_FF),
        "norm_final": gain(ks[13], (D_MODEL,)),
    }


def reference(x, norm_ffn1, ffn1_w_gate, ffn1_w_up, ffn1_w_down, norm_mix, w_in, ret_norm_gain,
              w_out, norm_ffn2, ffn2_w_gate, ffn2_w_up, ffn2_w_down, norm_final):
    b, s, _ = x.shape
    h = x
    for l in range(DEPTH):
        h = h + 0.5 * swiglu(rms_norm(h, norm_ffn1[l]), ffn1_w_gate[l], ffn1_w_up[l], ffn1_w_down[l])

        u = rms_norm(h, norm_mix[l]) @ w_in[l]
        rq, rk, rv, rg, aq, ak, av = jnp.split(
            u, np.cumsum([RET_WIDTH] * 4 + [ATT_WIDTH] * 2).tolist(), axis=-1)

        rq = apply_rotary(split_heads(rq, RET_HEADS))
        rk = apply_rotary(split_heads(rk, RET_HEADS)) * (RET_HEAD_DIM ** -0.5)
        ret = chunkwise_retention(rq, rk, split_heads(rv, RET_HEADS)).astype(jnp.float32)
        mu = jnp.mean(ret, axis=-1, keepdims=True)
        var = jnp.mean(jnp.square(ret - mu), axis=-1, keepdims=True)
        ret = ((ret - mu) * lax.rsqrt(var + GN_EPS)).transpose(0, 2, 1, 3).reshape(b, s, RET_WIDTH)
        ret = (ret * ret_norm_gain[l].astype(jnp.float32)).astype(x.dtype) * jax.nn.silu(rg)

        att = dilated_attention(split_heads(aq, ATT_HEADS) * (ATT_HEAD_DIM ** -0.5),
                                split_heads(ak, ATT_HEADS), split_heads(av, ATT_HEADS))
        att = att.transpose(0, 2, 1, 3).reshape(b, s, ATT_WIDTH)

        h = h + jnp.concatenate([ret, att], axis=-1) @ w_out[l]

        h = h + 0.5 * swiglu(rms_norm(h, norm_ffn2[l]), ffn2_w_gate[l], ffn2_w_up[l], ffn2_w_down[l])
    return rms_norm(h, norm_final)
```

```python
import numpy as np
import ml_dtypes
import concourse.bass as bass
import concourse.mybir as mybir
from concourse.bass_utils import run_bass_kernel_spmd

F32 = mybir.dt.float32
BF16 = mybir.dt.bfloat16
ALU = mybir.AluOpType
AF = mybir.ActivationFunctionType
AX = mybir.AxisListType

D = 1024
DFF = 2816
NMC = DFF // 128
T = 512
NSLOT = 5
NCV = 12
NRING = 20
VW = 768
NMASK = 8
EPS = 1e-6
INCOLS = 3584


class _Op:
    __slots__ = ("eng", "fn", "deps", "inc", "dma", "val")


class Sched:
    ENGS = ("pe", "act", "dve", "pool", "sp")

    def __init__(self):
        self.q = {e: [] for e in self.ENGS}
        self.lastw = {}
        self.readers = {}
        self.dma_n = {}

    def add(self, eng, fn, reads=(), writes=(), dma=None):
        op = _Op()
        op.eng = eng
        op.fn = fn
        op.dma = dma
        op.inc = False
        op.val = 0
        deps = {}
        for k in reads:
            w = self.lastw.get(k)
            if w is not None:
                deps[id(w)] = w
        for k in writes:
            w = self.lastw.get(k)
            if w is not None and (w.dma is not None or dma is not None or w.eng != eng or eng != "pe"):
                deps[id(w)] = w
            for r in self.readers.get(k, ()):
                if r.dma is not None or dma is not None or r.eng != eng:
                    deps[id(r)] = r
        op.deps = list(deps.values())
        for o in op.deps:
            o.inc = True
        for k in reads:
            self.readers.setdefault(k, []).append(op)
        for k in writes:
            self.lastw[k] = op
            self.readers[k] = []
        if dma is not None:
            n = self.dma_n.get(dma, 0) + 1
            self.dma_n[dma] = n
            op.val = 16 * n
            op.inc = True
        self.q[eng].append(op)
        return op

    def emit(self, nc, block):
        esem = {e: nc.alloc_semaphore("sem_" + e) for e in self.ENGS}
        dsem = {k: nc.alloc_semaphore("dsem_%d" % i) for i, k in enumerate(self.dma_n)}
        for e in self.ENGS:
            c = 0
            for op in self.q[e]:
                if op.dma is None and op.inc:
                    c += 1
                    op.val = c

        def semof(op):
            return dsem[op.dma] if op.dma is not None else esem[op.eng]

        def run(e, eng):
            seen = {}
            for op in self.q[e]:
                need = {}
                for d in op.deps:
                    s = semof(d)
                    if need.get(s.num, (None, 0))[1] < d.val:
                        need[s.num] = (s, d.val)
                for num, (s, v) in need.items():
                    if seen.get(num, 0) < v:
                        eng.wait_ge(s, v)
                        seen[num] = v
                ins = op.fn(eng)
                if op.inc:
                    ins.then_inc(semof(op), 16 if op.dma is not None else 1)

        @block.tensor
        def _(eng):
            run("pe", eng)

        @block.scalar
        def _(eng):
            run("act", eng)

        @block.vector
        def _(eng):
            run("dve", eng)

        @block.gpsimd
        def _(eng):
            run("pool", eng)

        @block.sync
        def _(eng):
            run("sp", eng)


def _gammas():
    return 1.0 - 2.0 ** (-5.0 - np.arange(4, dtype=np.float64))


def _mask_tables():
    near = np.zeros((8, 128, 512), np.float32)
    i = np.arange(128)[:, None]
    j = np.arange(512)[None, :]
    for d in (-1, 0):
        for g in range(4):
            dist = j - (512 * d + g + 4 * i)
            m = ((dist >= 0) & (dist <= 128)).astype(np.float32)
            m += ((dist >= 0) & (dist % 4 == 0) & (dist <= 512)).astype(np.float32)
            m += ((dist >= 0) & (dist % 16 == 0) & (dist <= 2048)).astype(np.float32)
            near[(d + 1) * 4 + g] = m
    far = np.zeros((2, 128, 128), np.float32)
    ip = np.arange(128)[None, :]
    for k, d in enumerate((-4, -3)):
        dist = 4 * (ip - i) - 512 * d
        far[k] = ((dist % 16 == 0) & (dist <= 2048) & (dist >= 0)).astype(np.float32)
    dist2 = 4 * (ip - i) + 1024
    assert np.array_equal(((dist2 % 16 == 0) & (dist2 <= 2048)).astype(np.float32), far[1])
    assert dist2.min() > 512
    return near, far


def _consts():
    g = _gammas()
    c = np.arange(128, dtype=np.float64)
    decq = np.stack([g[h] ** (c + 1) for h in range(4)], axis=1)
    deck = np.stack([(128.0 ** -0.5) * g[h] ** (-(c + 1)) for h in range(4)], axis=1)
    g128 = np.repeat(g ** 128, 128)[None, :].repeat(128, axis=0)
    caus = (np.arange(128)[None, :] >= np.arange(128)[:, None]).astype(np.float32)
    ident = np.eye(128, dtype=np.float32)
    shif = np.zeros((128, 128), np.float32)
    for k in range(64):
        shif[k, 64 + k] = 1.0
    small = np.zeros((128, 8), np.float32)
    small[:, 0:4] = decq
    small[:, 4:8] = deck
    near, far = _mask_tables()
    return dict(small=small, g128=g128.astype(np.float32), caus=caus, ident=ident, shif=shif,
                masks=near, fmasks=far)


def _rope_tables(pos):
    pos = pos.astype(np.float32)
    inv = (10000.0 ** (-np.arange(0, 128, 2, dtype=np.float32) / 128.0)).astype(np.float32)
    ang = (pos[:, None] * inv[None, :]).astype(np.float32)
    cos = np.repeat(np.cos(ang).astype(np.float32), 2, axis=1)
    sin = np.repeat(np.sin(ang).astype(np.float32), 2, axis=1)
    sgn = np.tile(np.array([-1.0, 1.0], np.float32), 64)[None, :]
    return np.ascontiguousarray(cos), np.ascontiguousarray(sin * sgn)


def _units():
    units = []
    for f in (0, 1):
        for mc in range(NMC):
            units.append(("GU", f, mc))
        for cg in range(2):
            for q in range(6):
                units.append(("DN", f, cg, q))
    for g in range(5):
        for half in range(2):
            units.append(("TM", g, half))
    for i in range(4):
        units.append(("FM", i))
    for cg in range(2):
        for half in range(2):
            units.append(("OUT", cg, half))
    return units


UNITS = _units()
UIDX = {u: i for i, u in enumerate(UNITS)}
TM_COL = {0: 0, 1: 512, 2: 1024, 3: 1536, 4: 3072}


def build(nh=8, no=8, dbg=()):
    nc = bass.Bass("TRN2", target_bir_lowering=False)
    sc = Sched()
    ntile = nh + no
    ntok = ntile * T

    def din(name, shape, dt=F32):
        return nc.dram_tensor(name, list(shape), dt, kind="ExternalInput")

    xin = din("xin", [ntok, D])
    cosd = din("cosd", [ntok, 128])
    ssd = din("ssd", [ntok, 128])
    wd = {
        ("g", 0): din("w_g1", [D, DFF]), ("u", 0): din("w_u1", [D, DFF]), ("d", 0): din("w_d1", [DFF, D]),
        ("g", 1): din("w_g2", [D, DFF]), ("u", 1): din("w_u2", [D, DFF]), ("d", 1): din("w_d2", [DFF, D]),
    }
    w_in = din("w_in", [D, INCOLS])
    w_out = din("w_out", [D, D])
    gains_d = din("gains", [4, 128, D])
    rgain_d = din("rgain", [128, 512])
    small_d = din("small", [128, 8])
    g128_d = din("g128", [128, 512])
    caus_d = din("caus", [128, 128])
    ident_d = din("ident", [128, 128])
    shif_d = din("shif", [128, 128])
    masks_d = din("masks", [NMASK, 128, 512])
    fmasks_d = din("fmasks", [2, 128, 128])
    flags_d = din("flags", [128, 64])
    yout = nc.dram_tensor("yout", [no * T, D], F32, kind="ExternalOutput")
    wsc = nc.dram_tensor("wsc", [len(UNITS), 128, 2048], BF16)
    dbg_out = {}
    for name, shape in dbg:
        dbg_out[name] = nc.dram_tensor("dbg_" + name, list(shape), F32, kind="ExternalOutput")

    def sb(name, cols, dt):
        return nc.alloc_sbuf_tensor("s_" + name, [128, cols], dt)

    xt = [sb("xt0", 4096, F32), sb("xt1", 4096, F32)]
    Vb = sb("Vb", 4096, BF16)
    xnT = sb("xnT", 4096, BF16)
    U = sb("U", 24 * 512, BF16)
    sil = [sb("sil0", 512, F32), sb("sil1", 512, F32)]
    sil_bf = [t_.bitcast(BF16) for t_ in sil]
    rt = [sb("rt0", 512, F32), sb("rt1", 512, F32)]
    aqT = sb("aqT", 8 * 512, BF16)
    kring = sb("kring", 4 * NRING * 128, BF16)
    vring = sb("vring", NRING * VW, BF16)
    masks = sb("masks", NMASK * 512, BF16)
    fmasks = sb("fmasks", 2 * 128, BF16)
    eb = [sb("eb%d" % i_, 512, BF16) for i_ in range(4)]
    wr = sb("wr", NSLOT * 2048, BF16)
    tab = sb("tab", 2 * 512, F32)
    gn = sb("gn", 2 * 1024, F32)
    rgain = sb("rgain_sb", 512, F32)
    Sf = sb("Sf", 512, F32)
    Sb = sb("Sb", 6 * 512, BF16)
    retf = sb("retf", 512, F32)
    sqf = sb("sqf", 512, F32)
    rsb = sb("rsb2", 512, F32)
    lnd = sb("lnd", 512, F32)
    scb = [sb("scb0", 512, BF16), sb("scb1", 512, BF16)]
    Rb = sb("Rb", 512, BF16)
    ident = sb("ident_sb", 128, BF16)
    shif = sb("shif_sb", 128, BF16)
    caus = sb("caus_sb", 128, F32)
    small = sb("small_sb", 8, F32)
    g128 = sb("g128_sb", 512, F32)
    flagsb = sb("flags_sb", 64, BF16)
    onesb = sb("ones_sb", 64, BF16)
    st = sb("st", 32, F32)
    ss = sb("ss", 8, F32)

    B = [nc.alloc_psum_tensor("B%d" % i, [128, 512], F32) for i in range(4)]
    R = [nc.alloc_psum_tensor("R%d" % i, [128, 512], F32) for i in range(2)]
    TB = [nc.alloc_psum_tensor("TB%d" % i, [128, 1024], BF16) for i in range(2)]

    def pstep(t):
        return t[:, :].ap[0][0]

    def ap(t, off, dims, p0=0, npart=128):
        ps = pstep(t)
        return bass.AP(t, p0 * ps + off, [[ps, npart]] + [list(d) for d in dims])

    cload_keys = []

    def cload(dst_ap, src_ap, key, eng="sp"):
        sc.add(eng, lambda e, o=dst_ap, i=src_ap: e.dma_start(out=o, in_=i), writes=[key], dma=("c", key))
        cload_keys.append(key)

    cload(small[:, :], small_d[:, :], "small")
    cload(g128[:, :], g128_d[:, :], "g128")
    cload(caus[:, :], caus_d[:, :], "caus")
    cload(rgain[:, :], rgain_d[:, :], "rgain")
    cload(ident[:, :], ident_d[:, :], "ident", eng="pool")
    cload(shif[:, :], shif_d[:, :], "shif", eng="pool")
    cload(flagsb[:, :], flags_d[:, :], "flags", eng="pool")
    cload(masks[:, :].rearrange("p (r q) -> p r q", r=NMASK), masks_d[:, :, :].rearrange("r p q -> p r q"),
          "masks", eng="pool")
    cload(fmasks[:, :].rearrange("p (r q) -> p r q", r=2), fmasks_d[:, :, :].rearrange("r p q -> p r q"),
          "fmasks", eng="pool")
    sc.add("dve", lambda e: e.memset(onesb[:, :], 1.0), writes=["ones"])
    sc.add("dve", lambda e: e.memset(vring[:, :], 0.0), writes=["vring_init"])
    sc.add("dve", lambda e: e.memset(kring[:, :], 0.0), writes=["kring_init"])
    sc.add("dve", lambda e: e.memset(Sf[:, :], 0.0), writes=["Sf"])
    sc.add("dve", lambda e: e.memset(Sb[:, :], 0.0), writes=["Sb_init"])
    sc.add("dve", lambda e: e.memset(st[:, 0:28], 0.0), writes=["st_init"])
    allc = cload_keys + ["ones", "vring_init", "kring_init", "Sb_init", "st_init"]
    for e in ("pe", "act", "dve", "pool"):
        if e == "pe":
            sc.add(e, lambda en: en.transpose(out=TB[1][:, 0:128], in_=ident[:, :], identity=ident[:, :]),
                   reads=allc, writes=[("T", 1)])
        elif e == "act":
            sc.add(e, lambda en: en.copy(out=st[:, 28:30], in_=small[:, 0:2]), reads=allc, writes=["bar_act"])
        elif e == "dve":
            sc.add(e, lambda en: en.memset(st[:, 30:32], 0.0), reads=allc, writes=["bar_dve"])
        else:
            sc.add(e, lambda en: en.memset(aqT[:, :], 0.0), reads=allc, writes=[("aqT", i) for i in range(8)])

    cv_state = {"next": 0, "n": 0}

    def wview(w):
        return w[:, :].rearrange("(kc p) n -> p kc n", p=128)

    def conv_dma(dst, src, key):
        k = cv_state["n"]
        cv_state["n"] += 1
        sc.add("pool", lambda e, o=dst, i=src: e.dma_start(out=o, in_=i),
               writes=[key, ("cvslot", k % NCV)], dma=("cv", k % NCV))

    def conv_unit(ui):
        u = UNITS[ui]
        dst = wsc[ui, :, :]
        if u[0] == "GU":
            _, f, mc = u
            d4 = dst.rearrange("p (t kc j) -> p t kc j", t=2, kc=8)
            conv_dma(d4[:, 0, :, :], wview(wd[("g", f)])[:, :, mc * 128:(mc + 1) * 128], ("wsc", ui, 0))
            conv_dma(d4[:, 1, :, :], wview(wd[("u", f)])[:, :, mc * 128:(mc + 1) * 128], ("wsc", ui, 1))
        elif u[0] == "DN":
            _, f, cg, q = u
            nk = 4 if q < 5 else 2
            d3 = dst.rearrange("p (k n) -> p k n", k=4)
            conv_dma(d3[:, 0:nk, :], wview(wd[("d", f)])[:, 4 * q:4 * q + nk, cg * 512:(cg + 1) * 512],
                     ("wsc", ui, 0))
        elif u[0] == "TM":
            _, g, half = u
            c0 = TM_COL[g]
            d3 = dst.rearrange("p (k n) -> p k n", k=4)
            conv_dma(d3, wview(w_in)[:, 4 * half:4 * half + 4, c0:c0 + 512], ("wsc", ui, 0))
        elif u[0] == "FM":
            _, i = u
            d4 = dst.rearrange("p (t kc j) -> p t kc j", t=2, kc=8)
            for m in range(2):
                c0 = 2048 + (2 * i + m) * 128
                conv_dma(d4[:, m, :, :], wview(w_in)[:, :, c0:c0 + 128], ("wsc", ui, m))
        else:
            _, cg, half = u
            d3 = dst.rearrange("p (k n) -> p k n", k=4)
            conv_dma(d3, wview(w_out)[:, 4 * half:4 * half + 4, cg * 512:(cg + 1) * 512], ("wsc", ui, 0))

    conv_done = set()

    def ensure_conv(ui):
        if ui not in conv_done:
            conv_done.add(ui)
            conv_unit(ui)

    stream = []
    stream_pos = {"issued": 0}

    def unit_list(kind):
        l = [UIDX[("GU", 0, mc)] for mc in range(NMC)]
        l += [UIDX[("DN", 0, cg, q)] for cg in range(2) for q in range(6)]
        if kind == "halo_kv" or kind == "halo_k":
            gs = [1, 2]
        else:
            gs = [1, 2, 0, 3]
        for g in gs:
            l += [UIDX[("TM", g, h)] for h in range(2)]
        if kind == "halo_kv":
            l += [UIDX[("FM", 2)], UIDX[("FM", 3)]]
            l += [UIDX[("TM", 4, h)] for h in range(2)]
        if kind == "own":
            l += [UIDX[("FM", i)] for i in range(4)]
            l += [UIDX[("TM", 4, h)] for h in range(2)]
            l += [UIDX[("OUT", cg, h)] for cg in range(2) for h in range(2)]
            l += [UIDX[("GU", 1, mc)] for mc in range(NMC)]
            l += [UIDX[("DN", 1, cg, q)] for cg in range(2) for q in range(6)]
        return l

    def tile_kind(t):
        if t >= nh:
            return "own"
        return "halo_kv" if t >= nh - 4 else "halo_k"

    for t in range(ntile):
        stream += unit_list(tile_kind(t))
    nstream = len(stream)
    use_ptr = {"i": 0}

    def issue_load(qi):
        ui = stream[qi]
        ensure_conv(ui)
        slot = qi % NSLOT
        nparts = 2 if UNITS[ui][0] in ("GU", "FM") else 1
        ncol = 1024 if (UNITS[ui][0] == "DN" and UNITS[ui][3] == 5) else 2048
        sc.add("sp", lambda e, s=slot, u=ui, ncol=ncol: e.dma_start(out=wr[:, s * 2048:s * 2048 + ncol],
                                                                    in_=wsc[u, :, 0:ncol]),
               reads=[("wsc", ui, p) for p in range(nparts)], writes=[("w", slot)], dma=("w", slot))

    def next_unit(expect):
        qi = use_ptr["i"]
        assert stream[qi] == expect, (UNITS[stream[qi]], UNITS[expect])
        while stream_pos["issued"] < min(nstream, qi + NSLOT - 1):
            issue_load(stream_pos["issued"])
            stream_pos["issued"] += 1
        for k in range(qi, min(nstream, qi + 14)):
            ensure_conv(stream[k])
        use_ptr["i"] += 1
        return qi % NSLOT

    def wslice(slot, off, n):
        return wr[:, slot * 2048 + off: slot * 2048 + off + n]

    def dump(name, src_ap, keys):
        if name in dbg_out:
            d = dbg_out[name]
            sc.add("sp", lambda e, o=d[:, :], i=src_ap: e.dma_start(out=o, in_=i), reads=keys,
                   dma=("dbg", name))

    gn_state = {"n": 0}

    def load_gain(gi):
        s = gn_state["n"] % 2
        gn_state["n"] += 1
        sc.add("pool", lambda e, s=s, gi=gi: e.dma_start(out=gn[:, s * 1024:(s + 1) * 1024], in_=gains_d[gi, :, :]),
               writes=[("gn", s)], dma=("gn", s))
        return s

    def xkeys(par):
        return [("xt", par, tc) for tc in range(4)]

    def norm_stage(par, gslot, final=False, chain_only=False):
        x = xt[par]
        for tc in range(4):
            sc.add("act", lambda e, tc=tc: e.activation(
                out=Vb[:, tc * 1024:(tc + 1) * 1024], in_=x[:, tc * 1024:(tc + 1) * 1024], func=AF.Square,
                accum_out=ss[:, tc:tc + 1]),
                reads=[("xt", par, tc)], writes=[("V", 2 * tc), ("V", 2 * tc + 1), ("ss", tc)])
        sc.add("dve", lambda e: e.tensor_scalar(out=ss[:, 4:8], in0=ss[:, 0:4], scalar1=1.0 / D, scalar2=EPS,
                                                 op0=ALU.mult, op1=ALU.add),
               reads=[("ss", tc) for tc in range(4)], writes=["rs"])
        sc.add("act", lambda e: e.activation(out=ss[:, 4:8], in_=ss[:, 4:8], func=AF.Sqrt),
               reads=["rs"], writes=["rs"])
        sc.add("dve", lambda e: e.reciprocal(out=ss[:, 4:8], in_=ss[:, 4:8]),
               reads=["rs"], writes=["rs"])
        for tc in range(4):
            if final:
                o = x[:, tc * 1024:(tc + 1) * 1024]
                wk = [("xt", par, tc)]
            else:
                o = Vb[:, tc * 1024:(tc + 1) * 1024]
                wk = [("V", 2 * tc), ("V", 2 * tc + 1)]
            sc.add("dve", lambda e, tc=tc, o=o: e.scalar_tensor_tensor(
                out=o, in0=x[:, tc * 1024:(tc + 1) * 1024], scalar=ss[:, 4 + tc:5 + tc],
                in1=gn[:, gslot * 1024:(gslot + 1) * 1024], op0=ALU.mult, op1=ALU.mult),
                reads=[("xt", par, tc), "rs", ("gn", gslot)], writes=wk)
        if final or chain_only:
            return
        norm_transposes()

    def norm_transposes():
        for kp in range(4):
            tb = TB[kp % 2]

            def f(e, kp=kp, tb=tb):
                ins = None
                for k2 in range(2):
                    kc = 2 * kp + k2
                    for tc in range(4):
                        ins = e.transpose(out=tb[:, k2 * 512 + tc * 128: k2 * 512 + (tc + 1) * 128],
                                          in_=Vb[:, tc * 1024 + kc * 128: tc * 1024 + (kc + 1) * 128],
                                          identity=ident[:, :])
                return ins
            sc.add("pe", f, reads=[("V", i) for i in range(8)], writes=[("T", kp % 2)])
            eng = "act" if kp % 2 == 0 else "dve"
            if eng == "act":
                fn = lambda e, kp=kp, tb=tb: e.copy(out=xnT[:, kp * 1024:(kp + 1) * 1024], in_=tb[:, :])
            else:
                fn = lambda e, kp=kp, tb=tb: e.tensor_copy(out=xnT[:, kp * 1024:(kp + 1) * 1024], in_=tb[:, :])
            sc.add(eng, fn, reads=[("T", kp % 2)], writes=[("xnT", 2 * kp), ("xnT", 2 * kp + 1)])

    XNT = [("xnT", k) for k in range(8)]

    DNBANK = [[(("B", i), B[i]) for i in range(4)],
              [(("R", 0), R[0]), (("R", 1), R[1]), (("B", 0), B[0]), (("B", 1), B[1])]]

    def ffn_stage(par, f, gu_hook=None, mid_hook=None):
        x = xt[par]
        for mc in range(NMC):
            slot = next_unit(UIDX[("GU", f, mc)])
            bg, bu = B[2 * (mc % 2)], B[2 * (mc % 2) + 1]
            for which, bank, bk in ((0, bg, 2 * (mc % 2)), (1, bu, 2 * (mc % 2) + 1)):
                def fmm(e, slot=slot, which=which, bank=bank):
                    ins = None
                    for kc in range(8):
                        ins = e.matmul(bank[:, :], lhsT=wslice(slot, (which * 8 + kc) * 128, 128),
                                       rhs=xnT[:, kc * 512:(kc + 1) * 512], start=(kc == 0), stop=(kc == 7))
                    return ins
                sc.add("pe", fmm, reads=XNT + [("w", slot)], writes=[("B", bk)])
            sl = sil[mc % 2]
            sc.add("act", lambda e, bg=bg, sl=sl: e.activation(out=sl[:, :], in_=bg[:, :], func=AF.Silu),
                   reads=[("B", 2 * (mc % 2))], writes=[("sil", mc % 2)])
            sc.add("dve", lambda e, bu=bu, sl=sl, mc=mc: e.tensor_tensor(
                out=U[:, mc * 512:(mc + 1) * 512], in0=bu[:, :], in1=sl[:, :], op=ALU.mult),
                reads=[("B", 2 * (mc % 2) + 1), ("sil", mc % 2)], writes=[("U", mc)])
        if gu_hook is not None:
            gu_hook()
        for cg in range(2):
            if cg == 1 and mid_hook is not None:
                mid_hook()
            for q in range(6):
                slot = next_unit(UIDX[("DN", f, cg, q)])
                nk = 4 if q < 5 else 2
                for tc in range(4):
                    bkey, bank = DNBANK[cg][tc]

                    def fmm(e, slot=slot, q=q, nk=nk, tc=tc, bank=bank):
                        ins = None
                        for i in range(nk):
                            kc = 4 * q + i
                            ins = e.matmul(bank[:, :], lhsT=U[:, kc * 512 + tc * 128: kc * 512 + (tc + 1) * 128],
                                           rhs=wslice(slot, i * 512, 512), start=(kc == 0), stop=(kc == NMC - 1))
                        return ins
                    sc.add("pe", fmm, reads=[("U", 4 * q + i) for i in range(nk)] + [("w", slot)],
                           writes=[bkey])
            for tc in range(4):
                bkey, bank = DNBANK[cg][tc]
                xs = x[:, tc * 1024 + cg * 512: tc * 1024 + (cg + 1) * 512]
                sc.add("dve", lambda e, tc=tc, xs=xs, bank=bank: e.scalar_tensor_tensor(
                    out=xs, in0=bank[:, :], scalar=0.5, in1=xs, op0=ALU.mult, op1=ALU.add),
                    reads=[bkey, ("xt", par, tc)], writes=[("xt", par, tc)])

    bank_rr = {"i": 0}

    def nextbank():
        i = bank_rr["i"] % 4
        bank_rr["i"] += 1
        return i

    def tm_group(g, strided=False):
        s0 = next_unit(UIDX[("TM", g, 0)])
        s1 = next_unit(UIDX[("TM", g, 1)])
        res = []
        for tc in range(4):
            bi = nextbank()

            def fmm(e, tc=tc, bi=bi, s0=s0, s1=s1):
                ins = None
                for kc in range(8):
                    s = s0 if kc < 4 else s1
                    lt = (ap(xnT, kc * 512 + tc, [[4, 128]]) if strided
                          else xnT[:, kc * 512 + tc * 128: kc * 512 + (tc + 1) * 128])
                    ins = e.matmul(B[bi][:, :], lhsT=lt,
                                   rhs=wslice(s, (kc % 4) * 512, 512), start=(kc == 0), stop=(kc == 7))
                return ins
            sc.add("pe", fmm, reads=XNT + [("w", s0), ("w", s1)], writes=[("B", bi)])
            res.append((tc, bi))
            yield tc, bi

    def rotary(tc, bi, is_q):
        bank = B[bi]
        cosb = ap(tab, tc * 128, [[0, 4], [1, 128]])
        sse = ap(tab, 512 + tc * 128, [[0, 4], [2, 64]])
        sso = ap(tab, 512 + tc * 128 + 1, [[0, 4], [2, 64]])
        be = ap(bank, 0, [[128, 4], [2, 64]])
        bo = ap(bank, 1, [[128, 4], [2, 64]])
        r1e = ap(rt[1], 0, [[128, 4], [2, 64]])
        r1o = ap(rt[1], 1, [[128, 4], [2, 64]])
        sc.add("dve", lambda e: e.tensor_tensor(out=rt[0][:, :], in0=bank[:, :], in1=cosb, op=ALU.mult),
               reads=[("B", bi), "tabc"], writes=["rt0"])
        sc.add("dve", lambda e: e.tensor_tensor(out=r1e, in0=bo, in1=sse, op=ALU.mult),
               reads=[("B", bi), "tabs"], writes=["rt1e"])
        sc.add("dve", lambda e: e.tensor_tensor(out=r1o, in0=be, in1=sso, op=ALU.mult),
               reads=[("B", bi), "tabs"], writes=["rt1o"])
        sc.add("pool", lambda e: e.tensor_tensor(out=rt[0][:, :], in0=rt[0][:, :], in1=rt[1][:, :], op=ALU.add),
               reads=["rt0", "rt1e", "rt1o"], writes=["rt0"])
        page = tc if is_q else 4 + tc
        dec = ap(small, 0 if is_q else 4, [[1, 4], [0, 128]])
        sc.add("pool", lambda e: e.tensor_tensor(out=U[:, page * 512:(page + 1) * 512], in0=rt[0][:, :], in1=dec,
                                                  op=ALU.mult),
               reads=["rt0"], writes=[("U", page)])
        return page

    tb_rr = {"i": 0}

    def transpose4(src_page_ap_fn, src_keys, dst_ap, dst_keys, evac_eng):
        ti = tb_rr["i"] % 2
        tb_rr["i"] += 1
        tb = TB[ti]

        def f(e):
            ins = None
            for h in range(4):
                ins = e.transpose(out=tb[:, h * 128:(h + 1) * 128], in_=src_page_ap_fn(h), identity=ident[:, :])
            return ins
        sc.add("pe", f, reads=src_keys, writes=[("T", ti)])
        src = ap(tb, 0, [[128, 4], [1, 128]])
        if evac_eng == "act":
            sc.add("act", lambda e: e.copy(out=dst_ap, in_=src), reads=[("T", ti)], writes=dst_keys)
        else:
            sc.add(evac_eng, lambda e: e.tensor_copy(out=dst_ap, in_=src), reads=[("T", ti)], writes=dst_keys)

    def inproj_stage(t, kind, hook_after_rv=None):
        tau = t
        n0 = 4 * tau
        deferred = []
        for tc, bi in tm_group(1):
            pg = rotary(tc, bi, False)
            if kind == "own":
                deferred.append(lambda tc=tc, pg=pg: transpose4(
                    lambda h, pg=pg: U[:, pg * 512 + h * 128: pg * 512 + (h + 1) * 128], [("U", pg)],
                    ap(U, 20 * 512 + tc * 128, [[512, 4], [1, 128]]), [("U", 20 + h) for h in range(4)], "act"))
        for tc, bi in tm_group(2):
            sc.add("act", lambda e, tc=tc, bi=bi: e.copy(out=U[:, (8 + tc) * 512:(9 + tc) * 512], in_=B[bi][:, :]),
                   reads=[("B", bi)], writes=[("U", 8 + tc)])
        for f_ in deferred:
            f_()
        deferred = []
        if hook_after_rv is not None:
            hook_after_rv()
        for tc in range(4):
            ri = tc % 2

            def fkv(e, tc=tc, ri=ri):
                ins = None
                for h in range(4):
                    ins = e.matmul(R[ri][:, h * 128:(h + 1) * 128],
                                   lhsT=U[:, (4 + tc) * 512 + h * 128:(4 + tc) * 512 + (h + 1) * 128],
                                   rhs=U[:, (8 + tc) * 512 + h * 128:(8 + tc) * 512 + (h + 1) * 128],
                                   start=True, stop=True)
                return ins
            sc.add("pe", fkv, reads=[("U", 4 + tc), ("U", 8 + tc)], writes=[("R", ri)])
            sc.add("dve", lambda e, ri=ri: e.tensor_tensor(out=Sf[:, :], in0=R[ri][:, :], in1=Sf[:, :], op=ALU.add),
                   reads=[("R", ri), "Sf"], writes=["Sf"])
            sc.add("dve", lambda e: e.tensor_tensor(out=Sf[:, :], in0=Sf[:, :], in1=g128[:, :], op=ALU.mult),
                   reads=["Sf"], writes=["Sf"])
            sl = (n0 + tc + 1) % 6
            sc.add("act", lambda e, sl=sl: e.copy(out=Sb[:, sl * 512:(sl + 1) * 512], in_=Sf[:, :]),
                   reads=["Sf"], writes=[("Sb", sl)])
        if kind == "halo_k":
            return
        if kind == "own":
            for tc, bi in tm_group(0):
                pg = rotary(tc, bi, True)
                deferred.append(lambda tc=tc, pg=pg: transpose4(
                    lambda h, pg=pg: U[:, pg * 512 + h * 128: pg * 512 + (h + 1) * 128], [("U", pg)],
                    ap(U, 16 * 512 + tc * 128, [[512, 4], [1, 128]]), [("U", 16 + h) for h in range(4)], "act"))
            for tc, bi in tm_group(3):
                sc.add("act", lambda e, tc=tc, bi=bi: e.activation(out=U[:, (12 + tc) * 512:(13 + tc) * 512],
                                                                    in_=B[bi][:, :], func=AF.Silu),
                       reads=[("B", bi)], writes=[("U", 12 + tc)])
                sc.add("pool", lambda e, tc=tc: e.tensor_tensor(
                    out=U[:, (12 + tc) * 512:(13 + tc) * 512], in0=U[:, (12 + tc) * 512:(13 + tc) * 512],
                    in1=rgain[:, :], op=ALU.mult), reads=[("U", 12 + tc)], writes=[("U", 12 + tc)])
            for f_ in deferred:
                f_()
            deferred = []
        s0 = (4 * tau) % NRING
        fm_list = [0, 1, 2, 3] if kind == "own" else [2, 3]
        for i in fm_list:
            slot = next_unit(UIDX[("FM", i)])
            for m in range(2):
                mc = 2 * i + m
                bi = nextbank()

                def fmm(e, slot=slot, m=m, bi=bi):
                    ins = None
                    for kc in range(8):
                        ins = e.matmul(B[bi][:, :], lhsT=wslice(slot, (m * 8 + kc) * 128, 128),
                                       rhs=xnT[:, kc * 512:(kc + 1) * 512], start=(kc == 0), stop=(kc == 7))
                    return ins
                sc.add("pe", fmm, reads=XNT + [("w", slot)], writes=[("B", bi)])
                if mc < 4:
                    for a_ in range(2):
                        hd_ = 2 * mc + a_
                        sc.add("act", lambda e, hd_=hd_, a_=a_, bi=bi: e.activation(
                            out=aqT[64 * a_:64 * a_ + 64, hd_ * 512:(hd_ + 1) * 512],
                            in_=B[bi][64 * a_:64 * a_ + 64, :], func=AF.Copy, scale=0.125),
                            reads=[("B", bi)], writes=[("aqT", hd_)])
                else:
                    j = mc - 4
                    off = j * NRING * 128 + s0 * 128
                    sc.add("dve", lambda e, off=off, bi=bi: e.tensor_copy(out=kring[:, off:off + 512], in_=B[bi][:, :]),
                           reads=[("B", bi)], writes=[("kr", j, s0 + c) for c in range(4)])
        for tc, bi in tm_group(4, strided=True):
            slot_v = s0 + tc
            dst = ap(vring, slot_v * VW, [[192, 4], [128, 2], [1, 64]])
            src = ap(B[bi], 0, [[128, 4], [64, 2], [1, 64]])
            sc.add("act", lambda e, dst=dst, src=src: e.copy(out=dst, in_=src), reads=[("B", bi)],
                   writes=[("vr", slot_v)])
            odst = ap(vring, slot_v * VW + 64, [[192, 4], [1, 64]])
            osrc_t = onesb if kind == "own" else flagsb
            osrc = ap(osrc_t, 0, [[0, 4], [1, 64]])
            sc.add("pool", lambda e, odst=odst, osrc=osrc: e.tensor_copy(out=odst, in_=osrc), reads=[],
                   writes=[("vr1", slot_v)])

    ybf = [sil_bf[k // 2][:, (k % 2) * 512:(k % 2) * 512 + 512] for k in range(4)]

    def retention_stage(t):
        n0 = 4 * t
        cb = ap(caus, 0, [[0, 4], [1, 128]])
        pend = []

        def sc_op(tc):
            def fsc(e, tc=tc):
                ins = None
                for h in range(4):
                    ins = e.matmul(TBf[0][:, h * 128:(h + 1) * 128],
                                   lhsT=U[:, (20 + h) * 512 + tc * 128:(20 + h) * 512 + (tc + 1) * 128],
                                   rhs=U[:, (16 + h) * 512 + tc * 128:(16 + h) * 512 + (tc + 1) * 128],
                                   start=True, stop=True)
                return ins
            sc.add("pe", fsc, reads=[("U", 16 + h) for h in range(8)], writes=[("T", 0)])
            sb_ = scb[tc % 2]
            sc.add("dve", lambda e, sb_=sb_: e.tensor_tensor(
                out=sb_[:, :], in0=TBf[0][:, :], in1=cb, op=ALU.mult),
                reads=[("T", 0)], writes=[("scb", tc % 2)])

        def out_op(tc):
            sb_ = scb[tc % 2]
            ri = tc % 2
            sl = (n0 + tc) % 6

            def fo(e, tc=tc, ri=ri, sb_=sb_, sl=sl):
                ins = None
                for h in range(4):
                    e.matmul(TBf[1][:, h * 128:(h + 1) * 128], lhsT=sb_[:, h * 128:(h + 1) * 128],
                             rhs=U[:, (8 + tc) * 512 + h * 128:(8 + tc) * 512 + (h + 1) * 128],
                             start=True, stop=False)
                    ins = e.matmul(TBf[1][:, h * 128:(h + 1) * 128],
                                   lhsT=U[:, (16 + h) * 512 + tc * 128:(16 + h) * 512 + (tc + 1) * 128],
                                   rhs=Sb[:, sl * 512 + h * 128: sl * 512 + (h + 1) * 128],
                                   start=False, stop=True)
                return ins
            sc.add("pe", fo, reads=[("scb", tc % 2), ("U", 8 + tc), ("Sb", sl)] + [("U", 16 + h) for h in range(4)],
                   writes=[("T", 1)])
            sc.add("act", lambda e: e.copy(out=retf[:, :], in_=TBf[1][:, :]), reads=[("T", 1)],
                   writes=["retf"])
            sc.add("act", lambda e: e.activation(out=sqf[:, :], in_=TBf[1][:, :], func=AF.Square),
                   reads=[("T", 1)], writes=["sqf"])
            sc.add("dve", lambda e: e.tensor_reduce(out=st[:, 0:4], in_=ap(retf, 0, [[128, 4], [1, 128]]),
                                                    axis=AX.X, op=ALU.add), reads=["retf"], writes=["st_a"])
            sc.add("dve", lambda e: e.tensor_reduce(out=st[:, 4:8], in_=ap(sqf, 0, [[128, 4], [1, 128]]),
                                                    axis=AX.X, op=ALU.add), reads=["sqf"], writes=["st_b"])
            sc.add("dve", lambda e: e.tensor_single_scalar(out=st[:, 8:16], in_=st[:, 0:8], scalar=1.0 / 128,
                                                           op=ALU.mult), reads=["st_a", "st_b"], writes=["st_c"])
            sc.add("dve", lambda e: e.tensor_tensor(out=st[:, 16:20], in0=st[:, 8:12], in1=st[:, 8:12], op=ALU.mult),
                   reads=["st_c"], writes=["st_d"])
            sc.add("dve", lambda e: e.tensor_tensor(out=st[:, 20:24], in0=st[:, 12:16], in1=st[:, 16:20],
                                                    op=ALU.subtract), reads=["st_c", "st_d"], writes=["st_e"])
            sc.add("dve", lambda e: e.tensor_single_scalar(out=st[:, 24:28], in_=st[:, 20:24], scalar=EPS, op=ALU.add),
                   reads=["st_e"], writes=["st_f"])
            sc.add("act", lambda e: e.activation(out=st[:, 24:28], in_=st[:, 24:28], func=AF.Sqrt),
                   reads=["st_f"], writes=["st_f"])
            sc.add("dve", lambda e: e.reciprocal(out=st[:, 24:28], in_=st[:, 24:28]),
                   reads=["st_f"], writes=["st_f"])
            mean_b = ap(st, 8, [[1, 4], [0, 128]])
            rstd_b = ap(st, 24, [[1, 4], [0, 128]])
            for h in range(4):
                sc.add("dve", lambda e, h=h: e.tensor_scalar(
                    out=retf[:, h * 128:(h + 1) * 128], in0=retf[:, h * 128:(h + 1) * 128],
                    scalar1=st[:, 8 + h:9 + h], scalar2=st[:, 24 + h:25 + h], op0=ALU.subtract, op1=ALU.mult),
                    reads=["retf", "st_c", "st_f"], writes=["retf"])
            yb = ybf[tc]
            ykeys = [("ybf", tc), ("sil", tc // 2)]
            sc.add("pool", lambda e, tc=tc, yb=yb: e.tensor_tensor(out=yb, in0=retf[:, :],
                                                                   in1=U[:, (12 + tc) * 512:(13 + tc) * 512], op=ALU.mult),
                   reads=["retf", ("U", 12 + tc)], writes=ykeys)
            base = (tc % 2) * 512
            sbt = sil_bf[tc // 2]
            pend.append(lambda tc=tc, sbt=sbt, base=base, ykeys=ykeys: transpose4(
                lambda h: sbt[:, base + h * 128: base + (h + 1) * 128], ykeys,
                ap(Vb, tc * 128, [[512, 4], [1, 128]]), [("V", h) for h in range(4)], "dve"))

        return sc_op, out_op, pend

    SBANKS = [(("B", 0), B[0]), (("B", 1), B[1]), (("R", 1), R[1])]
    TBf = [TB[0].bitcast(F32), TB[1].bitcast(F32)]

    def attention_stage(t, ret):
        ret_sc, ret_out, pend = ret
        units = [("n", 0, g) for g in range(4)] + [("n", -1, g) for g in range(4)] + [("f", d) for d in (-2, -3, -4)]
        NU = len(units)
        stream = [(hd, u) for hd in range(8) for u in range(NU)]
        hooks = {}

        def krcols(j, d, g):
            return j * NRING * 128 + ((t + d) % 5) * 512 + g

        def s_op(k):
            hd, u = stream[k]
            j = hd // 2
            un = units[u]
            sbk = k % 3
            skey, sbank = SBANKS[sbk]
            if un[0] == "n":
                _, d, g = un
                kc0 = krcols(j, d, g)
                rk = [("kr", j, ((t + d) % 5) * 4 + c) for c in range(4)]
                sc.add("pe", lambda e, kc0=kc0, sbank=sbank, hd=hd: e.matmul(
                    sbank[:, :], lhsT=ap(kring, kc0, [[4, 128]]), rhs=aqT[:, hd * 512:(hd + 1) * 512],
                    start=True, stop=True), reads=rk + [("aqT", hd)], writes=[skey])
                mk = masks[:, ((d + 1) * 4 + g) * 512:((d + 1) * 4 + g + 1) * 512]
            else:
                _, d = un
                rk = [("kr", j, ((t + d) % 5) * 4 + c) for c in range(4)]
                kcs = [krcols(j, d, g) for g in range(4)]

                def fs(e, kcs=kcs, sbank=sbank, hd=hd):
                    ins = None
                    for g in range(4):
                        ins = e.matmul(sbank[:, g * 128:(g + 1) * 128], lhsT=ap(kring, kcs[g], [[4, 128]]),
                                       rhs=ap(aqT, hd * 512 + g, [[4, 128]]), start=True, stop=True)
                    return ins
                sc.add("pe", fs, reads=rk + [("aqT", hd)], writes=[skey])
                fk = 0 if d == -4 else 1
                mk = ap(fmasks, fk * 128, [[0, 4], [1, 128]])
            ebk = k % 4
            sc.add("act", lambda e, ebk=ebk, sbank=sbank: e.activation(out=eb[ebk][:, :], in_=sbank[:, :], func=AF.Exp),
                   reads=[skey], writes=[("eb", ebk)])
            sc.add("dve", lambda e, ebk=ebk, mk=mk: e.tensor_tensor(
                out=eb[ebk][:, :], in0=eb[ebk][:, :], in1=mk, op=ALU.mult),
                reads=[("eb", ebk)], writes=[("eb", ebk)])

        def pv_op(k):
            hd, u = stream[k]
            j, a = hd // 2, hd % 2
            ob = 2 + hd % 2
            un = units[u]
            sbk = k % 4
            if un[0] == "n":
                _, d, g = un
                slot = ((t + d) % 5) * 4 + g
                vcol = slot * VW + j * 192 + 64 * a
                sc.add("pe", lambda e, vcol=vcol, sbk=sbk, u=u, ob=ob: e.matmul(
                    B[ob][:, :], lhsT=vring[:, vcol:vcol + 128], rhs=eb[sbk][:, :], start=(u == 0), stop=False),
                    reads=[("vr", slot), ("vr1", slot), ("eb", sbk)], writes=[("B", ob)])
            else:
                _, d = un
                slots = [((t + d) % 5) * 4 + g for g in range(4)]

                def fp(e, slots=slots, sbk=sbk, u=u, j=j, a=a, ob=ob):
                    ins = None
                    for g in range(4):
                        vcol = slots[g] * VW + j * 192 + 64 * a
                        ins = e.matmul(ap(B[ob], g, [[4, 128]]), lhsT=vring[:, vcol:vcol + 128],
                                       rhs=eb[sbk][:, g * 128:(g + 1) * 128], start=False,
                                       stop=(u == NU - 1 and g == 3))
                    return ins
                sc.add("pe", fp, reads=[("vr", sl_) for sl_ in slots] + [("vr1", sl_) for sl_ in slots] + [("eb", sbk)],
                       writes=[("B", ob)])

        def recip_fn(hd, piece):
            if piece != 0:
                return
            a = hd % 2
            ob = 2 + hd % 2
            d0_ = 64 if a == 0 else 0
            sc.add("act", lambda e, d0_=d0_, ob=ob: e.activation(
                out=lnd[d0_:d0_ + 64, :], in_=B[ob][d0_:d0_ + 64, :], func=AF.Ln),
                reads=[("B", ob)], writes=["lnd"])
            sc.add("act", lambda e, d0_=d0_: e.activation(
                out=Rb[d0_:d0_ + 64, :], in_=lnd[d0_:d0_ + 64, :], func=AF.Exp, scale=-1.0),
                reads=["lnd"], writes=[("Rb", p_) for p_ in range(4)])

        def post_fn(hd):
            j, a = hd // 2, hd % 2
            ob = 2 + hd % 2
            n0_ = 0 if a == 0 else 64
            ri = 0
            if a == 0:
                sc.add("pe", lambda e, ri=ri: e.matmul(R[ri][0:64, :], lhsT=ident[64:128, 64:128], rhs=Rb[64:128, :],
                                                       start=True, stop=True), reads=[("Rb", p_) for p_ in range(4)],
                       writes=[("R", ri)])
            else:
                sc.add("pe", lambda e, ri=ri: e.matmul(R[ri][:, :], lhsT=shif[0:64, :], rhs=Rb[0:64, :],
                                                       start=True, stop=True), reads=[("Rb", p_) for p_ in range(4)],
                       writes=[("R", ri)])
            sc.add("act", lambda e, ri=ri, n0_=n0_: e.copy(out=rsb[n0_:n0_ + 64, :], in_=R[ri][n0_:n0_ + 64, :]),
                   reads=[("R", ri)], writes=["rsb2"])
            pg = 4 + j
            sc.add("dve", lambda e, n0_=n0_, ob=ob, pg=pg: e.tensor_tensor(
                out=Vb[n0_:n0_ + 64, pg * 512:(pg + 1) * 512], in0=B[ob][n0_:n0_ + 64, :], in1=rsb[n0_:n0_ + 64, :],
                op=ALU.mult), reads=[("B", ob), "rsb2"], writes=[("V", pg)])

        nstream_ = len(stream)
        for hd in range(8):
            kend = hd * NU + NU - 1
            for piece in range(4):
                hooks.setdefault(kend + 2 + piece, []).append(lambda hd=hd, piece=piece: recip_fn(hd, piece))
            hooks.setdefault(kend + 8, []).append(lambda hd=hd: post_fn(hd))
            if hd < 4:
                k0 = hd * 2 * NU
                hooks.setdefault(k0 + 3, []).append(lambda hd=hd: ret_sc(hd))
                hooks.setdefault(k0 + 7, []).append(lambda hd=hd: ret_out(hd))
                hooks.setdefault(k0 + 2 * NU + 1, []).append(lambda hd=hd: pend[hd]())
        for k in range(3):
            s_op(k)
        for k in range(nstream_):
            if k + 3 < nstream_:
                s_op(k + 3)
            pv_op(k)
            for f_ in hooks.pop(k, []):
                f_()
        for k in sorted(hooks):
            for f_ in hooks[k]:
                f_()

    def outproj_stage(par):
        x = xt[par]
        vk = [("V", i) for i in range(8)]
        for cg in range(2):
            s0 = next_unit(UIDX[("OUT", cg, 0)])
            s1 = next_unit(UIDX[("OUT", cg, 1)])
            for tc in range(4):
                def fmm(e, tc=tc, s0=s0, s1=s1):
                    ins = None
                    for kc in range(8):
                        s = s0 if kc < 4 else s1
                        ins = e.matmul(B[tc][:, :], lhsT=Vb[:, kc * 512 + tc * 128: kc * 512 + (tc + 1) * 128],
                                       rhs=wslice(s, (kc % 4) * 512, 512), start=(kc == 0), stop=(kc == 7))
                    return ins
                sc.add("pe", fmm, reads=vk + [("w", s0), ("w", s1)], writes=[("B", tc)])
            for tc in range(4):
                xs = x[:, tc * 1024 + cg * 512: tc * 1024 + (cg + 1) * 512]
                sc.add("dve", lambda e, tc=tc, xs=xs: e.tensor_tensor(out=xs, in0=B[tc][:, :], in1=xs, op=ALU.add),
                       reads=[("B", tc), ("xt", par, tc)], writes=[("xt", par, tc)])

    def load_x(t):
        par = t % 2
        src = xin[t * T:(t + 1) * T, :].rearrange("(tc p) d -> p tc d", p=128)
        dst = xt[par][:, :].rearrange("p (tc d) -> p tc d", tc=4)
        sc.add("pool", lambda e: e.dma_start(out=dst, in_=src), writes=xkeys(par), dma=("xt", par))

    def load_tab(t):
        srcc = cosd[t * T:(t + 1) * T, :].rearrange("(tc p) d -> p tc d", p=128)
        srcs = ssd[t * T:(t + 1) * T, :].rearrange("(tc p) d -> p tc d", p=128)
        sc.add("pool", lambda e: e.dma_start(out=tab[:, 0:512].rearrange("p (tc d) -> p tc d", tc=4), in_=srcc),
               writes=["tabc"], dma="tabc")
        sc.add("pool", lambda e: e.dma_start(out=tab[:, 512:1024].rearrange("p (tc d) -> p tc d", tc=4), in_=srcs),
               writes=["tabs"], dma="tabs")

    load_x(0)
    if ntile > 1:
        load_x(1)
    nxt = {}
    defer_x = []

    def hoist_chain(tn):
        nxt["gs"] = load_gain(0)
        norm_stage(tn % 2, nxt["gs"], chain_only=True)

    hoist_chain(0)
    norm_transposes()
    for t in range(ntile):
        par = t % 2
        kind = tile_kind(t)
        load_tab(t)
        gs2 = load_gain(1)
        ffn_stage(par, 0)
        if t == 0:
            dump("h1", xt[par][:, 0:1024], xkeys(par))
        norm_stage(par, gs2)
        last = (t + 1 == ntile)
        if kind != "own":
            inproj_stage(t, kind, hook_after_rv=(None if last else (lambda: hoist_chain(t + 1))))
            if not last:
                if t + 2 < ntile:
                    load_x(t + 2)
                norm_transposes()
            continue
        inproj_stage(t, kind)
        if defer_x:
            load_x(defer_x.pop(0))
        gs3 = load_gain(2)
        pend = retention_stage(t)
        attention_stage(t, pend)
        outproj_stage(par)
        norm_stage(par, gs3)
        gs4 = load_gain(3)
        ffn_stage(par, 1, gu_hook=(None if last else (lambda: hoist_chain(t + 1))),
                  mid_hook=(None if last else norm_transposes))
        norm_stage(par, gs4, final=True)
        ot = t - nh
        dst = yout[ot * T:(ot + 1) * T, :].rearrange("(tc p) d -> p tc d", p=128)
        src = xt[par][:, :].rearrange("p (tc d) -> p tc d", tc=4)
        sc.add("pool", lambda e, dst=dst, src=src: e.dma_start(out=dst, in_=src), reads=xkeys(par),
               dma=("xt_out", par))
        if t + 2 < ntile:
            defer_x.append(t + 2)
    sc.add("sp", lambda e: e.nop(), writes=xkeys(0) + xkeys(1))

    with nc.allow_low_precision("bf16 matmul operands by design; fp32 accumulation"):
        with nc.Block() as block:
            sc.emit(nc, block)
    return nc


_CACHE = {}


def _prep_core(x, b, s, half, common):
    own = x[b, s * half:(s + 1) * half]
    if s == 0:
        halo = np.zeros_like(own)
        flag = 0.0
    else:
        halo = x[b, (s - 1) * half:s * half]
        flag = 1.0
    xin = np.ascontiguousarray(np.concatenate([halo, own], axis=0))
    pos = np.arange((s - 1) * half, (s + 1) * half)
    cos, ssn = _rope_tables(pos)
    m = dict(common)
    m.update(xin=xin, cosd=cos, ssd=ssn, flags=np.full((128, 64), flag, np.float32))
    return m


def run(x, params, nh, no, dbg=()):
    key = (nh, no, tuple(dbg))
    if key not in _CACHE:
        _CACHE[key] = build(nh, no, dbg)
    nc = _CACHE[key]
    cst = _consts()
    f32 = lambda a: np.ascontiguousarray(np.asarray(a, np.float32))
    gains = np.stack([np.broadcast_to(f32(params[k]).reshape(1, D), (128, D))
                      for k in ("norm_ffn1", "norm_mix", "norm_ffn2", "norm_final")]).copy()
    common = dict(
        w_g1=f32(params["ffn1_w_gate"][0]), w_u1=f32(params["ffn1_w_up"][0]), w_d1=f32(params["ffn1_w_down"][0]),
        w_g2=f32(params["ffn2_w_gate"][0]), w_u2=f32(params["ffn2_w_up"][0]), w_d2=f32(params["ffn2_w_down"][0]),
        w_in=f32(params["w_in"][0]), w_out=f32(params["w_out"][0]), gains=gains,
        rgain=np.broadcast_to(f32(params["ret_norm_gain"]).reshape(1, 512), (128, 512)).copy(),
        small=cst["small"], g128=cst["g128"], caus=cst["caus"], ident=cst["ident"], shif=cst["shif"],
        masks=cst["masks"], fmasks=cst["fmasks"],
    )
    half = no * T
    nb = x.shape[0]
    in_maps = []
    for c in range(8):
        b, s = (c // 2) % nb, c % 2
        in_maps.append(_prep_core(x, b, s, half, common))
    res = run_bass_kernel_spmd(nc, in_maps, core_ids=list(range(8)))
    return res


def kernel(x, norm_ffn1, ffn1_w_gate, ffn1_w_up, ffn1_w_down, norm_mix, w_in, ret_norm_gain, w_out,
           norm_ffn2, ffn2_w_gate, ffn2_w_up, ffn2_w_down, norm_final):
    x = np.asarray(x, np.float32)
    params = dict(norm_ffn1=norm_ffn1, ffn1_w_gate=ffn1_w_gate, ffn1_w_up=ffn1_w_up, ffn1_w_down=ffn1_w_down,
                  norm_mix=norm_mix, w_in=w_in, ret_norm_gain=ret_norm_gain, w_out=w_out, norm_ffn2=norm_ffn2,
                  ffn2_w_gate=ffn2_w_gate, ffn2_w_up=ffn2_w_up, ffn2_w_down=ffn2_w_down, norm_final=norm_final)
    params = {k: np.asarray(v) for k, v in params.items()}
    res = run(x, params, 8, 8)
    out = np.empty((4, 8192, D), np.float32)
    for c in range(8):
        b, s = c // 2, c % 2
        out[b, s * 4096:(s + 1) * 4096] = np.asarray(res.results[c]["yout"], np.float32)
    return out
```

```python
import numpy as np
import ml_dtypes
import concourse.bass as bass
import concourse.mybir as mybir
from concourse.bass_utils import run_bass_kernel_spmd

F32 = mybir.dt.float32
BF16 = mybir.dt.bfloat16
ALU = mybir.AluOpType
AF = mybir.ActivationFunctionType
AX = mybir.AxisListType

D = 1024
DFF = 2816
NMC = DFF // 128
T = 512
NSLOT = 5
NCV = 12
NRING = 20
VW = 768
NMASK = 8
EPS = 1e-6
INCOLS = 3584


class _Op:
    __slots__ = ("eng", "fn", "deps", "inc", "dma", "val")


class Sched:
    ENGS = ("pe", "act", "dve", "pool", "sp")

    def __init__(self):
        self.q = {e: [] for e in self.ENGS}
        self.lastw = {}
        self.readers = {}
        self.dma_n = {}

    def add(self, eng, fn, reads=(), writes=(), dma=None):
        op = _Op()
        op.eng = eng
        op.fn = fn
        op.dma = dma
        op.inc = False
        op.val = 0
        deps = {}
        for k in reads:
            w = self.lastw.get(k)
            if w is not None:
                deps[id(w)] = w
        for k in writes:
            w = self.lastw.get(k)
            if w is not None and (w.dma is not None or dma is not None or w.eng != eng or eng != "pe"):
                deps[id(w)] = w
            for r in self.readers.get(k, ()):
                if r.dma is not None or dma is not None or r.eng != eng:
                    deps[id(r)] = r
        op.deps = list(deps.values())
        for o in op.deps:
            o.inc = True
        for k in reads:
            self.readers.setdefault(k, []).append(op)
        for k in writes:
            self.lastw[k] = op
            self.readers[k] = []
        if dma is not None:
            n = self.dma_n.get(dma, 0) + 1
            self.dma_n[dma] = n
            op.val = 16 * n
            op.inc = True
        self.q[eng].append(op)
        return op

    def emit(self, nc, block):
        esem = {e: nc.alloc_semaphore("sem_" + e) for e in self.ENGS}
        dsem = {k: nc.alloc_semaphore("dsem_%d" % i) for i, k in enumerate(self.dma_n)}
        for e in self.ENGS:
            c = 0
            for op in self.q[e]:
                if op.dma is None and op.inc:
                    c += 1
                    op.val = c

        def semof(op):
            return dsem[op.dma] if op.dma is not None else esem[op.eng]

        def run(e, eng):
            seen = {}
            for op in self.q[e]:
                need = {}
                for d in op.deps:
                    s = semof(d)
                    if need.get(s.num, (None, 0))[1] < d.val:
                        need[s.num] = (s, d.val)
                for num, (s, v) in need.items():
                    if seen.get(num, 0) < v:
                        eng.wait_ge(s, v)
                        seen[num] = v
                ins = op.fn(eng)
                if op.inc:
                    ins.then_inc(semof(op), 16 if op.dma is not None else 1)

        @block.tensor
        def _(eng):
            run("pe", eng)

        @block.scalar
        def _(eng):
            run("act", eng)

        @block.vector
        def _(eng):
            run("dve", eng)

        @block.gpsimd
        def _(eng):
            run("pool", eng)

        @block.sync
        def _(eng):
            run("sp", eng)


def _gammas():
    return 1.0 - 2.0 ** (-5.0 - np.arange(4, dtype=np.float64))


def _mask_tables():
    near = np.zeros((8, 128, 512), np.float32)
    i = np.arange(128)[:, None]
    j = np.arange(512)[None, :]
    for d in (-1, 0):
        for g in range(4):
            dist = j - (512 * d + g + 4 * i)
            m = ((dist >= 0) & (dist <= 128)).astype(np.float32)
            m += ((dist >= 0) & (dist % 4 == 0) & (dist <= 512)).astype(np.float32)
            m += ((dist >= 0) & (dist % 16 == 0) & (dist <= 2048)).astype(np.float32)
            near[(d + 1) * 4 + g] = m
    far = np.zeros((2, 128, 128), np.float32)
    ip = np.arange(128)[None, :]
    for k, d in enumerate((-4, -3)):
        dist = 4 * (ip - i) - 512 * d
        far[k] = ((dist % 16 == 0) & (dist <= 2048) & (dist >= 0)).astype(np.float32)
    dist2 = 4 * (ip - i) + 1024
    assert np.array_equal(((dist2 % 16 == 0) & (dist2 <= 2048)).astype(np.float32), far[1])
    assert dist2.min() > 512
    return near, far


def _consts():
    g = _gammas()
    c = np.arange(128, dtype=np.float64)
    decq = np.stack([g[h] ** (c + 1) for h in range(4)], axis=1)
    deck = np.stack([(128.0 ** -0.5) * g[h] ** (-(c + 1)) for h in range(4)], axis=1)
    g128 = np.repeat(g ** 128, 128)[None, :].repeat(128, axis=0)
    caus = (np.arange(128)[None, :] >= np.arange(128)[:, None]).astype(np.float32)
    ident = np.eye(128, dtype=np.float32)
    shif = np.zeros((128, 128), np.float32)
    for k in range(64):
        shif[k, 64 + k] = 1.0
    small = np.zeros((128, 8), np.float32)
    small[:, 0:4] = decq
    small[:, 4:8] = deck
    near, far = _mask_tables()
    return dict(small=small, g128=g128.astype(np.float32), caus=caus, ident=ident, shif=shif,
                masks=near, fmasks=far)


def _rope_tables(pos):
    pos = pos.astype(np.float32)
    inv = (10000.0 ** (-np.arange(0, 128, 2, dtype=np.float32) / 128.0)).astype(np.float32)
    ang = (pos[:, None] * inv[None, :]).astype(np.float32)
    cos = np.repeat(np.cos(ang).astype(np.float32), 2, axis=1)
    sin = np.repeat(np.sin(ang).astype(np.float32), 2, axis=1)
    sgn = np.tile(np.array([-1.0, 1.0], np.float32), 64)[None, :]
    return np.ascontiguousarray(cos), np.ascontiguousarray(sin * sgn)


def _units():
    units = []
    for f in (0, 1):
        for mc in range(NMC):
            units.append(("GU", f, mc))
        for cg in range(2):
            for q in range(6):
                units.append(("DN", f, cg, q))
    for g in range(5):
        for half in range(2):
            units.append(("TM", g, half))
    for i in range(4):
        units.append(("FM", i))
    for cg in range(2):
        for half in range(2):
            units.append(("OUT", cg, half))
    return units


UNITS = _units()
UIDX = {u: i for i, u in enumerate(UNITS)}
TM_COL = {0: 0, 1: 512, 2: 1024, 3: 1536, 4: 3072}


def build(nh=8, no=8, dbg=()):
    nc = bass.Bass("TRN2", target_bir_lowering=False)
    sc = Sched()
    ntile = nh + no
    ntok = ntile * T

    def din(name, shape, dt=F32):
        return nc.dram_tensor(name, list(shape), dt, kind="ExternalInput")

    xin = din("xin", [ntok, D])
    cosd = din("cosd", [ntok, 128])
    ssd = din("ssd", [ntok, 128])
    wd = {
        ("g", 0): din("w_g1", [D, DFF]), ("u", 0): din("w_u1", [D, DFF]), ("d", 0): din("w_d1", [DFF, D]),
        ("g", 1): din("w_g2", [D, DFF]), ("u", 1): din("w_u2", [D, DFF]), ("d", 1): din("w_d2", [DFF, D]),
    }
    w_in = din("w_in", [D, INCOLS])
    w_out = din("w_out", [D, D])
    gains_d = din("gains", [4, 128, D])
    rgain_d = din("rgain", [128, 512])
    small_d = din("small", [128, 8])
    g128_d = din("g128", [128, 512])
    caus_d = din("caus", [128, 128])
    ident_d = din("ident", [128, 128])
    shif_d = din("shif", [128, 128])
    masks_d = din("masks", [NMASK, 128, 512])
    fmasks_d = din("fmasks", [2, 128, 128])
    flags_d = din("flags", [128, 64])
    yout = nc.dram_tensor("yout", [no * T, D], F32, kind="ExternalOutput")
    wsc = nc.dram_tensor("wsc", [len(UNITS), 128, 2048], BF16)
    dbg_out = {}
    for name, shape in dbg:
        dbg_out[name] = nc.dram_tensor("dbg_" + name, list(shape), F32, kind="ExternalOutput")

    def sb(name, cols, dt):
        return nc.alloc_sbuf_tensor("s_" + name, [128, cols], dt)

    xt = [sb("xt0", 4096, F32), sb("xt1", 4096, F32)]
    Vb = sb("Vb", 4096, BF16)
    xnT = sb("xnT", 4096, BF16)
    U = sb("U", 24 * 512, BF16)
    sil = [sb("sil0", 512, F32), sb("sil1", 512, F32)]
    sil_bf = [t_.bitcast(BF16) for t_ in sil]
    rt = [sb("rt0", 512, F32), sb("rt1", 512, F32)]
    aqT = sb("aqT", 8 * 512, BF16)
    kring = sb("kring", 4 * NRING * 128, BF16)
    vring = sb("vring", NRING * VW, BF16)
    masks = sb("masks", NMASK * 512, BF16)
    fmasks = sb("fmasks", 2 * 128, BF16)
    eb = [sb("eb%d" % i_, 512, BF16) for i_ in range(4)]
    wr = sb("wr", NSLOT * 2048, BF16)
    tab = sb("tab", 2 * 512, F32)
    gn = sb("gn", 2 * 1024, F32)
    rgain = sb("rgain_sb", 512, F32)
    Sf = sb("Sf", 512, F32)
    Sb = sb("Sb", 6 * 512, BF16)
    retf = sb("retf", 512, F32)
    sqf = sb("sqf", 512, F32)
    rsb = sb("rsb2", 512, F32)
    lnd = sb("lnd", 512, F32)
    scb = [sb("scb0", 512, BF16), sb("scb1", 512, BF16)]
    Rb = sb("Rb", 512, BF16)
    ident = sb("ident_sb", 128, BF16)
    shif = sb("shif_sb", 128, BF16)
    caus = sb("caus_sb", 128, F32)
    small = sb("small_sb", 8, F32)
    g128 = sb("g128_sb", 512, F32)
    flagsb = sb("flags_sb", 64, BF16)
    onesb = sb("ones_sb", 64, BF16)
    st = sb("st", 32, F32)
    ss = sb("ss", 8, F32)

    B = [nc.alloc_psum_tensor("B%d" % i, [128, 512], F32) for i in range(4)]
    R = [nc.alloc_psum_tensor("R%d" % i, [128, 512], F32) for i in range(2)]
    TB = [nc.alloc_psum_tensor("TB%d" % i, [128, 1024], BF16) for i in range(2)]

    def pstep(t):
        return t[:, :].ap[0][0]

    def ap(t, off, dims, p0=0, npart=128):
        ps = pstep(t)
        return bass.AP(t, p0 * ps + off, [[ps, npart]] + [list(d) for d in dims])

    cload_keys = []

    def cload(dst_ap, src_ap, key, eng="sp"):
        sc.add(eng, lambda e, o=dst_ap, i=src_ap: e.dma_start(out=o, in_=i), writes=[key], dma=("c", key))
        cload_keys.append(key)

    cload(small[:, :], small_d[:, :], "small")
    cload(g128[:, :], g128_d[:, :], "g128")
    cload(caus[:, :], caus_d[:, :], "caus")
    cload(rgain[:, :], rgain_d[:, :], "rgain")
    cload(ident[:, :], ident_d[:, :], "ident", eng="pool")
    cload(shif[:, :], shif_d[:, :], "shif", eng="pool")
    cload(flagsb[:, :], flags_d[:, :], "flags", eng="pool")
    cload(masks[:, :].rearrange("p (r q) -> p r q", r=NMASK), masks_d[:, :, :].rearrange("r p q -> p r q"),
          "masks", eng="pool")
    cload(fmasks[:, :].rearrange("p (r q) -> p r q", r=2), fmasks_d[:, :, :].rearrange("r p q -> p r q"),
          "fmasks", eng="pool")
    sc.add("dve", lambda e: e.memset(onesb[:, :], 1.0), writes=["ones"])
    sc.add("dve", lambda e: e.memset(vring[:, :], 0.0), writes=["vring_init"])
    sc.add("dve", lambda e: e.memset(kring[:, :], 0.0), writes=["kring_init"])
    sc.add("dve", lambda e: e.memset(Sf[:, :], 0.0), writes=["Sf"])
    sc.add("dve", lambda e: e.memset(Sb[:, :], 0.0), writes=["Sb_init"])
    sc.add("dve", lambda e: e.memset(st[:, 0:28], 0.0), writes=["st_init"])
    allc = cload_keys + ["ones", "vring_init", "kring_init", "Sb_init", "st_init"]
    for e in ("pe", "act", "dve", "pool"):
        if e == "pe":
            sc.add(e, lambda en: en.transpose(out=TB[1][:, 0:128], in_=ident[:, :], identity=ident[:, :]),
                   reads=allc, writes=[("T", 1)])
        elif e == "act":
            sc.add(e, lambda en: en.copy(out=st[:, 28:30], in_=small[:, 0:2]), reads=allc, writes=["bar_act"])
        elif e == "dve":
            sc.add(e, lambda en: en.memset(st[:, 30:32], 0.0), reads=allc, writes=["bar_dve"])
        else:
            sc.add(e, lambda en: en.memset(aqT[:, :], 0.0), reads=allc, writes=[("aqT", i) for i in range(8)])

    cv_state = {"next": 0, "n": 0}

    def wview(w):
        return w[:, :].rearrange("(kc p) n -> p kc n", p=128)

    def conv_dma(dst, src, key):
        k = cv_state["n"]
        cv_state["n"] += 1
        sc.add("pool", lambda e, o=dst, i=src: e.dma_start(out=o, in_=i),
               writes=[key, ("cvslot", k % NCV)], dma=("cv", k % NCV))

    def conv_unit(ui):
        u = UNITS[ui]
        dst = wsc[ui, :, :]
        if u[0] == "GU":
            _, f, mc = u
            d4 = dst.rearrange("p (t kc j) -> p t kc j", t=2, kc=8)
            conv_dma(d4[:, 0, :, :], wview(wd[("g", f)])[:, :, mc * 128:(mc + 1) * 128], ("wsc", ui, 0))
            conv_dma(d4[:, 1, :, :], wview(wd[("u", f)])[:, :, mc * 128:(mc + 1) * 128], ("wsc", ui, 1))
        elif u[0] == "DN":
            _, f, cg, q = u
            nk = 4 if q < 5 else 2
            d3 = dst.rearrange("p (k n) -> p k n", k=4)
            conv_dma(d3[:, 0:nk, :], wview(wd[("d", f)])[:, 4 * q:4 * q + nk, cg * 512:(cg + 1) * 512],
                     ("wsc", ui, 0))
        elif u[0] == "TM":
            _, g, half = u
            c0 = TM_COL[g]
            d3 = dst.rearrange("p (k n) -> p k n", k=4)
            conv_dma(d3, wview(w_in)[:, 4 * half:4 * half + 4, c0:c0 + 512], ("wsc", ui, 0))
        elif u[0] == "FM":
            _, i = u
            d4 = dst.rearrange("p (t kc j) -> p t kc j", t=2, kc=8)
            for m in range(2):
                c0 = 2048 + (2 * i + m) * 128
                conv_dma(d4[:, m, :, :], wview(w_in)[:, :, c0:c0 + 128], ("wsc", ui, m))
        else:
            _, cg, half = u
            d3 = dst.rearrange("p (k n) -> p k n", k=4)
            conv_dma(d3, wview(w_out)[:, 4 * half:4 * half + 4, cg * 512:(cg + 1) * 512], ("wsc", ui, 0))

    conv_done = set()

    def ensure_conv(ui):
        if ui not in conv_done:
            conv_done.add(ui)
            conv_unit(ui)

    stream = []
    stream_pos = {"issued": 0}

    def unit_list(kind):
        l = [UIDX[("GU", 0, mc)] for mc in range(NMC)]
        l += [UIDX[("DN", 0, cg, q)] for cg in range(2) for q in range(6)]
        if kind == "halo_kv" or kind == "halo_k":
            gs = [1, 2]
        else:
            gs = [1, 2, 0, 3]
        for g in gs:
            l += [UIDX[("TM", g, h)] for h in range(2)]
        if kind == "halo_kv":
            l += [UIDX[("FM", 2)], UIDX[("FM", 3)]]
            l += [UIDX[("TM", 4, h)] for h in range(2)]
        if kind == "own":
            l += [UIDX[("FM", i)] for i in range(4)]
            l += [UIDX[("TM", 4, h)] for h in range(2)]
            l += [UIDX[("OUT", cg, h)] for cg in range(2) for h in range(2)]
            l += [UIDX[("GU", 1, mc)] for mc in range(NMC)]
            l += [UIDX[("DN", 1, cg, q)] for cg in range(2) for q in range(6)]
        return l

    def tile_kind(t):
        if t >= nh:
            return "own"
        return "halo_kv" if t >= nh - 4 else "halo_k"

    for t in range(ntile):
        stream += unit_list(tile_kind(t))
    nstream = len(stream)
    use_ptr = {"i": 0}

    def issue_load(qi):
        ui = stream[qi]
        ensure_conv(ui)
        slot = qi % NSLOT
        nparts = 2 if UNITS[ui][0] in ("GU", "FM") else 1
        ncol = 1024 if (UNITS[ui][0] == "DN" and UNITS[ui][3] == 5) else 2048
        sc.add("sp", lambda e, s=slot, u=ui, ncol=ncol: e.dma_start(out=wr[:, s * 2048:s * 2048 + ncol],
                                                                    in_=wsc[u, :, 0:ncol]),
               reads=[("wsc", ui, p) for p in range(nparts)], writes=[("w", slot)], dma=("w", slot))

    def next_unit(expect):
        qi = use_ptr["i"]
        assert stream[qi] == expect, (UNITS[stream[qi]], UNITS[expect])
        while stream_pos["issued"] < min(nstream, qi + NSLOT - 1):
            issue_load(stream_pos["issued"])
            stream_pos["issued"] += 1
        for k in range(qi, min(nstream, qi + 14)):
            ensure_conv(stream[k])
        use_ptr["i"] += 1
        return qi % NSLOT

    def wslice(slot, off, n):
        return wr[:, slot * 2048 + off: slot * 2048 + off + n]

    def dump(name, src_ap, keys):
        if name in dbg_out:
            d = dbg_out[name]
            sc.add("sp", lambda e, o=d[:, :], i=src_ap: e.dma_start(out=o, in_=i), reads=keys,
                   dma=("dbg", name))

    gn_state = {"n": 0}

    def load_gain(gi):
        s = gn_state["n"] % 2
        gn_state["n"] += 1
        sc.add("sp", lambda e, s=s, gi=gi: e.dma_start(out=gn[:, s * 1024:(s + 1) * 1024], in_=gains_d[gi, :, :]),
               writes=[("gn", s)], dma=("gn", s))
        return s

    def xkeys(par):
        return [("xt", par, tc) for tc in range(4)]

    def norm_stage(par, gslot, final=False, chain_only=False):
        x = xt[par]
        for tc in range(4):
            sc.add("act", lambda e, tc=tc: e.activation(
                out=Vb[:, tc * 1024:(tc + 1) * 1024], in_=x[:, tc * 1024:(tc + 1) * 1024], func=AF.Square,
                accum_out=ss[:, tc:tc + 1]),
                reads=[("xt", par, tc)], writes=[("V", 2 * tc), ("V", 2 * tc + 1), ("ss", tc)])
        sc.add("dve", lambda e: e.tensor_scalar(out=ss[:, 4:8], in0=ss[:, 0:4], scalar1=1.0 / D, scalar2=EPS,
                                                 op0=ALU.mult, op1=ALU.add),
               reads=[("ss", tc) for tc in range(4)], writes=["rs"])
        sc.add("act", lambda e: e.activation(out=ss[:, 4:8], in_=ss[:, 4:8], func=AF.Sqrt),
               reads=["rs"], writes=["rs"])
        sc.add("dve", lambda e: e.reciprocal(out=ss[:, 4:8], in_=ss[:, 4:8]),
               reads=["rs"], writes=["rs"])
        for tc in range(4):
            if final:
                o = x[:, tc * 1024:(tc + 1) * 1024]
                wk = [("xt", par, tc)]
            else:
                o = Vb[:, tc * 1024:(tc + 1) * 1024]
                wk = [("V", 2 * tc), ("V", 2 * tc + 1)]
            sc.add("dve", lambda e, tc=tc, o=o: e.scalar_tensor_tensor(
                out=o, in0=x[:, tc * 1024:(tc + 1) * 1024], scalar=ss[:, 4 + tc:5 + tc],
                in1=gn[:, gslot * 1024:(gslot + 1) * 1024], op0=ALU.mult, op1=ALU.mult),
                reads=[("xt", par, tc), "rs", ("gn", gslot)], writes=wk)
        if final or chain_only:
            return
        norm_transposes()

    def norm_transposes():
        for kp in range(4):
            tb = TB[kp % 2]

            def f(e, kp=kp, tb=tb):
                ins = None
                for k2 in range(2):
                    kc = 2 * kp + k2
                    for tc in range(4):
                        ins = e.transpose(out=tb[:, k2 * 512 + tc * 128: k2 * 512 + (tc + 1) * 128],
                                          in_=Vb[:, tc * 1024 + kc * 128: tc * 1024 + (kc + 1) * 128],
                                          identity=ident[:, :])
                return ins
            sc.add("pe", f, reads=[("V", i) for i in range(8)], writes=[("T", kp % 2)])
            eng = "act" if kp % 2 == 0 else "dve"
            if eng == "act":
                fn = lambda e, kp=kp, tb=tb: e.copy(out=xnT[:, kp * 1024:(kp + 1) * 1024], in_=tb[:, :])
            else:
                fn = lambda e, kp=kp, tb=tb: e.tensor_copy(out=xnT[:, kp * 1024:(kp + 1) * 1024], in_=tb[:, :])
            sc.add(eng, fn, reads=[("T", kp % 2)], writes=[("xnT", 2 * kp), ("xnT", 2 * kp + 1)])

    XNT = [("xnT", k) for k in range(8)]

    DNBANK = [[(("B", i), B[i]) for i in range(4)],
              [(("R", 0), R[0]), (("R", 1), R[1]), (("B", 0), B[0]), (("B", 1), B[1])]]

    def ffn_stage(par, f, gu_hook=None, mid_hook=None, early_hook=None):
        x = xt[par]
        for mc in range(NMC):
            if mc == 8 and early_hook is not None:
                early_hook()
            slot = next_unit(UIDX[("GU", f, mc)])
            bg, bu = B[2 * (mc % 2)], B[2 * (mc % 2) + 1]
            for which, bank, bk in ((0, bg, 2 * (mc % 2)), (1, bu, 2 * (mc % 2) + 1)):
                def fmm(e, slot=slot, which=which, bank=bank):
                    ins = None
                    for kc in range(8):
                        ins = e.matmul(bank[:, :], lhsT=wslice(slot, (which * 8 + kc) * 128, 128),
                                       rhs=xnT[:, kc * 512:(kc + 1) * 512], start=(kc == 0), stop=(kc == 7))
                    return ins
                sc.add("pe", fmm, reads=XNT + [("w", slot)], writes=[("B", bk)])
            sl = sil[mc % 2]
            sc.add("act", lambda e, bg=bg, sl=sl: e.activation(out=sl[:, :], in_=bg[:, :], func=AF.Silu),
                   reads=[("B", 2 * (mc % 2))], writes=[("sil", mc % 2)])
            sc.add("dve", lambda e, bu=bu, sl=sl, mc=mc: e.tensor_tensor(
                out=U[:, mc * 512:(mc + 1) * 512], in0=bu[:, :], in1=sl[:, :], op=ALU.mult),
                reads=[("B", 2 * (mc % 2) + 1), ("sil", mc % 2)], writes=[("U", mc)])
        if gu_hook is not None:
            gu_hook()
        for cg in range(2):
            if cg == 1 and mid_hook is not None:
                mid_hook()
            for q in range(6):
                slot = next_unit(UIDX[("DN", f, cg, q)])
                nk = 4 if q < 5 else 2
                for tc in range(4):
                    bkey, bank = DNBANK[cg][tc]

                    def fmm(e, slot=slot, q=q, nk=nk, tc=tc, bank=bank):
                        ins = None
                        for i in range(nk):
                            kc = 4 * q + i
                            ins = e.matmul(bank[:, :], lhsT=U[:, kc * 512 + tc * 128: kc * 512 + (tc + 1) * 128],
                                           rhs=wslice(slot, i * 512, 512), start=(kc == 0), stop=(kc == NMC - 1))
                        return ins
                    sc.add("pe", fmm, reads=[("U", 4 * q + i) for i in range(nk)] + [("w", slot)],
                           writes=[bkey])
            for tc in range(4):
                bkey, bank = DNBANK[cg][tc]
                xs = x[:, tc * 1024 + cg * 512: tc * 1024 + (cg + 1) * 512]
                sc.add("dve", lambda e, tc=tc, xs=xs, bank=bank: e.scalar_tensor_tensor(
                    out=xs, in0=bank[:, :], scalar=0.5, in1=xs, op0=ALU.mult, op1=ALU.add),
                    reads=[bkey, ("xt", par, tc)], writes=[("xt", par, tc)])

    bank_rr = {"i": 0}

    def nextbank():
        i = bank_rr["i"] % 4
        bank_rr["i"] += 1
        return i

    def tm_group(g, strided=False):
        s0 = next_unit(UIDX[("TM", g, 0)])
        s1 = next_unit(UIDX[("TM", g, 1)])
        res = []
        for tc in range(4):
            bi = nextbank()

            def fmm(e, tc=tc, bi=bi, s0=s0, s1=s1):
                ins = None
                for kc in range(8):
                    s = s0 if kc < 4 else s1
                    lt = (ap(xnT, kc * 512 + tc, [[4, 128]]) if strided
                          else xnT[:, kc * 512 + tc * 128: kc * 512 + (tc + 1) * 128])
                    ins = e.matmul(B[bi][:, :], lhsT=lt,
                                   rhs=wslice(s, (kc % 4) * 512, 512), start=(kc == 0), stop=(kc == 7))
                return ins
            sc.add("pe", fmm, reads=XNT + [("w", s0), ("w", s1)], writes=[("B", bi)])
            res.append((tc, bi))
            yield tc, bi

    def rotary(tc, bi, is_q):
        bank = B[bi]
        cosb = ap(tab, tc * 128, [[0, 4], [1, 128]])
        sse = ap(tab, 512 + tc * 128, [[0, 4], [2, 64]])
        sso = ap(tab, 512 + tc * 128 + 1, [[0, 4], [2, 64]])
        be = ap(bank, 0, [[128, 4], [2, 64]])
        bo = ap(bank, 1, [[128, 4], [2, 64]])
        r1e = ap(rt[1], 0, [[128, 4], [2, 64]])
        r1o = ap(rt[1], 1, [[128, 4], [2, 64]])
        sc.add("dve", lambda e: e.tensor_tensor(out=rt[0][:, :], in0=bank[:, :], in1=cosb, op=ALU.mult),
               reads=[("B", bi), "tabc"], writes=["rt0"])
        sc.add("dve", lambda e: e.tensor_tensor(out=r1e, in0=bo, in1=sse, op=ALU.mult),
               reads=[("B", bi), "tabs"], writes=["rt1e"])
        sc.add("dve", lambda e: e.tensor_tensor(out=r1o, in0=be, in1=sso, op=ALU.mult),
               reads=[("B", bi), "tabs"], writes=["rt1o"])
        sc.add("pool", lambda e: e.tensor_tensor(out=rt[0][:, :], in0=rt[0][:, :], in1=rt[1][:, :], op=ALU.add),
               reads=["rt0", "rt1e", "rt1o"], writes=["rt0"])
        page = tc if is_q else 4 + tc
        dec = ap(small, 0 if is_q else 4, [[1, 4], [0, 128]])
        sc.add("pool", lambda e: e.tensor_tensor(out=U[:, page * 512:(page + 1) * 512], in0=rt[0][:, :], in1=dec,
                                                  op=ALU.mult),
               reads=["rt0"], writes=[("U", page)])
        return page

    tb_rr = {"i": 0}

    def transpose4(src_page_ap_fn, src_keys, dst_ap, dst_keys, evac_eng):
        ti = tb_rr["i"] % 2
        tb_rr["i"] += 1
        tb = TB[ti]

        def f(e):
            ins = None
            for h in range(4):
                ins = e.transpose(out=tb[:, h * 128:(h + 1) * 128], in_=src_page_ap_fn(h), identity=ident[:, :])
            return ins
        sc.add("pe", f, reads=src_keys, writes=[("T", ti)])
        src = ap(tb, 0, [[128, 4], [1, 128]])
        if evac_eng == "act":
            sc.add("act", lambda e: e.copy(out=dst_ap, in_=src), reads=[("T", ti)], writes=dst_keys)
        else:
            sc.add(evac_eng, lambda e: e.tensor_copy(out=dst_ap, in_=src), reads=[("T", ti)], writes=dst_keys)

    def inproj_stage(t, kind, hook_after_rv=None):
        tau = t
        n0 = 4 * tau
        deferred = []
        for tc, bi in tm_group(1):
            pg = rotary(tc, bi, False)
            if kind == "own":
                deferred.append(lambda tc=tc, pg=pg: transpose4(
                    lambda h, pg=pg: U[:, pg * 512 + h * 128: pg * 512 + (h + 1) * 128], [("U", pg)],
                    ap(U, 20 * 512 + tc * 128, [[512, 4], [1, 128]]), [("U", 20 + h) for h in range(4)], "act"))
        for tc, bi in tm_group(2):
            sc.add("act", lambda e, tc=tc, bi=bi: e.copy(out=U[:, (8 + tc) * 512:(9 + tc) * 512], in_=B[bi][:, :]),
                   reads=[("B", bi)], writes=[("U", 8 + tc)])
        for f_ in deferred:
            f_()
        deferred = []
        if hook_after_rv is not None:
            hook_after_rv()
        for tc in range(4):
            ri = tc % 2

            def fkv(e, tc=tc, ri=ri):
                ins = None
                for h in range(4):
                    ins = e.matmul(R[ri][:, h * 128:(h + 1) * 128],
                                   lhsT=U[:, (4 + tc) * 512 + h * 128:(4 + tc) * 512 + (h + 1) * 128],
                                   rhs=U[:, (8 + tc) * 512 + h * 128:(8 + tc) * 512 + (h + 1) * 128],
                                   start=True, stop=True)
                return ins
            sc.add("pe", fkv, reads=[("U", 4 + tc), ("U", 8 + tc)], writes=[("R", ri)])
            sc.add("dve", lambda e, ri=ri: e.tensor_tensor(out=Sf[:, :], in0=R[ri][:, :], in1=Sf[:, :], op=ALU.add),
                   reads=[("R", ri), "Sf"], writes=["Sf"])
            sc.add("dve", lambda e: e.tensor_tensor(out=Sf[:, :], in0=Sf[:, :], in1=g128[:, :], op=ALU.mult),
                   reads=["Sf"], writes=["Sf"])
            sl = (n0 + tc + 1) % 6
            sc.add("act", lambda e, sl=sl: e.copy(out=Sb[:, sl * 512:(sl + 1) * 512], in_=Sf[:, :]),
                   reads=["Sf"], writes=[("Sb", sl)])
        if kind == "halo_k":
            return
        if kind == "own":
            for tc, bi in tm_group(0):
                pg = rotary(tc, bi, True)
                deferred.append(lambda tc=tc, pg=pg: transpose4(
                    lambda h, pg=pg: U[:, pg * 512 + h * 128: pg * 512 + (h + 1) * 128], [("U", pg)],
                    ap(U, 16 * 512 + tc * 128, [[512, 4], [1, 128]]), [("U", 16 + h) for h in range(4)], "act"))
            for tc, bi in tm_group(3):
                sc.add("act", lambda e, tc=tc, bi=bi: e.activation(out=U[:, (12 + tc) * 512:(13 + tc) * 512],
                                                                    in_=B[bi][:, :], func=AF.Silu),
                       reads=[("B", bi)], writes=[("U", 12 + tc)])
                sc.add("pool", lambda e, tc=tc: e.tensor_tensor(
                    out=U[:, (12 + tc) * 512:(13 + tc) * 512], in0=U[:, (12 + tc) * 512:(13 + tc) * 512],
                    in1=rgain[:, :], op=ALU.mult), reads=[("U", 12 + tc)], writes=[("U", 12 + tc)])
            for f_ in deferred:
                f_()
            deferred = []
        s0 = (4 * tau) % NRING
        fm_list = [0, 1, 2, 3] if kind == "own" else [2, 3]
        for i in fm_list:
            slot = next_unit(UIDX[("FM", i)])
            for m in range(2):
                mc = 2 * i + m
                bi = nextbank()

                def fmm(e, slot=slot, m=m, bi=bi):
                    ins = None
                    for kc in range(8):
                        ins = e.matmul(B[bi][:, :], lhsT=wslice(slot, (m * 8 + kc) * 128, 128),
                                       rhs=xnT[:, kc * 512:(kc + 1) * 512], start=(kc == 0), stop=(kc == 7))
                    return ins
                sc.add("pe", fmm, reads=XNT + [("w", slot)], writes=[("B", bi)])
                if mc < 4:
                    for a_ in range(2):
                        hd_ = 2 * mc + a_
                        sc.add("act", lambda e, hd_=hd_, a_=a_, bi=bi: e.activation(
                            out=aqT[64 * a_:64 * a_ + 64, hd_ * 512:(hd_ + 1) * 512],
                            in_=B[bi][64 * a_:64 * a_ + 64, :], func=AF.Copy, scale=0.125),
                            reads=[("B", bi)], writes=[("aqT", hd_)])
                else:
                    j = mc - 4
                    off = j * NRING * 128 + s0 * 128
                    sc.add("dve", lambda e, off=off, bi=bi: e.tensor_copy(out=kring[:, off:off + 512], in_=B[bi][:, :]),
                           reads=[("B", bi)], writes=[("kr", j, s0 + c) for c in range(4)])
        for tc, bi in tm_group(4, strided=True):
            slot_v = s0 + tc
            dst = ap(vring, slot_v * VW, [[192, 4], [128, 2], [1, 64]])
            src = ap(B[bi], 0, [[128, 4], [64, 2], [1, 64]])
            sc.add("act", lambda e, dst=dst, src=src: e.copy(out=dst, in_=src), reads=[("B", bi)],
                   writes=[("vr", slot_v)])
            odst = ap(vring, slot_v * VW + 64, [[192, 4], [1, 64]])
            osrc_t = onesb if kind == "own" else flagsb
            osrc = ap(osrc_t, 0, [[0, 4], [1, 64]])
            sc.add("pool", lambda e, odst=odst, osrc=osrc: e.tensor_copy(out=odst, in_=osrc), reads=[],
                   writes=[("vr1", slot_v)])

    ybf = [sil_bf[k // 2][:, (k % 2) * 512:(k % 2) * 512 + 512] for k in range(4)]

    def retention_stage(t):
        n0 = 4 * t
        cb = ap(caus, 0, [[0, 4], [1, 128]])
        pend = []

        def sc_op(tc):
            def fsc(e, tc=tc):
                ins = None
                for h in range(4):
                    ins = e.matmul(TBf[0][:, h * 128:(h + 1) * 128],
                                   lhsT=U[:, (20 + h) * 512 + tc * 128:(20 + h) * 512 + (tc + 1) * 128],
                                   rhs=U[:, (16 + h) * 512 + tc * 128:(16 + h) * 512 + (tc + 1) * 128],
                                   start=True, stop=True)
                return ins
            sc.add("pe", fsc, reads=[("U", 16 + h) for h in range(8)], writes=[("T", 0)])
            sb_ = scb[tc % 2]
            sc.add("dve", lambda e, sb_=sb_: e.tensor_tensor(
                out=sb_[:, :], in0=TBf[0][:, :], in1=cb, op=ALU.mult),
                reads=[("T", 0)], writes=[("scb", tc % 2)])

        def out_op(tc):
            sb_ = scb[tc % 2]
            ri = tc % 2
            sl = (n0 + tc) % 6

            def fo(e, tc=tc, ri=ri, sb_=sb_, sl=sl):
                ins = None
                for h in range(4):
                    e.matmul(TBf[1][:, h * 128:(h + 1) * 128], lhsT=sb_[:, h * 128:(h + 1) * 128],
                             rhs=U[:, (8 + tc) * 512 + h * 128:(8 + tc) * 512 + (h + 1) * 128],
                             start=True, stop=False)
                    ins = e.matmul(TBf[1][:, h * 128:(h + 1) * 128],
                                   lhsT=U[:, (16 + h) * 512 + tc * 128:(16 + h) * 512 + (tc + 1) * 128],
                                   rhs=Sb[:, sl * 512 + h * 128: sl * 512 + (h + 1) * 128],
                                   start=False, stop=True)
                return ins
            sc.add("pe", fo, reads=[("scb", tc % 2), ("U", 8 + tc), ("Sb", sl)] + [("U", 16 + h) for h in range(4)],
                   writes=[("T", 1)])
            sc.add("act", lambda e: e.copy(out=retf[:, :], in_=TBf[1][:, :]), reads=[("T", 1)],
                   writes=["retf"])
            sc.add("act", lambda e: e.activation(out=sqf[:, :], in_=TBf[1][:, :], func=AF.Square),
                   reads=[("T", 1)], writes=["sqf"])
            sc.add("dve", lambda e: e.tensor_reduce(out=st[:, 0:4], in_=ap(retf, 0, [[128, 4], [1, 128]]),
                                                    axis=AX.X, op=ALU.add), reads=["retf"], writes=["st_a"])
            sc.add("dve", lambda e: e.tensor_reduce(out=st[:, 4:8], in_=ap(sqf, 0, [[128, 4], [1, 128]]),
                                                    axis=AX.X, op=ALU.add), reads=["sqf"], writes=["st_b"])
            sc.add("dve", lambda e: e.tensor_single_scalar(out=st[:, 8:16], in_=st[:, 0:8], scalar=1.0 / 128,
                                                           op=ALU.mult), reads=["st_a", "st_b"], writes=["st_c"])
            sc.add("dve", lambda e: e.tensor_tensor(out=st[:, 16:20], in0=st[:, 8:12], in1=st[:, 8:12], op=ALU.mult),
                   reads=["st_c"], writes=["st_d"])
            sc.add("dve", lambda e: e.tensor_tensor(out=st[:, 20:24], in0=st[:, 12:16], in1=st[:, 16:20],
                                                    op=ALU.subtract), reads=["st_c", "st_d"], writes=["st_e"])
            sc.add("dve", lambda e: e.tensor_single_scalar(out=st[:, 24:28], in_=st[:, 20:24], scalar=EPS, op=ALU.add),
                   reads=["st_e"], writes=["st_f"])
            sc.add("act", lambda e: e.activation(out=st[:, 24:28], in_=st[:, 24:28], func=AF.Sqrt),
                   reads=["st_f"], writes=["st_f"])
            sc.add("dve", lambda e: e.reciprocal(out=st[:, 24:28], in_=st[:, 24:28]),
                   reads=["st_f"], writes=["st_f"])
            mean_b = ap(st, 8, [[1, 4], [0, 128]])
            rstd_b = ap(st, 24, [[1, 4], [0, 128]])
            for h in range(4):
                sc.add("dve", lambda e, h=h: e.tensor_scalar(
                    out=retf[:, h * 128:(h + 1) * 128], in0=retf[:, h * 128:(h + 1) * 128],
                    scalar1=st[:, 8 + h:9 + h], scalar2=st[:, 24 + h:25 + h], op0=ALU.subtract, op1=ALU.mult),
                    reads=["retf", "st_c", "st_f"], writes=["retf"])
            yb = ybf[tc]
            ykeys = [("ybf", tc), ("sil", tc // 2)]
            sc.add("pool", lambda e, tc=tc, yb=yb: e.tensor_tensor(out=yb, in0=retf[:, :],
                                                                   in1=U[:, (12 + tc) * 512:(13 + tc) * 512], op=ALU.mult),
                   reads=["retf", ("U", 12 + tc)], writes=ykeys)
            base = (tc % 2) * 512
            sbt = sil_bf[tc // 2]
            pend.append(lambda tc=tc, sbt=sbt, base=base, ykeys=ykeys: transpose4(
                lambda h: sbt[:, base + h * 128: base + (h + 1) * 128], ykeys,
                ap(Vb, tc * 128, [[512, 4], [1, 128]]), [("V", h) for h in range(4)], "dve"))

        return sc_op, out_op, pend

    SBANKS = [(("B", 0), B[0]), (("B", 1), B[1]), (("R", 1), R[1])]
    TBf = [TB[0].bitcast(F32), TB[1].bitcast(F32)]

    def attention_stage(t, ret):
        ret_sc, ret_out, pend = ret
        units = [("n", 0, g) for g in range(4)] + [("n", -1, g) for g in range(4)] + [("f", d) for d in (-2, -3, -4)]
        NU = len(units)
        stream = [(hd, u) for hd in range(8) for u in range(NU)]
        hooks = {}

        def krcols(j, d, g):
            return j * NRING * 128 + ((t + d) % 5) * 512 + g

        def s_op(k):
            hd, u = stream[k]
            j = hd // 2
            un = units[u]
            sbk = k % 3
            skey, sbank = SBANKS[sbk]
            if un[0] == "n":
                _, d, g = un
                kc0 = krcols(j, d, g)
                rk = [("kr", j, ((t + d) % 5) * 4 + c) for c in range(4)]
                sc.add("pe", lambda e, kc0=kc0, sbank=sbank, hd=hd: e.matmul(
                    sbank[:, :], lhsT=ap(kring, kc0, [[4, 128]]), rhs=aqT[:, hd * 512:(hd + 1) * 512],
                    start=True, stop=True), reads=rk + [("aqT", hd)], writes=[skey])
                mk = masks[:, ((d + 1) * 4 + g) * 512:((d + 1) * 4 + g + 1) * 512]
            else:
                _, d = un
                rk = [("kr", j, ((t + d) % 5) * 4 + c) for c in range(4)]
                kcs = [krcols(j, d, g) for g in range(4)]

                def fs(e, kcs=kcs, sbank=sbank, hd=hd):
                    ins = None
                    for g in range(4):
                        ins = e.matmul(sbank[:, g * 128:(g + 1) * 128], lhsT=ap(kring, kcs[g], [[4, 128]]),
                                       rhs=ap(aqT, hd * 512 + g, [[4, 128]]), start=True, stop=True)
                    return ins
                sc.add("pe", fs, reads=rk + [("aqT", hd)], writes=[skey])
                fk = 0 if d == -4 else 1
                mk = ap(fmasks, fk * 128, [[0, 4], [1, 128]])
            ebk = k % 4
            sc.add("act", lambda e, ebk=ebk, sbank=sbank: e.activation(out=eb[ebk][:, :], in_=sbank[:, :], func=AF.Exp),
                   reads=[skey], writes=[("eb", ebk)])
            sc.add("dve", lambda e, ebk=ebk, mk=mk: e.tensor_tensor(
                out=eb[ebk][:, :], in0=eb[ebk][:, :], in1=mk, op=ALU.mult),
                reads=[("eb", ebk)], writes=[("eb", ebk)])

        def pv_op(k):
            hd, u = stream[k]
            j, a = hd // 2, hd % 2
            ob = 2 + hd % 2
            un = units[u]
            sbk = k % 4
            if un[0] == "n":
                _, d, g = un
                slot = ((t + d) % 5) * 4 + g
                vcol = slot * VW + j * 192 + 64 * a
                sc.add("pe", lambda e, vcol=vcol, sbk=sbk, u=u, ob=ob: e.matmul(
                    B[ob][:, :], lhsT=vring[:, vcol:vcol + 128], rhs=eb[sbk][:, :], start=(u == 0), stop=False),
                    reads=[("vr", slot), ("vr1", slot), ("eb", sbk)], writes=[("B", ob)])
            else:
                _, d = un
                slots = [((t + d) % 5) * 4 + g for g in range(4)]

                def fp(e, slots=slots, sbk=sbk, u=u, j=j, a=a, ob=ob):
                    ins = None
                    for g in range(4):
                        vcol = slots[g] * VW + j * 192 + 64 * a
                        ins = e.matmul(ap(B[ob], g, [[4, 128]]), lhsT=vring[:, vcol:vcol + 128],
                                       rhs=eb[sbk][:, g * 128:(g + 1) * 128], start=False,
                                       stop=(u == NU - 1 and g == 3))
                    return ins
                sc.add("pe", fp, reads=[("vr", sl_) for sl_ in slots] + [("vr1", sl_) for sl_ in slots] + [("eb", sbk)],
                       writes=[("B", ob)])

        def recip_fn(hd, piece):
            if piece != 0:
                return
            a = hd % 2
            ob = 2 + hd % 2
            d0_ = 64 if a == 0 else 0
            sc.add("act", lambda e, d0_=d0_, ob=ob: e.activation(
                out=lnd[d0_:d0_ + 64, :], in_=B[ob][d0_:d0_ + 64, :], func=AF.Ln),
                reads=[("B", ob)], writes=["lnd"])
            sc.add("act", lambda e, d0_=d0_: e.activation(
                out=Rb[d0_:d0_ + 64, :], in_=lnd[d0_:d0_ + 64, :], func=AF.Exp, scale=-1.0),
                reads=["lnd"], writes=[("Rb", p_) for p_ in range(4)])

        def post_fn(hd):
            j, a = hd // 2, hd % 2
            ob = 2 + hd % 2
            n0_ = 0 if a == 0 else 64
            ri = 0
            if a == 0:
                sc.add("pe", lambda e, ri=ri: e.matmul(R[ri][0:64, :], lhsT=ident[64:128, 64:128], rhs=Rb[64:128, :],
                                                       start=True, stop=True), reads=[("Rb", p_) for p_ in range(4)],
                       writes=[("R", ri)])
            else:
                sc.add("pe", lambda e, ri=ri: e.matmul(R[ri][:, :], lhsT=shif[0:64, :], rhs=Rb[0:64, :],
                                                       start=True, stop=True), reads=[("Rb", p_) for p_ in range(4)],
                       writes=[("R", ri)])
            sc.add("act", lambda e, ri=ri, n0_=n0_: e.copy(out=rsb[n0_:n0_ + 64, :], in_=R[ri][n0_:n0_ + 64, :]),
                   reads=[("R", ri)], writes=["rsb2"])
            pg = 4 + j
            sc.add("dve", lambda e, n0_=n0_, ob=ob, pg=pg: e.tensor_tensor(
                out=Vb[n0_:n0_ + 64, pg * 512:(pg + 1) * 512], in0=B[ob][n0_:n0_ + 64, :], in1=rsb[n0_:n0_ + 64, :],
                op=ALU.mult), reads=[("B", ob), "rsb2"], writes=[("V", pg)])

        nstream_ = len(stream)
        for hd in range(8):
            kend = hd * NU + NU - 1
            for piece in range(4):
                hooks.setdefault(kend + 2 + piece, []).append(lambda hd=hd, piece=piece: recip_fn(hd, piece))
            hooks.setdefault(kend + 8, []).append(lambda hd=hd: post_fn(hd))
            if hd < 4:
                k0 = hd * 2 * NU
                hooks.setdefault(k0 + 3, []).append(lambda hd=hd: ret_sc(hd))
                hooks.setdefault(k0 + 7, []).append(lambda hd=hd: ret_out(hd))
                hooks.setdefault(k0 + 2 * NU + 1, []).append(lambda hd=hd: pend[hd]())
        for k in range(3):
            s_op(k)
        for k in range(nstream_):
            if k + 3 < nstream_:
                s_op(k + 3)
            pv_op(k)
            for f_ in hooks.pop(k, []):
                f_()
        for k in sorted(hooks):
            for f_ in hooks[k]:
                f_()

    def outproj_stage(par):
        x = xt[par]
        vk = [("V", i) for i in range(8)]
        for cg in range(2):
            s0 = next_unit(UIDX[("OUT", cg, 0)])
            s1 = next_unit(UIDX[("OUT", cg, 1)])
            for tc in range(4):
                def fmm(e, tc=tc, s0=s0, s1=s1):
                    ins = None
                    for kc in range(8):
                        s = s0 if kc < 4 else s1
                        ins = e.matmul(B[tc][:, :], lhsT=Vb[:, kc * 512 + tc * 128: kc * 512 + (tc + 1) * 128],
                                       rhs=wslice(s, (kc % 4) * 512, 512), start=(kc == 0), stop=(kc == 7))
                    return ins
                sc.add("pe", fmm, reads=vk + [("w", s0), ("w", s1)], writes=[("B", tc)])
            for tc in range(4):
                xs = x[:, tc * 1024 + cg * 512: tc * 1024 + (cg + 1) * 512]
                sc.add("dve", lambda e, tc=tc, xs=xs: e.tensor_tensor(out=xs, in0=B[tc][:, :], in1=xs, op=ALU.add),
                       reads=[("B", tc), ("xt", par, tc)], writes=[("xt", par, tc)])

    def load_x(t):
        par = t % 2
        src = xin[t * T:(t + 1) * T, :].rearrange("(tc p) d -> p tc d", p=128)
        dst = xt[par][:, :].rearrange("p (tc d) -> p tc d", tc=4)
        sc.add("sp", lambda e: e.dma_start(out=dst, in_=src), writes=xkeys(par), dma=("xt", par))

    def load_tab(t):
        srcc = cosd[t * T:(t + 1) * T, :].rearrange("(tc p) d -> p tc d", p=128)
        srcs = ssd[t * T:(t + 1) * T, :].rearrange("(tc p) d -> p tc d", p=128)
        sc.add("sp", lambda e: e.dma_start(out=tab[:, 0:512].rearrange("p (tc d) -> p tc d", tc=4), in_=srcc),
               writes=["tabc"], dma="tabc")
        sc.add("sp", lambda e: e.dma_start(out=tab[:, 512:1024].rearrange("p (tc d) -> p tc d", tc=4), in_=srcs),
               writes=["tabs"], dma="tabs")

    load_x(0)
    if ntile > 1:
        load_x(1)
    nxt = {}
    defer_x = []
    store_q = []

    def hoist_chain(tn):
        nxt["gs"] = load_gain(0)
        norm_stage(tn % 2, nxt["gs"], chain_only=True)

    hoist_chain(0)
    norm_transposes()
    for t in range(ntile):
        par = t % 2
        kind = tile_kind(t)
        load_tab(t)
        gs2 = load_gain(1)
        ffn_stage(par, 0, early_hook=(lambda: store_q.pop(0)()) if store_q else None)
        if t == 0:
            dump("h1", xt[par][:, 0:1024], xkeys(par))
        norm_stage(par, gs2)
        last = (t + 1 == ntile)
        if kind != "own":
            inproj_stage(t, kind, hook_after_rv=(None if last else (lambda: hoist_chain(t + 1))))
            if not last:
                if t + 2 < ntile:
                    load_x(t + 2)
                norm_transposes()
            continue
        inproj_stage(t, kind)
        if defer_x:
            load_x(defer_x.pop(0))
        gs3 = load_gain(2)
        pend = retention_stage(t)
        attention_stage(t, pend)
        outproj_stage(par)
        norm_stage(par, gs3)
        gs4 = load_gain(3)
        ffn_stage(par, 1, gu_hook=(None if last else (lambda: hoist_chain(t + 1))),
                  mid_hook=(None if last else norm_transposes))
        norm_stage(par, gs4, final=True)
        ot = t - nh
        dst = yout[ot * T:(ot + 1) * T, :].rearrange("(tc p) d -> p tc d", p=128)
        src = xt[par][:, :].rearrange("p (tc d) -> p tc d", tc=4)
        store_q.append(lambda dst=dst, src=src, par=par: sc.add(
            "sp", lambda e: e.dma_start(out=dst, in_=src), reads=xkeys(par), dma=("xt_out", par)))
        if last:
            store_q.pop(0)()
        if t + 2 < ntile:
            defer_x.append(t + 2)
    sc.add("sp", lambda e: e.nop(), writes=xkeys(0) + xkeys(1))

    with nc.allow_low_precision("bf16 matmul operands by design; fp32 accumulation"):
        with nc.Block() as block:
            sc.emit(nc, block)
    return nc


_CACHE = {}


def _prep_core(x, b, s, half, common):
    own = x[b, s * half:(s + 1) * half]
    if s == 0:
        halo = np.zeros_like(own)
        flag = 0.0
    else:
        halo = x[b, (s - 1) * half:s * half]
        flag = 1.0
    xin = np.ascontiguousarray(np.concatenate([halo, own], axis=0))
    pos = np.arange((s - 1) * half, (s + 1) * half)
    cos, ssn = _rope_tables(pos)
    m = dict(common)
    m.update(xin=xin, cosd=cos, ssd=ssn, flags=np.full((128, 64), flag, np.float32))
    return m


def run(x, params, nh, no, dbg=()):
    key = (nh, no, tuple(dbg))
    if key not in _CACHE:
        _CACHE[key] = build(nh, no, dbg)
    nc = _CACHE[key]
    cst = _consts()
    f32 = lambda a: np.ascontiguousarray(np.asarray(a, np.float32))
    gains = np.stack([np.broadcast_to(f32(params[k]).reshape(1, D), (128, D))
                      for k in ("norm_ffn1", "norm_mix", "norm_ffn2", "norm_final")]).copy()
    common = dict(
        w_g1=f32(params["ffn1_w_gate"][0]), w_u1=f32(params["ffn1_w_up"][0]), w_d1=f32(params["ffn1_w_down"][0]),
        w_g2=f32(params["ffn2_w_gate"][0]), w_u2=f32(params["ffn2_w_up"][0]), w_d2=f32(params["ffn2_w_down"][0]),
        w_in=f32(params["w_in"][0]), w_out=f32(params["w_out"][0]), gains=gains,
        rgain=np.broadcast_to(f32(params["ret_norm_gain"]).reshape(1, 512), (128, 512)).copy(),
        small=cst["small"], g128=cst["g128"], caus=cst["caus"], ident=cst["ident"], shif=cst["shif"],
        masks=cst["masks"], fmasks=cst["fmasks"],
    )
    half = no * T
    nb = x.shape[0]
    in_maps = []
    for c in range(8):
        b, s = (c // 2) % nb, c % 2
        in_maps.append(_prep_core(x, b, s, half, common))
    res = run_bass_kernel_spmd(nc, in_maps, core_ids=list(range(8)))
    return res


def kernel(x, norm_ffn1, ffn1_w_gate, ffn1_w_up, ffn1_w_down, norm_mix, w_in, ret_norm_gain, w_out,
           norm_ffn2, ffn2_w_gate, ffn2_w_up, ffn2_w_down, norm_final):
    x = np.asarray(x, np.float32)
    params = dict(norm_ffn1=norm_ffn1, ffn1_w_gate=ffn1_w_gate, ffn1_w_up=ffn1_w_up, ffn1_w_down=ffn1_w_down,
                  norm_mix=norm_mix, w_in=w_in, ret_norm_gain=ret_norm_gain, w_out=w_out, norm_ffn2=norm_ffn2,
                  ffn2_w_gate=ffn2_w_gate, ffn2_w_up=ffn2_w_up, ffn2_w_down=ffn2_w_down, norm_final=norm_final)
    params = {k: np.asarray(v) for k, v in params.items()}
    res = run(x, params, 8, 8)
    out = np.empty((4, 8192, D), np.float32)
    for c in range(8):
        b, s = c // 2, c % 2
        out[b, s * 4096:(s + 1) * 4096] = np.asarray(res.results[c]["yout"], np.float32)
    return out
```

```python
import numpy as np
import ml_dtypes
import concourse.bass as bass
import concourse.mybir as mybir
from concourse.bass_utils import run_bass_kernel_spmd

F32 = mybir.dt.float32
BF16 = mybir.dt.bfloat16
ALU = mybir.AluOpType
AF = mybir.ActivationFunctionType
AX = mybir.AxisListType

D = 1024
DFF = 2816
NMC = DFF // 128
T = 512
NSLOT = 5
NCV = 12
NRING = 20
VW = 768
NMASK = 8
EPS = 1e-6
INCOLS = 3584


class _Op:
    __slots__ = ("eng", "fn", "deps", "inc", "dma", "val")


class Sched:
    ENGS = ("pe", "act", "dve", "pool", "sp")

    def __init__(self):
        self.q = {e: [] for e in self.ENGS}
        self.lastw = {}
        self.readers = {}
        self.dma_n = {}

    def add(self, eng, fn, reads=(), writes=(), dma=None):
        op = _Op()
        op.eng = eng
        op.fn = fn
        op.dma = dma
        op.inc = False
        op.val = 0
        deps = {}
        for k in reads:
            w = self.lastw.get(k)
            if w is not None:
                deps[id(w)] = w
        for k in writes:
            w = self.lastw.get(k)
            if w is not None and (w.dma is not None or dma is not None or w.eng != eng or eng != "pe"):
                deps[id(w)] = w
            for r in self.readers.get(k, ()):
                if r.dma is not None or dma is not None or r.eng != eng:
                    deps[id(r)] = r
        op.deps = list(deps.values())
        for o in op.deps:
            o.inc = True
        for k in reads:
            self.readers.setdefault(k, []).append(op)
        for k in writes:
            self.lastw[k] = op
            self.readers[k] = []
        if dma is not None:
            n = self.dma_n.get(dma, 0) + 1
            self.dma_n[dma] = n
            op.val = 16 * n
            op.inc = True
        self.q[eng].append(op)
        return op

    def emit(self, nc, block):
        esem = {e: nc.alloc_semaphore("sem_" + e) for e in self.ENGS}
        dsem = {k: nc.alloc_semaphore("dsem_%d" % i) for i, k in enumerate(self.dma_n)}
        for e in self.ENGS:
            c = 0
            for op in self.q[e]:
                if op.dma is None and op.inc:
                    c += 1
                    op.val = c

        def semof(op):
            return dsem[op.dma] if op.dma is not None else esem[op.eng]

        def run(e, eng):
            seen = {}
            for op in self.q[e]:
                need = {}
                for d in op.deps:
                    s = semof(d)
                    if need.get(s.num, (None, 0))[1] < d.val:
                        need[s.num] = (s, d.val)
                for num, (s, v) in need.items():
                    if seen.get(num, 0) < v:
                        eng.wait_ge(s, v)
                        seen[num] = v
                ins = op.fn(eng)
                if op.inc:
                    ins.then_inc(semof(op), 16 if op.dma is not None else 1)

        @block.tensor
        def _(eng):
            run("pe", eng)

        @block.scalar
        def _(eng):
            run("act", eng)

        @block.vector
        def _(eng):
            run("dve", eng)

        @block.gpsimd
        def _(eng):
            run("pool", eng)

        @block.sync
        def _(eng):
            run("sp", eng)


def _gammas():
    return 1.0 - 2.0 ** (-5.0 - np.arange(4, dtype=np.float64))


def _mask_tables():
    near = np.zeros((8, 128, 512), np.float32)
    i = np.arange(128)[:, None]
    j = np.arange(512)[None, :]
    for d in (-1, 0):
        for g in range(4):
            dist = j - (512 * d + g + 4 * i)
            m = ((dist >= 0) & (dist <= 128)).astype(np.float32)
            m += ((dist >= 0) & (dist % 4 == 0) & (dist <= 512)).astype(np.float32)
            m += ((dist >= 0) & (dist % 16 == 0) & (dist <= 2048)).astype(np.float32)
            near[(d + 1) * 4 + g] = m
    far = np.zeros((2, 128, 128), np.float32)
    ip = np.arange(128)[None, :]
    for k, d in enumerate((-4, -3)):
        dist = 4 * (ip - i) - 512 * d
        far[k] = ((dist % 16 == 0) & (dist <= 2048) & (dist >= 0)).astype(np.float32)
    dist2 = 4 * (ip - i) + 1024
    assert np.array_equal(((dist2 % 16 == 0) & (dist2 <= 2048)).astype(np.float32), far[1])
    assert dist2.min() > 512
    return near, far


def _consts():
    g = _gammas()
    c = np.arange(128, dtype=np.float64)
    decq = np.stack([g[h] ** (c + 1) for h in range(4)], axis=1)
    deck = np.stack([(128.0 ** -0.5) * g[h] ** (-(c + 1)) for h in range(4)], axis=1)
    g128 = np.repeat(g ** 128, 128)[None, :].repeat(128, axis=0)
    caus = (np.arange(128)[None, :] >= np.arange(128)[:, None]).astype(np.float32)
    ident = np.eye(128, dtype=np.float32)
    shif = np.zeros((128, 128), np.float32)
    for k in range(64):
        shif[k, 64 + k] = 1.0
    small = np.zeros((128, 8), np.float32)
    small[:, 0:4] = decq
    small[:, 4:8] = deck
    near, far = _mask_tables()
    return dict(small=small, g128=g128.astype(np.float32), caus=caus, ident=ident, shif=shif,
                masks=near, fmasks=far)


def _rope_tables(pos):
    pos = pos.astype(np.float32)
    inv = (10000.0 ** (-np.arange(0, 128, 2, dtype=np.float32) / 128.0)).astype(np.float32)
    ang = (pos[:, None] * inv[None, :]).astype(np.float32)
    cos = np.repeat(np.cos(ang).astype(np.float32), 2, axis=1)
    sin = np.repeat(np.sin(ang).astype(np.float32), 2, axis=1)
    sgn = np.tile(np.array([-1.0, 1.0], np.float32), 64)[None, :]
    return np.ascontiguousarray(cos), np.ascontiguousarray(sin * sgn)


def _units():
    units = []
    for f in (0, 1):
        for mc in range(NMC):
            units.append(("GU", f, mc))
        for cg in range(2):
            for q in range(6):
                units.append(("DN", f, cg, q))
    for g in range(5):
        for half in range(2):
            units.append(("TM", g, half))
    for i in range(4):
        units.append(("FM", i))
    for cg in range(2):
        for half in range(2):
            units.append(("OUT", cg, half))
    return units


UNITS = _units()
UIDX = {u: i for i, u in enumerate(UNITS)}
TM_COL = {0: 0, 1: 512, 2: 1024, 3: 1536, 4: 3072}


def build(nh=8, no=8, dbg=()):
    nc = bass.Bass("TRN2", target_bir_lowering=False)
    sc = Sched()
    ntile = nh + no
    ntok = ntile * T

    def din(name, shape, dt=F32):
        return nc.dram_tensor(name, list(shape), dt, kind="ExternalInput")

    xin = din("xin", [ntok, D])
    cosd = din("cosd", [ntok, 128])
    ssd = din("ssd", [ntok, 128])
    wd = {
        ("g", 0): din("w_g1", [D, DFF]), ("u", 0): din("w_u1", [D, DFF]), ("d", 0): din("w_d1", [DFF, D]),
        ("g", 1): din("w_g2", [D, DFF]), ("u", 1): din("w_u2", [D, DFF]), ("d", 1): din("w_d2", [DFF, D]),
    }
    w_in = din("w_in", [D, INCOLS])
    w_out = din("w_out", [D, D])
    gains_d = din("gains", [4, 128, D])
    rgain_d = din("rgain", [128, 512])
    small_d = din("small", [128, 8])
    g128_d = din("g128", [128, 512])
    caus_d = din("caus", [128, 128])
    ident_d = din("ident", [128, 128])
    shif_d = din("shif", [128, 128])
    masks_d = din("masks", [NMASK, 128, 512])
    fmasks_d = din("fmasks", [2, 128, 128])
    flags_d = din("flags", [128, 64])
    yout = nc.dram_tensor("yout", [no * T, D], F32, kind="ExternalOutput")
    wsc = nc.dram_tensor("wsc", [len(UNITS), 128, 2048], BF16)
    dbg_out = {}
    for name, shape in dbg:
        dbg_out[name] = nc.dram_tensor("dbg_" + name, list(shape), F32, kind="ExternalOutput")

    def sb(name, cols, dt):
        return nc.alloc_sbuf_tensor("s_" + name, [128, cols], dt)

    xt = [sb("xt0", 4096, F32), sb("xt1", 4096, F32)]
    Vb = sb("Vb", 4096, BF16)
    xnT = sb("xnT", 4096, BF16)
    U = sb("U", 24 * 512, BF16)
    sil = [sb("sil0", 512, F32), sb("sil1", 512, F32)]
    sil_bf = [t_.bitcast(BF16) for t_ in sil]
    rt = [sb("rt0", 512, F32), sb("rt1", 512, F32)]
    aqT = sb("aqT", 8 * 512, BF16)
    kring = sb("kring", 4 * NRING * 128, BF16)
    vring = sb("vring", NRING * VW, BF16)
    masks = sb("masks", NMASK * 512, BF16)
    fmasks = sb("fmasks", 2 * 128, BF16)
    eb = [sb("eb%d" % i_, 512, BF16) for i_ in range(4)]
    wr = sb("wr", NSLOT * 2048, BF16)
    tab = sb("tab", 2 * 512, F32)
    gn = sb("gn", 2 * 1024, F32)
    rgain = sb("rgain_sb", 512, F32)
    Sf = sb("Sf", 512, F32)
    Sb = sb("Sb", 6 * 512, BF16)
    retf = sb("retf", 512, F32)
    sqf = sb("sqf", 512, F32)
    rsb = sb("rsb2", 512, F32)
    lnd = sb("lnd", 512, F32)
    scb = [sb("scb0", 512, BF16), sb("scb1", 512, BF16)]
    Rb = sb("Rb", 512, BF16)
    ident = sb("ident_sb", 128, BF16)
    shif = sb("shif_sb", 128, BF16)
    caus = sb("caus_sb", 128, F32)
    small = sb("small_sb", 8, F32)
    g128 = sb("g128_sb", 512, F32)
    flagsb = sb("flags_sb", 64, BF16)
    onesb = sb("ones_sb", 64, BF16)
    st = sb("st", 32, F32)
    ss = sb("ss", 8, F32)

    B = [nc.alloc_psum_tensor("B%d" % i, [128, 512], F32) for i in range(4)]
    R = [nc.alloc_psum_tensor("R%d" % i, [128, 512], F32) for i in range(2)]
    TB = [nc.alloc_psum_tensor("TB%d" % i, [128, 1024], BF16) for i in range(2)]

    def pstep(t):
        return t[:, :].ap[0][0]

    def ap(t, off, dims, p0=0, npart=128):
        ps = pstep(t)
        return bass.AP(t, p0 * ps + off, [[ps, npart]] + [list(d) for d in dims])

    cload_keys = []

    def cload(dst_ap, src_ap, key, eng="sp"):
        sc.add(eng, lambda e, o=dst_ap, i=src_ap: e.dma_start(out=o, in_=i), writes=[key], dma=("c", key))
        cload_keys.append(key)

    cload(small[:, :], small_d[:, :], "small")
    cload(g128[:, :], g128_d[:, :], "g128")
    cload(caus[:, :], caus_d[:, :], "caus")
    cload(rgain[:, :], rgain_d[:, :], "rgain")
    cload(ident[:, :], ident_d[:, :], "ident", eng="pool")
    cload(shif[:, :], shif_d[:, :], "shif", eng="pool")
    cload(flagsb[:, :], flags_d[:, :], "flags", eng="pool")
    cload(masks[:, :].rearrange("p (r q) -> p r q", r=NMASK), masks_d[:, :, :].rearrange("r p q -> p r q"),
          "masks", eng="pool")
    cload(fmasks[:, :].rearrange("p (r q) -> p r q", r=2), fmasks_d[:, :, :].rearrange("r p q -> p r q"),
          "fmasks", eng="pool")
    sc.add("dve", lambda e: e.memset(onesb[:, :], 1.0), writes=["ones"])
    sc.add("dve", lambda e: e.memset(vring[:, :], 0.0), writes=["vring_init"])
    sc.add("dve", lambda e: e.memset(kring[:, :], 0.0), writes=["kring_init"])
    sc.add("dve", lambda e: e.memset(Sf[:, :], 0.0), writes=["Sf"])
    sc.add("dve", lambda e: e.memset(Sb[:, :], 0.0), writes=["Sb_init"])
    sc.add("dve", lambda e: e.memset(st[:, 0:28], 0.0), writes=["st_init"])
    allc = cload_keys + ["ones", "vring_init", "kring_init", "Sb_init", "st_init"]
    for e in ("pe", "act", "dve", "pool"):
        if e == "pe":
            sc.add(e, lambda en: en.transpose(out=TB[1][:, 0:128], in_=ident[:, :], identity=ident[:, :]),
                   reads=allc, writes=[("T", 1)])
        elif e == "act":
            sc.add(e, lambda en: en.copy(out=st[:, 28:30], in_=small[:, 0:2]), reads=allc, writes=["bar_act"])
        elif e == "dve":
            sc.add(e, lambda en: en.memset(st[:, 30:32], 0.0), reads=allc, writes=["bar_dve"])
        else:
            sc.add(e, lambda en: en.memset(aqT[:, :], 0.0), reads=allc, writes=[("aqT", i) for i in range(8)])

    cv_state = {"next": 0, "n": 0}

    def wview(w):
        return w[:, :].rearrange("(kc p) n -> p kc n", p=128)

    def conv_dma(dst, src, key):
        k = cv_state["n"]
        cv_state["n"] += 1
        sc.add("pool", lambda e, o=dst, i=src: e.dma_start(out=o, in_=i),
               writes=[key, ("cvslot", k % NCV)], dma=("cv", k % NCV))

    def conv_unit(ui):
        u = UNITS[ui]
        dst = wsc[ui, :, :]
        if u[0] == "GU":
            _, f, mc = u
            d4 = dst.rearrange("p (t kc j) -> p t kc j", t=2, kc=8)
            conv_dma(d4[:, 0, :, :], wview(wd[("g", f)])[:, :, mc * 128:(mc + 1) * 128], ("wsc", ui, 0))
            conv_dma(d4[:, 1, :, :], wview(wd[("u", f)])[:, :, mc * 128:(mc + 1) * 128], ("wsc", ui, 1))
        elif u[0] == "DN":
            _, f, cg, q = u
            nk = 4 if q < 5 else 2
            d3 = dst.rearrange("p (k n) -> p k n", k=4)
            conv_dma(d3[:, 0:nk, :], wview(wd[("d", f)])[:, 4 * q:4 * q + nk, cg * 512:(cg + 1) * 512],
                     ("wsc", ui, 0))
        elif u[0] == "TM":
            _, g, half = u
            c0 = TM_COL[g]
            d3 = dst.rearrange("p (k n) -> p k n", k=4)
            conv_dma(d3, wview(w_in)[:, 4 * half:4 * half + 4, c0:c0 + 512], ("wsc", ui, 0))
        elif u[0] == "FM":
            _, i = u
            d4 = dst.rearrange("p (t kc j) -> p t kc j", t=2, kc=8)
            for m in range(2):
                c0 = 2048 + (2 * i + m) * 128
                conv_dma(d4[:, m, :, :], wview(w_in)[:, :, c0:c0 + 128], ("wsc", ui, m))
        else:
            _, cg, half = u
            d3 = dst.rearrange("p (k n) -> p k n", k=4)
            conv_dma(d3, wview(w_out)[:, 4 * half:4 * half + 4, cg * 512:(cg + 1) * 512], ("wsc", ui, 0))

    conv_done = set()

    def ensure_conv(ui):
        if ui not in conv_done:
            conv_done.add(ui)
            conv_unit(ui)

    stream = []
    stream_pos = {"issued": 0}

    def unit_list(kind):
        l = [UIDX[("GU", 0, mc)] for mc in range(NMC)]
        l += [UIDX[("DN", 0, cg, q)] for cg in range(2) for q in range(6)]
        if kind == "halo_kv" or kind == "halo_k":
            gs = [1, 2]
        else:
            gs = [1, 2, 0, 3]
        for g in gs:
            l += [UIDX[("TM", g, h)] for h in range(2)]
        if kind == "halo_kv":
            l += [UIDX[("FM", 2)], UIDX[("FM", 3)]]
            l += [UIDX[("TM", 4, h)] for h in range(2)]
        if kind == "own":
            l += [UIDX[("FM", i)] for i in range(4)]
            l += [UIDX[("TM", 4, h)] for h in range(2)]
            l += [UIDX[("OUT", cg, h)] for cg in range(2) for h in range(2)]
            l += [UIDX[("GU", 1, mc)] for mc in range(NMC)]
            l += [UIDX[("DN", 1, cg, q)] for cg in range(2) for q in range(6)]
        return l

    def tile_kind(t):
        if t >= nh:
            return "own"
        return "halo_kv" if t >= nh - 4 else "halo_k"

    for t in range(ntile):
        stream += unit_list(tile_kind(t))
    nstream = len(stream)
    use_ptr = {"i": 0}

    def issue_load(qi):
        ui = stream[qi]
        ensure_conv(ui)
        slot = qi % NSLOT
        nparts = 2 if UNITS[ui][0] in ("GU", "FM") else 1
        ncol = 1024 if (UNITS[ui][0] == "DN" and UNITS[ui][3] == 5) else 2048
        sc.add("sp", lambda e, s=slot, u=ui, ncol=ncol: e.dma_start(out=wr[:, s * 2048:s * 2048 + ncol],
                                                                    in_=wsc[u, :, 0:ncol]),
               reads=[("wsc", ui, p) for p in range(nparts)], writes=[("w", slot)], dma=("w", slot))

    def next_unit(expect):
        qi = use_ptr["i"]
        assert stream[qi] == expect, (UNITS[stream[qi]], UNITS[expect])
        while stream_pos["issued"] < min(nstream, qi + NSLOT - 1):
            issue_load(stream_pos["issued"])
            stream_pos["issued"] += 1
        for k in range(qi, min(nstream, qi + 14)):
            ensure_conv(stream[k])
        use_ptr["i"] += 1
        return qi % NSLOT

    def wslice(slot, off, n):
        return wr[:, slot * 2048 + off: slot * 2048 + off + n]

    def dump(name, src_ap, keys):
        if name in dbg_out:
            d = dbg_out[name]
            sc.add("sp", lambda e, o=d[:, :], i=src_ap: e.dma_start(out=o, in_=i), reads=keys,
                   dma=("dbg", name))

    gn_state = {"n": 0}

    def load_gain(gi):
        s = gn_state["n"] % 2
        gn_state["n"] += 1
        sc.add("sp", lambda e, s=s, gi=gi: e.dma_start(out=gn[:, s * 1024:(s + 1) * 1024], in_=gains_d[gi, :, :]),
               writes=[("gn", s)], dma=("gn", s))
        return s

    def xkeys(par):
        return [("xt", par, tc) for tc in range(4)]

    def norm_stage(par, gslot, final=False, chain_only=False):
        x = xt[par]
        for tc in range(4):
            sc.add("act", lambda e, tc=tc: e.activation(
                out=Vb[:, tc * 1024:(tc + 1) * 1024], in_=x[:, tc * 1024:(tc + 1) * 1024], func=AF.Square,
                accum_out=ss[:, tc:tc + 1]),
                reads=[("xt", par, tc)], writes=[("V", 2 * tc), ("V", 2 * tc + 1), ("ss", tc)])
        sc.add("dve", lambda e: e.tensor_scalar(out=ss[:, 4:8], in0=ss[:, 0:4], scalar1=1.0 / D, scalar2=EPS,
                                                 op0=ALU.mult, op1=ALU.add),
               reads=[("ss", tc) for tc in range(4)], writes=["rs"])
        sc.add("act", lambda e: e.activation(out=ss[:, 4:8], in_=ss[:, 4:8], func=AF.Sqrt),
               reads=["rs"], writes=["rs"])
        sc.add("dve", lambda e: e.reciprocal(out=ss[:, 4:8], in_=ss[:, 4:8]),
               reads=["rs"], writes=["rs"])
        for tc in range(4):
            if final:
                o = x[:, tc * 1024:(tc + 1) * 1024]
                wk = [("xt", par, tc)]
            else:
                o = Vb[:, tc * 1024:(tc + 1) * 1024]
                wk = [("V", 2 * tc), ("V", 2 * tc + 1)]
            sc.add("dve", lambda e, tc=tc, o=o: e.scalar_tensor_tensor(
                out=o, in0=x[:, tc * 1024:(tc + 1) * 1024], scalar=ss[:, 4 + tc:5 + tc],
                in1=gn[:, gslot * 1024:(gslot + 1) * 1024], op0=ALU.mult, op1=ALU.mult),
                reads=[("xt", par, tc), "rs", ("gn", gslot)], writes=wk)
        if final or chain_only:
            return
        norm_transposes()

    def norm_transposes():
        for kp in range(4):
            tb = TB[kp % 2]

            def f(e, kp=kp, tb=tb):
                ins = None
                for k2 in range(2):
                    kc = 2 * kp + k2
                    for tc in range(4):
                        ins = e.transpose(out=tb[:, k2 * 512 + tc * 128: k2 * 512 + (tc + 1) * 128],
                                          in_=Vb[:, tc * 1024 + kc * 128: tc * 1024 + (kc + 1) * 128],
                                          identity=ident[:, :])
                return ins
            sc.add("pe", f, reads=[("V", i) for i in range(8)], writes=[("T", kp % 2)])
            eng = "act" if kp % 2 == 0 else "dve"
            if eng == "act":
                fn = lambda e, kp=kp, tb=tb: e.copy(out=xnT[:, kp * 1024:(kp + 1) * 1024], in_=tb[:, :])
            else:
                fn = lambda e, kp=kp, tb=tb: e.tensor_copy(out=xnT[:, kp * 1024:(kp + 1) * 1024], in_=tb[:, :])
            sc.add(eng, fn, reads=[("T", kp % 2)], writes=[("xnT", 2 * kp), ("xnT", 2 * kp + 1)])

    XNT = [("xnT", k) for k in range(8)]

    DNBANK = [[(("B", i), B[i]) for i in range(4)],
              [(("R", 0), R[0]), (("R", 1), R[1]), (("B", 0), B[0]), (("B", 1), B[1])]]

    def ffn_stage(par, f, gu_hook=None, mid_hook=None, early_hook=None):
        x = xt[par]
        for mc in range(NMC):
            if mc == 8 and early_hook is not None:
                early_hook()
            slot = next_unit(UIDX[("GU", f, mc)])
            bg, bu = B[2 * (mc % 2)], B[2 * (mc % 2) + 1]
            for which, bank, bk in ((0, bg, 2 * (mc % 2)), (1, bu, 2 * (mc % 2) + 1)):
                def fmm(e, slot=slot, which=which, bank=bank):
                    ins = None
                    for kc in range(8):
                        ins = e.matmul(bank[:, :], lhsT=wslice(slot, (which * 8 + kc) * 128, 128),
                                       rhs=xnT[:, kc * 512:(kc + 1) * 512], start=(kc == 0), stop=(kc == 7))
                    return ins
                sc.add("pe", fmm, reads=XNT + [("w", slot)], writes=[("B", bk)])
            sl = sil[mc % 2]
            sc.add("act", lambda e, bg=bg, sl=sl: e.activation(out=sl[:, :], in_=bg[:, :], func=AF.Silu),
                   reads=[("B", 2 * (mc % 2))], writes=[("sil", mc % 2)])
            sc.add("dve", lambda e, bu=bu, sl=sl, mc=mc: e.tensor_tensor(
                out=U[:, mc * 512:(mc + 1) * 512], in0=bu[:, :], in1=sl[:, :], op=ALU.mult),
                reads=[("B", 2 * (mc % 2) + 1), ("sil", mc % 2)], writes=[("U", mc)])
        if gu_hook is not None:
            gu_hook()
        for cg in range(2):
            if cg == 1 and mid_hook is not None:
                mid_hook()
            for q in range(6):
                slot = next_unit(UIDX[("DN", f, cg, q)])
                nk = 4 if q < 5 else 2
                for tc in range(4):
                    bkey, bank = DNBANK[cg][tc]

                    def fmm(e, slot=slot, q=q, nk=nk, tc=tc, bank=bank):
                        ins = None
                        for i in range(nk):
                            kc = 4 * q + i
                            ins = e.matmul(bank[:, :], lhsT=U[:, kc * 512 + tc * 128: kc * 512 + (tc + 1) * 128],
                                           rhs=wslice(slot, i * 512, 512), start=(kc == 0), stop=(kc == NMC - 1))
                        return ins
                    sc.add("pe", fmm, reads=[("U", 4 * q + i) for i in range(nk)] + [("w", slot)],
                           writes=[bkey])
            for tc in range(4):
                bkey, bank = DNBANK[cg][tc]
                xs = x[:, tc * 1024 + cg * 512: tc * 1024 + (cg + 1) * 512]
                sc.add("dve", lambda e, tc=tc, xs=xs, bank=bank: e.scalar_tensor_tensor(
                    out=xs, in0=bank[:, :], scalar=0.5, in1=xs, op0=ALU.mult, op1=ALU.add),
                    reads=[bkey, ("xt", par, tc)], writes=[("xt", par, tc)])

    bank_rr = {"i": 0}

    def nextbank():
        i = bank_rr["i"] % 4
        bank_rr["i"] += 1
        return i

    def tm_group(g, strided=False):
        s0 = next_unit(UIDX[("TM", g, 0)])
        s1 = next_unit(UIDX[("TM", g, 1)])
        res = []
        for tc in range(4):
            bi = nextbank()

            def fmm(e, tc=tc, bi=bi, s0=s0, s1=s1):
                ins = None
                for kc in range(8):
                    s = s0 if kc < 4 else s1
                    lt = (ap(xnT, kc * 512 + tc, [[4, 128]]) if strided
                          else xnT[:, kc * 512 + tc * 128: kc * 512 + (tc + 1) * 128])
                    ins = e.matmul(B[bi][:, :], lhsT=lt,
                                   rhs=wslice(s, (kc % 4) * 512, 512), start=(kc == 0), stop=(kc == 7))
                return ins
            sc.add("pe", fmm, reads=XNT + [("w", s0), ("w", s1)], writes=[("B", bi)])
            res.append((tc, bi))
            yield tc, bi

    def rotary(tc, bi, is_q):
        bank = B[bi]
        cosb = ap(tab, tc * 128, [[0, 4], [1, 128]])
        sse = ap(tab, 512 + tc * 128, [[0, 4], [2, 64]])
        sso = ap(tab, 512 + tc * 128 + 1, [[0, 4], [2, 64]])
        be = ap(bank, 0, [[128, 4], [2, 64]])
        bo = ap(bank, 1, [[128, 4], [2, 64]])
        r1e = ap(rt[1], 0, [[128, 4], [2, 64]])
        r1o = ap(rt[1], 1, [[128, 4], [2, 64]])
        sc.add("dve", lambda e: e.tensor_tensor(out=rt[0][:, :], in0=bank[:, :], in1=cosb, op=ALU.mult),
               reads=[("B", bi), "tabc"], writes=["rt0"])
        sc.add("dve", lambda e: e.tensor_tensor(out=r1e, in0=bo, in1=sse, op=ALU.mult),
               reads=[("B", bi), "tabs"], writes=["rt1e"])
        sc.add("dve", lambda e: e.tensor_tensor(out=r1o, in0=be, in1=sso, op=ALU.mult),
               reads=[("B", bi), "tabs"], writes=["rt1o"])
        sc.add("pool", lambda e: e.tensor_tensor(out=rt[0][:, :], in0=rt[0][:, :], in1=rt[1][:, :], op=ALU.add),
               reads=["rt0", "rt1e", "rt1o"], writes=["rt0"])
        page = tc if is_q else 4 + tc
        dec = ap(small, 0 if is_q else 4, [[1, 4], [0, 128]])
        sc.add("pool", lambda e: e.tensor_tensor(out=U[:, page * 512:(page + 1) * 512], in0=rt[0][:, :], in1=dec,
                                                  op=ALU.mult),
               reads=["rt0"], writes=[("U", page)])
        return page

    tb_rr = {"i": 0}

    def transpose4(src_page_ap_fn, src_keys, dst_ap, dst_keys, evac_eng):
        ti = tb_rr["i"] % 2
        tb_rr["i"] += 1
        tb = TB[ti]

        def f(e):
            ins = None
            for h in range(4):
                ins = e.transpose(out=tb[:, h * 128:(h + 1) * 128], in_=src_page_ap_fn(h), identity=ident[:, :])
            return ins
        sc.add("pe", f, reads=src_keys, writes=[("T", ti)])
        src = ap(tb, 0, [[128, 4], [1, 128]])
        if evac_eng == "act":
            sc.add("act", lambda e: e.copy(out=dst_ap, in_=src), reads=[("T", ti)], writes=dst_keys)
        else:
            sc.add(evac_eng, lambda e: e.tensor_copy(out=dst_ap, in_=src), reads=[("T", ti)], writes=dst_keys)

    def inproj_stage(t, kind, hook_after_rv=None):
        tau = t
        n0 = 4 * tau
        deferred = []
        for tc, bi in tm_group(1):
            pg = rotary(tc, bi, False)
            if kind == "own":
                deferred.append(lambda tc=tc, pg=pg: transpose4(
                    lambda h, pg=pg: U[:, pg * 512 + h * 128: pg * 512 + (h + 1) * 128], [("U", pg)],
                    ap(U, 20 * 512 + tc * 128, [[512, 4], [1, 128]]), [("U", 20 + h) for h in range(4)], "act"))
        for tc, bi in tm_group(2):
            sc.add("act", lambda e, tc=tc, bi=bi: e.copy(out=U[:, (8 + tc) * 512:(9 + tc) * 512], in_=B[bi][:, :]),
                   reads=[("B", bi)], writes=[("U", 8 + tc)])
        for f_ in deferred:
            f_()
        deferred = []
        if hook_after_rv is not None:
            hook_after_rv()
        for tc in range(4):
            ri = tc % 2

            def fkv(e, tc=tc, ri=ri):
                ins = None
                for h in range(4):
                    ins = e.matmul(R[ri][:, h * 128:(h + 1) * 128],
                                   lhsT=U[:, (4 + tc) * 512 + h * 128:(4 + tc) * 512 + (h + 1) * 128],
                                   rhs=U[:, (8 + tc) * 512 + h * 128:(8 + tc) * 512 + (h + 1) * 128],
                                   start=True, stop=True)
                return ins
            sc.add("pe", fkv, reads=[("U", 4 + tc), ("U", 8 + tc)], writes=[("R", ri)])
            sc.add("dve", lambda e, ri=ri: e.tensor_tensor(out=Sf[:, :], in0=R[ri][:, :], in1=Sf[:, :], op=ALU.add),
                   reads=[("R", ri), "Sf"], writes=["Sf"])
            sc.add("dve", lambda e: e.tensor_tensor(out=Sf[:, :], in0=Sf[:, :], in1=g128[:, :], op=ALU.mult),
                   reads=["Sf"], writes=["Sf"])
            sl = (n0 + tc + 1) % 6
            sc.add("act", lambda e, sl=sl: e.copy(out=Sb[:, sl * 512:(sl + 1) * 512], in_=Sf[:, :]),
                   reads=["Sf"], writes=[("Sb", sl)])
        if kind == "halo_k":
            return
        if kind == "own":
            for tc, bi in tm_group(0):
                pg = rotary(tc, bi, True)
                deferred.append(lambda tc=tc, pg=pg: transpose4(
                    lambda h, pg=pg: U[:, pg * 512 + h * 128: pg * 512 + (h + 1) * 128], [("U", pg)],
                    ap(U, 16 * 512 + tc * 128, [[512, 4], [1, 128]]), [("U", 16 + h) for h in range(4)], "act"))
            for tc, bi in tm_group(3):
                sc.add("act", lambda e, tc=tc, bi=bi: e.activation(out=U[:, (12 + tc) * 512:(13 + tc) * 512],
                                                                    in_=B[bi][:, :], func=AF.Silu),
                       reads=[("B", bi)], writes=[("U", 12 + tc)])
                sc.add("pool", lambda e, tc=tc: e.tensor_tensor(
                    out=U[:, (12 + tc) * 512:(13 + tc) * 512], in0=U[:, (12 + tc) * 512:(13 + tc) * 512],
                    in1=rgain[:, :], op=ALU.mult), reads=[("U", 12 + tc)], writes=[("U", 12 + tc)])
            for f_ in deferred:
                f_()
            deferred = []
        s0 = (4 * tau) % NRING
        fm_list = [0, 1, 2, 3] if kind == "own" else [2, 3]
        for i in fm_list:
            slot = next_unit(UIDX[("FM", i)])
            for m in range(2):
                mc = 2 * i + m
                bi = nextbank()

                def fmm(e, slot=slot, m=m, bi=bi):
                    ins = None
                    for kc in range(8):
                        ins = e.matmul(B[bi][:, :], lhsT=wslice(slot, (m * 8 + kc) * 128, 128),
                                       rhs=xnT[:, kc * 512:(kc + 1) * 512], start=(kc == 0), stop=(kc == 7))
                    return ins
                sc.add("pe", fmm, reads=XNT + [("w", slot)], writes=[("B", bi)])
                if mc < 4:
                    for a_ in range(2):
                        hd_ = 2 * mc + a_
                        sc.add("act", lambda e, hd_=hd_, a_=a_, bi=bi: e.activation(
                            out=aqT[64 * a_:64 * a_ + 64, hd_ * 512:(hd_ + 1) * 512],
                            in_=B[bi][64 * a_:64 * a_ + 64, :], func=AF.Copy, scale=0.125),
                            reads=[("B", bi)], writes=[("aqT", hd_)])
                else:
                    j = mc - 4
                    off = j * NRING * 128 + s0 * 128
                    sc.add("dve", lambda e, off=off, bi=bi: e.tensor_copy(out=kring[:, off:off + 512], in_=B[bi][:, :]),
                           reads=[("B", bi)], writes=[("kr", j, s0 + c) for c in range(4)])
        for tc, bi in tm_group(4, strided=True):
            slot_v = s0 + tc
            dst = ap(vring, slot_v * VW, [[192, 4], [128, 2], [1, 64]])
            src = ap(B[bi], 0, [[128, 4], [64, 2], [1, 64]])
            sc.add("act", lambda e, dst=dst, src=src: e.copy(out=dst, in_=src), reads=[("B", bi)],
                   writes=[("vr", slot_v)])
            odst = ap(vring, slot_v * VW + 64, [[192, 4], [1, 64]])
            osrc_t = onesb if kind == "own" else flagsb
            osrc = ap(osrc_t, 0, [[0, 4], [1, 64]])
            sc.add("pool", lambda e, odst=odst, osrc=osrc: e.tensor_copy(out=odst, in_=osrc), reads=[],
                   writes=[("vr1", slot_v)])

    ybf = [sil_bf[k // 2][:, (k % 2) * 512:(k % 2) * 512 + 512] for k in range(4)]

    def retention_stage(t):
        n0 = 4 * t
        cb = ap(caus, 0, [[0, 4], [1, 128]])
        pend = []

        def sc_op(tc):
            def fsc(e, tc=tc):
                ins = None
                for h in range(4):
                    ins = e.matmul(TBf[0][:, h * 128:(h + 1) * 128],
                                   lhsT=U[:, (20 + h) * 512 + tc * 128:(20 + h) * 512 + (tc + 1) * 128],
                                   rhs=U[:, (16 + h) * 512 + tc * 128:(16 + h) * 512 + (tc + 1) * 128],
                                   start=True, stop=True)
                return ins
            sc.add("pe", fsc, reads=[("U", 16 + h) for h in range(8)], writes=[("T", 0)])
            sb_ = scb[tc % 2]
            sc.add("dve", lambda e, sb_=sb_: e.tensor_tensor(
                out=sb_[:, :], in0=TBf[0][:, :], in1=cb, op=ALU.mult),
                reads=[("T", 0)], writes=[("scb", tc % 2)])

        def out_op(tc):
            sb_ = scb[tc % 2]
            ri = tc % 2
            sl = (n0 + tc) % 6

            def fo(e, tc=tc, ri=ri, sb_=sb_, sl=sl):
                ins = None
                for h in range(4):
                    e.matmul(TBf[1][:, h * 128:(h + 1) * 128], lhsT=sb_[:, h * 128:(h + 1) * 128],
                             rhs=U[:, (8 + tc) * 512 + h * 128:(8 + tc) * 512 + (h + 1) * 128],
                             start=True, stop=False)
                    ins = e.matmul(TBf[1][:, h * 128:(h + 1) * 128],
                                   lhsT=U[:, (16 + h) * 512 + tc * 128:(16 + h) * 512 + (tc + 1) * 128],
                                   rhs=Sb[:, sl * 512 + h * 128: sl * 512 + (h + 1) * 128],
                                   start=False, stop=True)
                return ins
            sc.add("pe", fo, reads=[("scb", tc % 2), ("U", 8 + tc), ("Sb", sl)] + [("U", 16 + h) for h in range(4)],
                   writes=[("T", 1)])
            sc.add("act", lambda e: e.copy(out=retf[:, :], in_=TBf[1][:, :]), reads=[("T", 1)],
                   writes=["retf"])
            sc.add("act", lambda e: e.activation(out=sqf[:, :], in_=TBf[1][:, :], func=AF.Square),
                   reads=[("T", 1)], writes=["sqf"])
            sc.add("dve", lambda e: e.tensor_reduce(out=st[:, 0:4], in_=ap(retf, 0, [[128, 4], [1, 128]]),
                                                    axis=AX.X, op=ALU.add), reads=["retf"], writes=["st_a"])
            sc.add("dve", lambda e: e.tensor_reduce(out=st[:, 4:8], in_=ap(sqf, 0, [[128, 4], [1, 128]]),
                                                    axis=AX.X, op=ALU.add), reads=["sqf"], writes=["st_b"])
            sc.add("dve", lambda e: e.tensor_single_scalar(out=st[:, 8:16], in_=st[:, 0:8], scalar=1.0 / 128,
                                                           op=ALU.mult), reads=["st_a", "st_b"], writes=["st_c"])
            sc.add("dve", lambda e: e.tensor_tensor(out=st[:, 16:20], in0=st[:, 8:12], in1=st[:, 8:12], op=ALU.mult),
                   reads=["st_c"], writes=["st_d"])
            sc.add("dve", lambda e: e.tensor_tensor(out=st[:, 20:24], in0=st[:, 12:16], in1=st[:, 16:20],
                                                    op=ALU.subtract), reads=["st_c", "st_d"], writes=["st_e"])
            sc.add("dve", lambda e: e.tensor_single_scalar(out=st[:, 24:28], in_=st[:, 20:24], scalar=EPS, op=ALU.add),
                   reads=["st_e"], writes=["st_f"])
            sc.add("act", lambda e: e.activation(out=st[:, 24:28], in_=st[:, 24:28], func=AF.Sqrt),
                   reads=["st_f"], writes=["st_f"])
            sc.add("dve", lambda e: e.reciprocal(out=st[:, 24:28], in_=st[:, 24:28]),
                   reads=["st_f"], writes=["st_f"])
            mean_b = ap(st, 8, [[1, 4], [0, 128]])
            rstd_b = ap(st, 24, [[1, 4], [0, 128]])
            for h in range(4):
                sc.add("dve", lambda e, h=h: e.tensor_scalar(
                    out=retf[:, h * 128:(h + 1) * 128], in0=retf[:, h * 128:(h + 1) * 128],
                    scalar1=st[:, 8 + h:9 + h], scalar2=st[:, 24 + h:25 + h], op0=ALU.subtract, op1=ALU.mult),
                    reads=["retf", "st_c", "st_f"], writes=["retf"])
            yb = ybf[tc]
            ykeys = [("ybf", tc), ("sil", tc // 2)]
            sc.add("pool", lambda e, tc=tc, yb=yb: e.tensor_tensor(out=yb, in0=retf[:, :],
                                                                   in1=U[:, (12 + tc) * 512:(13 + tc) * 512], op=ALU.mult),
                   reads=["retf", ("U", 12 + tc)], writes=ykeys)
            base = (tc % 2) * 512
            sbt = sil_bf[tc // 2]
            pend.append(lambda tc=tc, sbt=sbt, base=base, ykeys=ykeys: transpose4(
                lambda h: sbt[:, base + h * 128: base + (h + 1) * 128], ykeys,
                ap(Vb, tc * 128, [[512, 4], [1, 128]]), [("V", h) for h in range(4)], "dve"))

        return sc_op, out_op, pend

    SBANKS = [(("B", 0), B[0]), (("B", 1), B[1]), (("R", 1), R[1])]
    TBf = [TB[0].bitcast(F32), TB[1].bitcast(F32)]

    def attention_stage(t, ret):
        ret_sc, ret_out, pend = ret
        units = [("n", 0, g) for g in range(4)] + [("n", -1, g) for g in range(4)] + [("f", d) for d in (-2, -3, -4)]
        NU = len(units)
        stream = [(hd, u) for hd in range(8) for u in range(NU)]
        hooks = {}

        def krcols(j, d, g):
            return j * NRING * 128 + ((t + d) % 5) * 512 + g

        def s_op(k):
            hd, u = stream[k]
            j = hd // 2
            un = units[u]
            sbk = k % 3
            skey, sbank = SBANKS[sbk]
            if un[0] == "n":
                _, d, g = un
                kc0 = krcols(j, d, g)
                rk = [("kr", j, ((t + d) % 5) * 4 + c) for c in range(4)]
                sc.add("pe", lambda e, kc0=kc0, sbank=sbank, hd=hd: e.matmul(
                    sbank[:, :], lhsT=ap(kring, kc0, [[4, 128]]), rhs=aqT[:, hd * 512:(hd + 1) * 512],
                    start=True, stop=True), reads=rk + [("aqT", hd)], writes=[skey])
                mk = masks[:, ((d + 1) * 4 + g) * 512:((d + 1) * 4 + g + 1) * 512]
            else:
                _, d = un
                rk = [("kr", j, ((t + d) % 5) * 4 + c) for c in range(4)]
                kcs = [krcols(j, d, g) for g in range(4)]

                def fs(e, kcs=kcs, sbank=sbank, hd=hd):
                    ins = None
                    for g in range(4):
                        ins = e.matmul(sbank[:, g * 128:(g + 1) * 128], lhsT=ap(kring, kcs[g], [[4, 128]]),
                                       rhs=ap(aqT, hd * 512 + g, [[4, 128]]), start=True, stop=True)
                    return ins
                sc.add("pe", fs, reads=rk + [("aqT", hd)], writes=[skey])
                fk = 0 if d == -4 else 1
                mk = ap(fmasks, fk * 128, [[0, 4], [1, 128]])
            ebk = k % 4
            sc.add("act", lambda e, ebk=ebk, sbank=sbank: e.activation(out=eb[ebk][:, :], in_=sbank[:, :], func=AF.Exp),
                   reads=[skey], writes=[("eb", ebk)])
            sc.add("dve", lambda e, ebk=ebk, mk=mk: e.tensor_tensor(
                out=eb[ebk][:, :], in0=eb[ebk][:, :], in1=mk, op=ALU.mult),
                reads=[("eb", ebk)], writes=[("eb", ebk)])

        def pv_op(k):
            hd, u = stream[k]
            j, a = hd // 2, hd % 2
            ob = 2 + hd % 2
            un = units[u]
            sbk = k % 4
            if un[0] == "n":
                _, d, g = un
                slot = ((t + d) % 5) * 4 + g
                vcol = slot * VW + j * 192 + 64 * a
                sc.add("pe", lambda e, vcol=vcol, sbk=sbk, u=u, ob=ob: e.matmul(
                    B[ob][:, :], lhsT=vring[:, vcol:vcol + 128], rhs=eb[sbk][:, :], start=(u == 0), stop=False),
                    reads=[("vr", slot), ("vr1", slot), ("eb", sbk)], writes=[("B", ob)])
            else:
                _, d = un
                slots = [((t + d) % 5) * 4 + g for g in range(4)]

                def fp(e, slots=slots, sbk=sbk, u=u, j=j, a=a, ob=ob):
                    ins = None
                    for g in range(4):
                        vcol = slots[g] * VW + j * 192 + 64 * a
                        ins = e.matmul(ap(B[ob], g, [[4, 128]]), lhsT=vring[:, vcol:vcol + 128],
                                       rhs=eb[sbk][:, g * 128:(g + 1) * 128], start=False,
                                       stop=(u == NU - 1 and g == 3))
                    return ins
                sc.add("pe", fp, reads=[("vr", sl_) for sl_ in slots] + [("vr1", sl_) for sl_ in slots] + [("eb", sbk)],
                       writes=[("B", ob)])

        def recip_fn(hd, piece):
            if piece != 0:
                return
            a = hd % 2
            ob = 2 + hd % 2
            d0_ = 64 if a == 0 else 0
            sc.add("act", lambda e, d0_=d0_, ob=ob: e.activation(
                out=lnd[d0_:d0_ + 64, :], in_=B[ob][d0_:d0_ + 64, :], func=AF.Ln),
                reads=[("B", ob)], writes=["lnd"])
            sc.add("act", lambda e, d0_=d0_: e.activation(
                out=Rb[d0_:d0_ + 64, :], in_=lnd[d0_:d0_ + 64, :], func=AF.Exp, scale=-1.0),
                reads=["lnd"], writes=[("Rb", p_) for p_ in range(4)])

        def post_fn(hd):
            j, a = hd // 2, hd % 2
            ob = 2 + hd % 2
            n0_ = 0 if a == 0 else 64
            ri = 0
            if a == 0:
                sc.add("pe", lambda e, ri=ri: e.matmul(R[ri][0:64, :], lhsT=ident[64:128, 64:128], rhs=Rb[64:128, :],
                                                       start=True, stop=True), reads=[("Rb", p_) for p_ in range(4)],
                       writes=[("R", ri)])
            else:
                sc.add("pe", lambda e, ri=ri: e.matmul(R[ri][:, :], lhsT=shif[0:64, :], rhs=Rb[0:64, :],
                                                       start=True, stop=True), reads=[("Rb", p_) for p_ in range(4)],
                       writes=[("R", ri)])
            sc.add("act", lambda e, ri=ri, n0_=n0_: e.copy(out=rsb[n0_:n0_ + 64, :], in_=R[ri][n0_:n0_ + 64, :]),
                   reads=[("R", ri)], writes=["rsb2"])
            pg = 4 + j
            sc.add("dve", lambda e, n0_=n0_, ob=ob, pg=pg: e.tensor_tensor(
                out=Vb[n0_:n0_ + 64, pg * 512:(pg + 1) * 512], in0=B[ob][n0_:n0_ + 64, :], in1=rsb[n0_:n0_ + 64, :],
                op=ALU.mult), reads=[("B", ob), "rsb2"], writes=[("V", pg)])

        nstream_ = len(stream)
        for hd in range(8):
            kend = hd * NU + NU - 1
            for piece in range(4):
                hooks.setdefault(kend + 2 + piece, []).append(lambda hd=hd, piece=piece: recip_fn(hd, piece))
            hooks.setdefault(kend + 8, []).append(lambda hd=hd: post_fn(hd))
            if hd < 4:
                k0 = hd * 2 * NU
                hooks.setdefault(k0 + 3, []).append(lambda hd=hd: ret_sc(hd))
                hooks.setdefault(k0 + 7, []).append(lambda hd=hd: ret_out(hd))
                hooks.setdefault(k0 + 2 * NU + 1, []).append(lambda hd=hd: pend[hd]())
        for k in range(3):
            s_op(k)
        for k in range(nstream_):
            if k + 3 < nstream_:
                s_op(k + 3)
            pv_op(k)
            for f_ in hooks.pop(k, []):
                f_()
        for k in sorted(hooks):
            for f_ in hooks[k]:
                f_()

    def outproj_stage(par):
        x = xt[par]
        vk = [("V", i) for i in range(8)]
        for cg in range(2):
            s0 = next_unit(UIDX[("OUT", cg, 0)])
            s1 = next_unit(UIDX[("OUT", cg, 1)])
            for tc in range(4):
                def fmm(e, tc=tc, s0=s0, s1=s1):
                    ins = None
                    for kc in range(8):
                        s = s0 if kc < 4 else s1
                        ins = e.matmul(B[tc][:, :], lhsT=Vb[:, kc * 512 + tc * 128: kc * 512 + (tc + 1) * 128],
                                       rhs=wslice(s, (kc % 4) * 512, 512), start=(kc == 0), stop=(kc == 7))
                    return ins
                sc.add("pe", fmm, reads=vk + [("w", s0), ("w", s1)], writes=[("B", tc)])
            for tc in range(4):
                xs = x[:, tc * 1024 + cg * 512: tc * 1024 + (cg + 1) * 512]
                sc.add("dve", lambda e, tc=tc, xs=xs: e.tensor_tensor(out=xs, in0=B[tc][:, :], in1=xs, op=ALU.add),
                       reads=[("B", tc), ("xt", par, tc)], writes=[("xt", par, tc)])

    def load_x(t):
        par = t % 2
        src = xin[t * T:(t + 1) * T, :].rearrange("(tc p) d -> p tc d", p=128)
        dst = xt[par][:, :].rearrange("p (tc d) -> p tc d", tc=4)
        sc.add("sp", lambda e: e.dma_start(out=dst, in_=src), writes=xkeys(par), dma=("xt", par))

    def load_tab(t):
        srcc = cosd[t * T:(t + 1) * T, :].rearrange("(tc p) d -> p tc d", p=128)
        srcs = ssd[t * T:(t + 1) * T, :].rearrange("(tc p) d -> p tc d", p=128)
        sc.add("sp", lambda e: e.dma_start(out=tab[:, 0:512].rearrange("p (tc d) -> p tc d", tc=4), in_=srcc),
               writes=["tabc"], dma="tabc")
        sc.add("sp", lambda e: e.dma_start(out=tab[:, 512:1024].rearrange("p (tc d) -> p tc d", tc=4), in_=srcs),
               writes=["tabs"], dma="tabs")

    load_x(0)
    if ntile > 1:
        load_x(1)
    nxt = {}
    defer_x = []
    store_q = []

    def hoist_chain(tn):
        nxt["gs"] = nxt.pop("pre") if "pre" in nxt else load_gain(0)
        norm_stage(tn % 2, nxt["gs"], chain_only=True)

    hoist_chain(0)
    norm_transposes()
    for t in range(ntile):
        par = t % 2
        kind = tile_kind(t)
        load_tab(t)
        gs2 = load_gain(1)
        ffn_stage(par, 0, early_hook=(lambda: store_q.pop(0)()) if store_q else None)
        if t == 0:
            dump("h1", xt[par][:, 0:1024], xkeys(par))
        norm_stage(par, gs2)
        last = (t + 1 == ntile)
        if kind != "own":
            inproj_stage(t, kind, hook_after_rv=(None if last else (lambda: hoist_chain(t + 1))))
            if not last:
                if t + 2 < ntile:
                    load_x(t + 2)
                norm_transposes()
            continue
        inproj_stage(t, kind)
        if defer_x:
            load_x(defer_x.pop(0))
        gs3 = load_gain(2)
        pend = retention_stage(t)
        attention_stage(t, pend)
        outproj_stage(par)
        norm_stage(par, gs3)
        gs4 = load_gain(3)
        ffn_stage(par, 1, gu_hook=(None if last else (lambda: hoist_chain(t + 1))),
                  mid_hook=(None if last else norm_transposes),
                  early_hook=(None if last else (lambda: nxt.__setitem__("pre", load_gain(0)))))
        norm_stage(par, gs4, final=True)
        ot = t - nh
        dst = yout[ot * T:(ot + 1) * T, :].rearrange("(tc p) d -> p tc d", p=128)
        src = xt[par][:, :].rearrange("p (tc d) -> p tc d", tc=4)
        store_q.append(lambda dst=dst, src=src, par=par: sc.add(
            "sp", lambda e: e.dma_start(out=dst, in_=src), reads=xkeys(par), dma=("xt_out", par)))
        if last:
            store_q.pop(0)()
        if t + 2 < ntile:
            defer_x.append(t + 2)
    sc.add("sp", lambda e: e.nop(), writes=xkeys(0) + xkeys(1))

    with nc.allow_low_precision("bf16 matmul operands by design; fp32 accumulation"):
        with nc.Block() as block:
            sc.emit(nc, block)
    return nc


_CACHE = {}


def _prep_core(x, b, s, half, common):
    own = x[b, s * half:(s + 1) * half]
    if s == 0:
        halo = np.zeros_like(own)
        flag = 0.0
    else:
        halo = x[b, (s - 1) * half:s * half]
        flag = 1.0
    xin = np.ascontiguousarray(np.concatenate([halo, own], axis=0))
    pos = np.arange((s - 1) * half, (s + 1) * half)
    cos, ssn = _rope_tables(pos)
    m = dict(common)
    m.update(xin=xin, cosd=cos, ssd=ssn, flags=np.full((128, 64), flag, np.float32))
    return m


def run(x, params, nh, no, dbg=()):
    key = (nh, no, tuple(dbg))
    if key not in _CACHE:
        _CACHE[key] = build(nh, no, dbg)
    nc = _CACHE[key]
    cst = _consts()
    f32 = lambda a: np.ascontiguousarray(np.asarray(a, np.float32))
    gains = np.stack([np.broadcast_to(f32(params[k]).reshape(1, D), (128, D))
                      for k in ("norm_ffn1", "norm_mix", "norm_ffn2", "norm_final")]).copy()
    common = dict(
        w_g1=f32(params["ffn1_w_gate"][0]), w_u1=f32(params["ffn1_w_up"][0]), w_d1=f32(params["ffn1_w_down"][0]),
        w_g2=f32(params["ffn2_w_gate"][0]), w_u2=f32(params["ffn2_w_up"][0]), w_d2=f32(params["ffn2_w_down"][0]),
        w_in=f32(params["w_in"][0]), w_out=f32(params["w_out"][0]), gains=gains,
        rgain=np.broadcast_to(f32(params["ret_norm_gain"]).reshape(1, 512), (128, 512)).copy(),
        small=cst["small"], g128=cst["g128"], caus=cst["caus"], ident=cst["ident"], shif=cst["shif"],
        masks=cst["masks"], fmasks=cst["fmasks"],
    )
    half = no * T
    nb = x.shape[0]
    in_maps = []
    for c in range(8):
        b, s = (c // 2) % nb, c % 2
        in_maps.append(_prep_core(x, b, s, half, common))
    res = run_bass_kernel_spmd(nc, in_maps, core_ids=list(range(8)))
    return res


def kernel(x, norm_ffn1, ffn1_w_gate, ffn1_w_up, ffn1_w_down, norm_mix, w_in, ret_norm_gain, w_out,
           norm_ffn2, ffn2_w_gate, ffn2_w_up, ffn2_w_down, norm_final):
    x = np.asarray(x, np.float32)
    params = dict(norm_ffn1=norm_ffn1, ffn1_w_gate=ffn1_w_gate, ffn1_w_up=ffn1_w_up, ffn1_w_down=ffn1_w_down,
                  norm_mix=norm_mix, w_in=w_in, ret_norm_gain=ret_norm_gain, w_out=w_out, norm_ffn2=norm_ffn2,
                  ffn2_w_gate=ffn2_w_gate, ffn2_w_up=ffn2_w_up, ffn2_w_down=ffn2_w_down, norm_final=norm_final)
    params = {k: np.asarray(v) for k, v in params.items()}
    res = run(x, params, 8, 8)
    out = np.empty((4, 8192, D), np.float32)
    for c in range(8):
        b, s = c // 2, c % 2
        out[b, s * 4096:(s + 1) * 4096] = np.asarray(res.results[c]["yout"], np.float32)
    return out
```

```python
import numpy as np
import ml_dtypes
import concourse.bass as bass
import concourse.mybir as mybir
from concourse.bass_utils import run_bass_kernel_spmd

F32 = mybir.dt.float32
BF16 = mybir.dt.bfloat16
ALU = mybir.AluOpType
AF = mybir.ActivationFunctionType
AX = mybir.AxisListType

D = 1024
DFF = 2816
NMC = DFF // 128
T = 512
NSLOT = 5
NCV = 12
NRING = 20
VW = 768
NMASK = 8
EPS = 1e-6
INCOLS = 3584


class _Op:
    __slots__ = ("eng", "fn", "deps", "inc", "dma", "val")


class Sched:
    ENGS = ("pe", "act", "dve", "pool", "sp")

    def __init__(self):
        self.q = {e: [] for e in self.ENGS}
        self.lastw = {}
        self.readers = {}
        self.dma_n = {}

    def add(self, eng, fn, reads=(), writes=(), dma=None):
        op = _Op()
        op.eng = eng
        op.fn = fn
        op.dma = dma
        op.inc = False
        op.val = 0
        deps = {}
        for k in reads:
            w = self.lastw.get(k)
            if w is not None:
                deps[id(w)] = w
        for k in writes:
            w = self.lastw.get(k)
            if w is not None and (w.dma is not None or dma is not None or w.eng != eng or eng != "pe"):
                deps[id(w)] = w
            for r in self.readers.get(k, ()):
                if r.dma is not None or dma is not None or r.eng != eng:
                    deps[id(r)] = r
        op.deps = list(deps.values())
        for o in op.deps:
            o.inc = True
        for k in reads:
            self.readers.setdefault(k, []).append(op)
        for k in writes:
            self.lastw[k] = op
            self.readers[k] = []
        if dma is not None:
            n = self.dma_n.get(dma, 0) + 1
            self.dma_n[dma] = n
            op.val = 16 * n
            op.inc = True
        self.q[eng].append(op)
        return op

    def emit(self, nc, block):
        esem = {e: nc.alloc_semaphore("sem_" + e) for e in self.ENGS}
        dsem = {k: nc.alloc_semaphore("dsem_%d" % i) for i, k in enumerate(self.dma_n)}
        for e in self.ENGS:
            c = 0
            for op in self.q[e]:
                if op.dma is None and op.inc:
                    c += 1
                    op.val = c

        def semof(op):
            return dsem[op.dma] if op.dma is not None else esem[op.eng]

        def run(e, eng):
            seen = {}
            for op in self.q[e]:
                need = {}
                for d in op.deps:
                    s = semof(d)
                    if need.get(s.num, (None, 0))[1] < d.val:
                        need[s.num] = (s, d.val)
                for num, (s, v) in need.items():
                    if seen.get(num, 0) < v:
                        eng.wait_ge(s, v)
                        seen[num] = v
                ins = op.fn(eng)
                if op.inc:
                    ins.then_inc(semof(op), 16 if op.dma is not None else 1)

        @block.tensor
        def _(eng):
            run("pe", eng)

        @block.scalar
        def _(eng):
            run("act", eng)

        @block.vector
        def _(eng):
            run("dve", eng)

        @block.gpsimd
        def _(eng):
            run("pool", eng)

        @block.sync
        def _(eng):
            run("sp", eng)


def _gammas():
    return 1.0 - 2.0 ** (-5.0 - np.arange(4, dtype=np.float64))


def _mask_tables():
    near = np.zeros((8, 128, 512), np.float32)
    i = np.arange(128)[:, None]
    j = np.arange(512)[None, :]
    for d in (-1, 0):
        for g in range(4):
            dist = j - (512 * d + g + 4 * i)
            m = ((dist >= 0) & (dist <= 128)).astype(np.float32)
            m += ((dist >= 0) & (dist % 4 == 0) & (dist <= 512)).astype(np.float32)
            m += ((dist >= 0) & (dist % 16 == 0) & (dist <= 2048)).astype(np.float32)
            near[(d + 1) * 4 + g] = m
    far = np.zeros((2, 128, 128), np.float32)
    ip = np.arange(128)[None, :]
    for k, d in enumerate((-4, -3)):
        dist = 4 * (ip - i) - 512 * d
        far[k] = ((dist % 16 == 0) & (dist <= 2048) & (dist >= 0)).astype(np.float32)
    dist2 = 4 * (ip - i) + 1024
    assert np.array_equal(((dist2 % 16 == 0) & (dist2 <= 2048)).astype(np.float32), far[1])
    assert dist2.min() > 512
    return near, far


def _consts():
    g = _gammas()
    c = np.arange(128, dtype=np.float64)
    decq = np.stack([g[h] ** (c + 1) for h in range(4)], axis=1)
    deck = np.stack([(128.0 ** -0.5) * g[h] ** (-(c + 1)) for h in range(4)], axis=1)
    g128 = np.repeat(g ** 128, 128)[None, :].repeat(128, axis=0)
    caus = (np.arange(128)[None, :] >= np.arange(128)[:, None]).astype(np.float32)
    ident = np.eye(128, dtype=np.float32)
    shif = np.zeros((128, 128), np.float32)
    for k in range(64):
        shif[k, 64 + k] = 1.0
    small = np.zeros((128, 8), np.float32)
    small[:, 0:4] = decq
    small[:, 4:8] = deck
    near, far = _mask_tables()
    return dict(small=small, g128=g128.astype(np.float32), caus=caus, ident=ident, shif=shif,
                masks=near, fmasks=far)


def _rope_tables(pos):
    pos = pos.astype(np.float32)
    inv = (10000.0 ** (-np.arange(0, 128, 2, dtype=np.float32) / 128.0)).astype(np.float32)
    ang = (pos[:, None] * inv[None, :]).astype(np.float32)
    cos = np.repeat(np.cos(ang).astype(np.float32), 2, axis=1)
    sin = np.repeat(np.sin(ang).astype(np.float32), 2, axis=1)
    sgn = np.tile(np.array([-1.0, 1.0], np.float32), 64)[None, :]
    return np.ascontiguousarray(cos), np.ascontiguousarray(sin * sgn)


def _units():
    units = []
    for f in (0, 1):
        for mc in range(NMC):
            units.append(("GU", f, mc))
        for cg in range(2):
            for q in range(6):
                units.append(("DN", f, cg, q))
    for g in range(5):
        for half in range(2):
            units.append(("TM", g, half))
    for i in range(4):
        units.append(("FM", i))
    for cg in range(2):
        for half in range(2):
            units.append(("OUT", cg, half))
    return units


UNITS = _units()
UIDX = {u: i for i, u in enumerate(UNITS)}
TM_COL = {0: 0, 1: 512, 2: 1024, 3: 1536, 4: 3072}


def build(nh=8, no=8, dbg=()):
    nc = bass.Bass("TRN2", target_bir_lowering=False)
    sc = Sched()
    ntile = nh + no
    ntok = ntile * T

    def din(name, shape, dt=F32):
        return nc.dram_tensor(name, list(shape), dt, kind="ExternalInput")

    xin = din("xin", [ntok, D])
    cosd = din("cosd", [ntok, 128])
    ssd = din("ssd", [ntok, 128])
    wd = {
        ("g", 0): din("w_g1", [D, DFF]), ("u", 0): din("w_u1", [D, DFF]), ("d", 0): din("w_d1", [DFF, D]),
        ("g", 1): din("w_g2", [D, DFF]), ("u", 1): din("w_u2", [D, DFF]), ("d", 1): din("w_d2", [DFF, D]),
    }
    w_in = din("w_in", [D, INCOLS])
    w_out = din("w_out", [D, D])
    gains_d = din("gains", [4, 128, D])
    rgain_d = din("rgain", [128, 512])
    small_d = din("small", [128, 8])
    g128_d = din("g128", [128, 512])
    caus_d = din("caus", [128, 128])
    ident_d = din("ident", [128, 128])
    shif_d = din("shif", [128, 128])
    masks_d = din("masks", [NMASK, 128, 512])
    fmasks_d = din("fmasks", [2, 128, 128])
    flags_d = din("flags", [128, 64])
    yout = nc.dram_tensor("yout", [no * T, D], F32, kind="ExternalOutput")
    wsc = nc.dram_tensor("wsc", [len(UNITS), 128, 2048], BF16)
    dbg_out = {}
    for name, shape in dbg:
        dbg_out[name] = nc.dram_tensor("dbg_" + name, list(shape), F32, kind="ExternalOutput")

    def sb(name, cols, dt):
        return nc.alloc_sbuf_tensor("s_" + name, [128, cols], dt)

    xt = [sb("xt0", 4096, F32), sb("xt1", 4096, F32)]
    Vb = sb("Vb", 4096, BF16)
    xnT = sb("xnT", 4096, BF16)
    U = sb("U", 24 * 512, BF16)
    sil = [sb("sil0", 512, F32), sb("sil1", 512, F32)]
    sil_bf = [t_.bitcast(BF16) for t_ in sil]
    rt = [sb("rt0", 512, F32), sb("rt1", 512, F32)]
    aqT = sb("aqT", 8 * 512, BF16)
    kring = sb("kring", 4 * NRING * 128, BF16)
    vring = sb("vring", NRING * VW, BF16)
    masks = sb("masks", NMASK * 512, BF16)
    fmasks = sb("fmasks", 2 * 128, BF16)
    eb = [sb("eb%d" % i_, 512, BF16) for i_ in range(4)]
    wr = sb("wr", NSLOT * 2048, BF16)
    tab = sb("tab", 2 * 512, F32)
    gn = sb("gn", 2 * 1024, F32)
    rgain = sb("rgain_sb", 512, F32)
    Sf = sb("Sf", 512, F32)
    Sb = sb("Sb", 6 * 512, BF16)
    retf = sb("retf", 512, F32)
    sqf = sb("sqf", 512, F32)
    rsb = sb("rsb2", 512, F32)
    lnd = sb("lnd", 512, F32)
    scb = [sb("scb0", 512, BF16), sb("scb1", 512, BF16)]
    Rb = sb("Rb", 512, BF16)
    ident = sb("ident_sb", 128, BF16)
    shif = sb("shif_sb", 128, BF16)
    caus = sb("caus_sb", 128, F32)
    small = sb("small_sb", 8, F32)
    g128 = sb("g128_sb", 512, F32)
    flagsb = sb("flags_sb", 64, BF16)
    onesb = sb("ones_sb", 64, BF16)
    st = sb("st", 32, F32)
    ss = sb("ss", 8, F32)

    B = [nc.alloc_psum_tensor("B%d" % i, [128, 512], F32) for i in range(4)]
    R = [nc.alloc_psum_tensor("R%d" % i, [128, 512], F32) for i in range(2)]
    TB = [nc.alloc_psum_tensor("TB%d" % i, [128, 1024], BF16) for i in range(2)]

    def pstep(t):
        return t[:, :].ap[0][0]

    def ap(t, off, dims, p0=0, npart=128):
        ps = pstep(t)
        return bass.AP(t, p0 * ps + off, [[ps, npart]] + [list(d) for d in dims])

    cload_keys = []

    def cload(dst_ap, src_ap, key, eng="sp"):
        sc.add(eng, lambda e, o=dst_ap, i=src_ap: e.dma_start(out=o, in_=i), writes=[key], dma=("c", key))
        cload_keys.append(key)

    cload(small[:, :], small_d[:, :], "small")
    cload(g128[:, :], g128_d[:, :], "g128")
    cload(caus[:, :], caus_d[:, :], "caus")
    cload(rgain[:, :], rgain_d[:, :], "rgain")
    cload(ident[:, :], ident_d[:, :], "ident", eng="pool")
    cload(shif[:, :], shif_d[:, :], "shif", eng="pool")
    cload(flagsb[:, :], flags_d[:, :], "flags", eng="pool")
    cload(masks[:, :].rearrange("p (r q) -> p r q", r=NMASK), masks_d[:, :, :].rearrange("r p q -> p r q"),
          "masks", eng="pool")
    cload(fmasks[:, :].rearrange("p (r q) -> p r q", r=2), fmasks_d[:, :, :].rearrange("r p q -> p r q"),
          "fmasks", eng="pool")
    sc.add("dve", lambda e: e.memset(onesb[:, :], 1.0), writes=["ones"])
    sc.add("dve", lambda e: e.memset(vring[:, :], 0.0), writes=["vring_init"])
    sc.add("dve", lambda e: e.memset(kring[:, :], 0.0), writes=["kring_init"])
    sc.add("dve", lambda e: e.memset(Sf[:, :], 0.0), writes=["Sf"])
    sc.add("dve", lambda e: e.memset(Sb[:, :], 0.0), writes=["Sb_init"])
    sc.add("dve", lambda e: e.memset(st[:, 0:28], 0.0), writes=["st_init"])
    allc = cload_keys + ["ones", "vring_init", "kring_init", "Sb_init", "st_init"]
    for e in ("pe", "act", "dve", "pool"):
        if e == "pe":
            sc.add(e, lambda en: en.transpose(out=TB[1][:, 0:128], in_=ident[:, :], identity=ident[:, :]),
                   reads=allc, writes=[("T", 1)])
        elif e == "act":
            sc.add(e, lambda en: en.copy(out=st[:, 28:30], in_=small[:, 0:2]), reads=allc, writes=["bar_act"])
        elif e == "dve":
            sc.add(e, lambda en: en.memset(st[:, 30:32], 0.0), reads=allc, writes=["bar_dve"])
        else:
            sc.add(e, lambda en: en.memset(aqT[:, :], 0.0), reads=allc, writes=[("aqT", i) for i in range(8)])

    cv_state = {"next": 0, "n": 0}

    def wview(w):
        return w[:, :].rearrange("(kc p) n -> p kc n", p=128)

    def conv_dma(dst, src, key):
        k = cv_state["n"]
        cv_state["n"] += 1
        sc.add("pool", lambda e, o=dst, i=src: e.dma_start(out=o, in_=i),
               writes=[key, ("cvslot", k % NCV)], dma=("cv", k % NCV))

    def conv_unit(ui):
        u = UNITS[ui]
        dst = wsc[ui, :, :]
        if u[0] == "GU":
            _, f, mc = u
            d4 = dst.rearrange("p (t kc j) -> p t kc j", t=2, kc=8)
            conv_dma(d4[:, 0, :, :], wview(wd[("g", f)])[:, :, mc * 128:(mc + 1) * 128], ("wsc", ui, 0))
            conv_dma(d4[:, 1, :, :], wview(wd[("u", f)])[:, :, mc * 128:(mc + 1) * 128], ("wsc", ui, 1))
        elif u[0] == "DN":
            _, f, cg, q = u
            nk = 4 if q < 5 else 2
            d3 = dst.rearrange("p (k n) -> p k n", k=4)
            conv_dma(d3[:, 0:nk, :], wview(wd[("d", f)])[:, 4 * q:4 * q + nk, cg * 512:(cg + 1) * 512],
                     ("wsc", ui, 0))
        elif u[0] == "TM":
            _, g, half = u
            c0 = TM_COL[g]
            d3 = dst.rearrange("p (k n) -> p k n", k=4)
            conv_dma(d3, wview(w_in)[:, 4 * half:4 * half + 4, c0:c0 + 512], ("wsc", ui, 0))
        elif u[0] == "FM":
            _, i = u
            d4 = dst.rearrange("p (t kc j) -> p t kc j", t=2, kc=8)
            for m in range(2):
                c0 = 2048 + (2 * i + m) * 128
                conv_dma(d4[:, m, :, :], wview(w_in)[:, :, c0:c0 + 128], ("wsc", ui, m))
        else:
            _, cg, half = u
            d3 = dst.rearrange("p (k n) -> p k n", k=4)
            conv_dma(d3, wview(w_out)[:, 4 * half:4 * half + 4, cg * 512:(cg + 1) * 512], ("wsc", ui, 0))

    conv_done = set()

    def ensure_conv(ui):
        if ui not in conv_done:
            conv_done.add(ui)
            conv_unit(ui)

    stream = []
    stream_pos = {"issued": 0}

    def unit_list(kind):
        l = [UIDX[("GU", 0, mc)] for mc in range(NMC)]
        l += [UIDX[("DN", 0, cg, q)] for cg in range(2) for q in range(6)]
        if kind == "halo_kv" or kind == "halo_k":
            gs = [1, 2]
        else:
            gs = [1, 2, 0, 3]
        for g in gs:
            l += [UIDX[("TM", g, h)] for h in range(2)]
        if kind == "halo_kv":
            l += [UIDX[("FM", 2)], UIDX[("FM", 3)]]
            l += [UIDX[("TM", 4, h)] for h in range(2)]
        if kind == "own":
            l += [UIDX[("FM", i)] for i in range(4)]
            l += [UIDX[("TM", 4, h)] for h in range(2)]
            l += [UIDX[("OUT", cg, h)] for cg in range(2) for h in range(2)]
            l += [UIDX[("GU", 1, mc)] for mc in range(NMC)]
            l += [UIDX[("DN", 1, cg, q)] for cg in range(2) for q in range(6)]
        return l

    def tile_kind(t):
        if t >= nh:
            return "own"
        return "halo_kv" if t >= nh - 4 else "halo_k"

    for t in range(ntile):
        stream += unit_list(tile_kind(t))
    nstream = len(stream)
    use_ptr = {"i": 0}

    def issue_load(qi):
        ui = stream[qi]
        ensure_conv(ui)
        slot = qi % NSLOT
        nparts = 2 if UNITS[ui][0] in ("GU", "FM") else 1
        ncol = 1024 if (UNITS[ui][0] == "DN" and UNITS[ui][3] == 5) else 2048
        sc.add("sp", lambda e, s=slot, u=ui, ncol=ncol: e.dma_start(out=wr[:, s * 2048:s * 2048 + ncol],
                                                                    in_=wsc[u, :, 0:ncol]),
               reads=[("wsc", ui, p) for p in range(nparts)], writes=[("w", slot)], dma=("w", slot))

    def next_unit(expect):
        qi = use_ptr["i"]
        assert stream[qi] == expect, (UNITS[stream[qi]], UNITS[expect])
        while stream_pos["issued"] < min(nstream, qi + NSLOT - 1):
            issue_load(stream_pos["issued"])
            stream_pos["issued"] += 1
        for k in range(qi, min(nstream, qi + 14)):
            ensure_conv(stream[k])
        use_ptr["i"] += 1
        return qi % NSLOT

    def wslice(slot, off, n):
        return wr[:, slot * 2048 + off: slot * 2048 + off + n]

    def dump(name, src_ap, keys):
        if name in dbg_out:
            d = dbg_out[name]
            sc.add("sp", lambda e, o=d[:, :], i=src_ap: e.dma_start(out=o, in_=i), reads=keys,
                   dma=("dbg", name))

    gn_state = {"n": 0}

    def load_gain(gi):
        s = gn_state["n"] % 2
        gn_state["n"] += 1
        sc.add("sp", lambda e, s=s, gi=gi: e.dma_start(out=gn[:, s * 1024:(s + 1) * 1024], in_=gains_d[gi, :, :]),
               writes=[("gn", s)], dma=("gn", s))
        return s

    def xkeys(par):
        return [("xt", par, tc) for tc in range(4)]

    def norm_stage(par, gslot, final=False, chain_only=False):
        x = xt[par]
        for tc in range(4):
            sc.add("act", lambda e, tc=tc: e.activation(
                out=Vb[:, tc * 1024:(tc + 1) * 1024], in_=x[:, tc * 1024:(tc + 1) * 1024], func=AF.Square,
                accum_out=ss[:, tc:tc + 1]),
                reads=[("xt", par, tc)], writes=[("V", 2 * tc), ("V", 2 * tc + 1), ("ss", tc)])
        sc.add("dve", lambda e: e.tensor_scalar(out=ss[:, 4:8], in0=ss[:, 0:4], scalar1=1.0 / D, scalar2=EPS,
                                                 op0=ALU.mult, op1=ALU.add),
               reads=[("ss", tc) for tc in range(4)], writes=["rs"])
        sc.add("act", lambda e: e.activation(out=ss[:, 4:8], in_=ss[:, 4:8], func=AF.Sqrt),
               reads=["rs"], writes=["rs"])
        sc.add("dve", lambda e: e.reciprocal(out=ss[:, 4:8], in_=ss[:, 4:8]),
               reads=["rs"], writes=["rs"])
        for tc in range(4):
            if final:
                o = x[:, tc * 1024:(tc + 1) * 1024]
                wk = [("xt", par, tc)]
            else:
                o = Vb[:, tc * 1024:(tc + 1) * 1024]
                wk = [("V", 2 * tc), ("V", 2 * tc + 1)]
            sc.add("dve", lambda e, tc=tc, o=o: e.scalar_tensor_tensor(
                out=o, in0=x[:, tc * 1024:(tc + 1) * 1024], scalar=ss[:, 4 + tc:5 + tc],
                in1=gn[:, gslot * 1024:(gslot + 1) * 1024], op0=ALU.mult, op1=ALU.mult),
                reads=[("xt", par, tc), "rs", ("gn", gslot)], writes=wk)
        if final or chain_only:
            return
        norm_transposes()

    def norm_transposes():
        for kp in range(4):
            tb = TB[kp % 2]

            def f(e, kp=kp, tb=tb):
                ins = None
                for k2 in range(2):
                    kc = 2 * kp + k2
                    for tc in range(4):
                        ins = e.transpose(out=tb[:, k2 * 512 + tc * 128: k2 * 512 + (tc + 1) * 128],
                                          in_=Vb[:, tc * 1024 + kc * 128: tc * 1024 + (kc + 1) * 128],
                                          identity=ident[:, :])
                return ins
            sc.add("pe", f, reads=[("V", i) for i in range(8)], writes=[("T", kp % 2)])
            eng = "act" if kp % 2 == 0 else "dve"
            if eng == "act":
                fn = lambda e, kp=kp, tb=tb: e.copy(out=xnT[:, kp * 1024:(kp + 1) * 1024], in_=tb[:, :])
            else:
                fn = lambda e, kp=kp, tb=tb: e.tensor_copy(out=xnT[:, kp * 1024:(kp + 1) * 1024], in_=tb[:, :])
            sc.add(eng, fn, reads=[("T", kp % 2)], writes=[("xnT", 2 * kp), ("xnT", 2 * kp + 1)])

    XNT = [("xnT", k) for k in range(8)]

    DNBANK = [[(("B", i), B[i]) for i in range(4)],
              [(("R", 0), R[0]), (("R", 1), R[1]), (("B", 0), B[0]), (("B", 1), B[1])]]

    def ffn_stage(par, f, gu_hook=None, mid_hook=None, early_hook=None):
        x = xt[par]
        for mc in range(NMC):
            if mc == 8 and early_hook is not None:
                early_hook()
            slot = next_unit(UIDX[("GU", f, mc)])
            bg, bu = B[2 * (mc % 2)], B[2 * (mc % 2) + 1]
            for which, bank, bk in ((0, bg, 2 * (mc % 2)), (1, bu, 2 * (mc % 2) + 1)):
                def fmm(e, slot=slot, which=which, bank=bank):
                    ins = None
                    for kc in range(8):
                        ins = e.matmul(bank[:, :], lhsT=wslice(slot, (which * 8 + kc) * 128, 128),
                                       rhs=xnT[:, kc * 512:(kc + 1) * 512], start=(kc == 0), stop=(kc == 7))
                    return ins
                sc.add("pe", fmm, reads=XNT + [("w", slot)], writes=[("B", bk)])
            sl = sil[mc % 2]
            sc.add("act", lambda e, bg=bg, sl=sl: e.activation(out=sl[:, :], in_=bg[:, :], func=AF.Silu),
                   reads=[("B", 2 * (mc % 2))], writes=[("sil", mc % 2)])
            sc.add("dve", lambda e, bu=bu, sl=sl, mc=mc: e.tensor_tensor(
                out=U[:, mc * 512:(mc + 1) * 512], in0=bu[:, :], in1=sl[:, :], op=ALU.mult),
                reads=[("B", 2 * (mc % 2) + 1), ("sil", mc % 2)], writes=[("U", mc)])
        if gu_hook is not None:
            gu_hook()
        for cg in range(2):
            if cg == 1 and mid_hook is not None:
                mid_hook()
            for q in range(6):
                slot = next_unit(UIDX[("DN", f, cg, q)])
                nk = 4 if q < 5 else 2
                for tc in range(4):
                    bkey, bank = DNBANK[cg][tc]

                    def fmm(e, slot=slot, q=q, nk=nk, tc=tc, bank=bank):
                        ins = None
                        for i in range(nk):
                            kc = 4 * q + i
                            ins = e.matmul(bank[:, :], lhsT=U[:, kc * 512 + tc * 128: kc * 512 + (tc + 1) * 128],
                                           rhs=wslice(slot, i * 512, 512), start=(kc == 0), stop=(kc == NMC - 1))
                        return ins
                    sc.add("pe", fmm, reads=[("U", 4 * q + i) for i in range(nk)] + [("w", slot)],
                           writes=[bkey])
            for tc in range(4):
                bkey, bank = DNBANK[cg][tc]
                xs = x[:, tc * 1024 + cg * 512: tc * 1024 + (cg + 1) * 512]
                sc.add("dve", lambda e, tc=tc, xs=xs, bank=bank: e.scalar_tensor_tensor(
                    out=xs, in0=bank[:, :], scalar=0.5, in1=xs, op0=ALU.mult, op1=ALU.add),
                    reads=[bkey, ("xt", par, tc)], writes=[("xt", par, tc)])

    bank_rr = {"i": 0}

    def nextbank():
        i = bank_rr["i"] % 4
        bank_rr["i"] += 1
        return i

    def tm_group(g, strided=False):
        s0 = next_unit(UIDX[("TM", g, 0)])
        s1 = next_unit(UIDX[("TM", g, 1)])
        res = []
        for tc in range(4):
            bi = nextbank()

            def fmm(e, tc=tc, bi=bi, s0=s0, s1=s1):
                ins = None
                for kc in range(8):
                    s = s0 if kc < 4 else s1
                    lt = (ap(xnT, kc * 512 + tc, [[4, 128]]) if strided
                          else xnT[:, kc * 512 + tc * 128: kc * 512 + (tc + 1) * 128])
                    ins = e.matmul(B[bi][:, :], lhsT=lt,
                                   rhs=wslice(s, (kc % 4) * 512, 512), start=(kc == 0), stop=(kc == 7))
                return ins
            sc.add("pe", fmm, reads=XNT + [("w", s0), ("w", s1)], writes=[("B", bi)])
            res.append((tc, bi))
            yield tc, bi

    def rotary(tc, bi, is_q):
        bank = B[bi]
        cosb = ap(tab, tc * 128, [[0, 4], [1, 128]])
        sse = ap(tab, 512 + tc * 128, [[0, 4], [2, 64]])
        sso = ap(tab, 512 + tc * 128 + 1, [[0, 4], [2, 64]])
        be = ap(bank, 0, [[128, 4], [2, 64]])
        bo = ap(bank, 1, [[128, 4], [2, 64]])
        r1e = ap(rt[1], 0, [[128, 4], [2, 64]])
        r1o = ap(rt[1], 1, [[128, 4], [2, 64]])
        sc.add("dve", lambda e: e.tensor_tensor(out=rt[0][:, :], in0=bank[:, :], in1=cosb, op=ALU.mult),
               reads=[("B", bi), "tabc"], writes=["rt0"])
        sc.add("dve", lambda e: e.tensor_tensor(out=r1e, in0=bo, in1=sse, op=ALU.mult),
               reads=[("B", bi), "tabs"], writes=["rt1e"])
        sc.add("dve", lambda e: e.tensor_tensor(out=r1o, in0=be, in1=sso, op=ALU.mult),
               reads=[("B", bi), "tabs"], writes=["rt1o"])
        sc.add("pool", lambda e: e.tensor_tensor(out=rt[0][:, :], in0=rt[0][:, :], in1=rt[1][:, :], op=ALU.add),
               reads=["rt0", "rt1e", "rt1o"], writes=["rt0"])
        page = tc if is_q else 4 + tc
        dec = ap(small, 0 if is_q else 4, [[1, 4], [0, 128]])
        sc.add("pool", lambda e: e.tensor_tensor(out=U[:, page * 512:(page + 1) * 512], in0=rt[0][:, :], in1=dec,
                                                  op=ALU.mult),
               reads=["rt0"], writes=[("U", page)])
        return page

    tb_rr = {"i": 0}

    def transpose4(src_page_ap_fn, src_keys, dst_ap, dst_keys, evac_eng):
        ti = tb_rr["i"] % 2
        tb_rr["i"] += 1
        tb = TB[ti]

        def f(e):
            ins = None
            for h in range(4):
                ins = e.transpose(out=tb[:, h * 128:(h + 1) * 128], in_=src_page_ap_fn(h), identity=ident[:, :])
            return ins
        sc.add("pe", f, reads=src_keys, writes=[("T", ti)])
        src = ap(tb, 0, [[128, 4], [1, 128]])
        if evac_eng == "act":
            sc.add("act", lambda e: e.copy(out=dst_ap, in_=src), reads=[("T", ti)], writes=dst_keys)
        else:
            sc.add(evac_eng, lambda e: e.tensor_copy(out=dst_ap, in_=src), reads=[("T", ti)], writes=dst_keys)

    def inproj_stage(t, kind, hook_after_rv=None):
        tau = t
        n0 = 4 * tau
        deferred = []
        for tc, bi in tm_group(1):
            pg = rotary(tc, bi, False)
            if kind == "own":
                deferred.append(lambda tc=tc, pg=pg: transpose4(
                    lambda h, pg=pg: U[:, pg * 512 + h * 128: pg * 512 + (h + 1) * 128], [("U", pg)],
                    ap(U, 20 * 512 + tc * 128, [[512, 4], [1, 128]]), [("U", 20 + h) for h in range(4)], "act"))
        for tc, bi in tm_group(2):
            sc.add("act", lambda e, tc=tc, bi=bi: e.copy(out=U[:, (8 + tc) * 512:(9 + tc) * 512], in_=B[bi][:, :]),
                   reads=[("B", bi)], writes=[("U", 8 + tc)])
        for f_ in deferred:
            f_()
        deferred = []
        if hook_after_rv is not None:
            hook_after_rv()
        for tc in range(4):
            ri = tc % 2

            def fkv(e, tc=tc, ri=ri):
                ins = None
                for h in range(4):
                    ins = e.matmul(R[ri][:, h * 128:(h + 1) * 128],
                                   lhsT=U[:, (4 + tc) * 512 + h * 128:(4 + tc) * 512 + (h + 1) * 128],
                                   rhs=U[:, (8 + tc) * 512 + h * 128:(8 + tc) * 512 + (h + 1) * 128],
                                   start=True, stop=True)
                return ins
            sc.add("pe", fkv, reads=[("U", 4 + tc), ("U", 8 + tc)], writes=[("R", ri)])
            sc.add("dve", lambda e, ri=ri: e.tensor_tensor(out=Sf[:, :], in0=R[ri][:, :], in1=Sf[:, :], op=ALU.add),
                   reads=[("R", ri), "Sf"], writes=["Sf"])
            sc.add("dve", lambda e: e.tensor_tensor(out=Sf[:, :], in0=Sf[:, :], in1=g128[:, :], op=ALU.mult),
                   reads=["Sf"], writes=["Sf"])
            sl = (n0 + tc + 1) % 6
            sc.add("act", lambda e, sl=sl: e.copy(out=Sb[:, sl * 512:(sl + 1) * 512], in_=Sf[:, :]),
                   reads=["Sf"], writes=[("Sb", sl)])
        if kind == "halo_k":
            return
        if kind == "own":
            for tc, bi in tm_group(0):
                pg = rotary(tc, bi, True)
                deferred.append(lambda tc=tc, pg=pg: transpose4(
                    lambda h, pg=pg: U[:, pg * 512 + h * 128: pg * 512 + (h + 1) * 128], [("U", pg)],
                    ap(U, 16 * 512 + tc * 128, [[512, 4], [1, 128]]), [("U", 16 + h) for h in range(4)], "act"))
            for tc, bi in tm_group(3):
                sc.add("act", lambda e, tc=tc, bi=bi: e.activation(out=U[:, (12 + tc) * 512:(13 + tc) * 512],
                                                                    in_=B[bi][:, :], func=AF.Silu),
                       reads=[("B", bi)], writes=[("U", 12 + tc)])
                sc.add("pool", lambda e, tc=tc: e.tensor_tensor(
                    out=U[:, (12 + tc) * 512:(13 + tc) * 512], in0=U[:, (12 + tc) * 512:(13 + tc) * 512],
                    in1=rgain[:, :], op=ALU.mult), reads=[("U", 12 + tc)], writes=[("U", 12 + tc)])
            for f_ in deferred:
                f_()
            deferred = []
        s0 = (4 * tau) % NRING
        fm_list = [0, 1, 2, 3] if kind == "own" else [2, 3]
        for i in fm_list:
            slot = next_unit(UIDX[("FM", i)])
            for m in range(2):
                mc = 2 * i + m
                bi = nextbank()

                def fmm(e, slot=slot, m=m, bi=bi):
                    ins = None
                    for kc in range(8):
                        ins = e.matmul(B[bi][:, :], lhsT=wslice(slot, (m * 8 + kc) * 128, 128),
                                       rhs=xnT[:, kc * 512:(kc + 1) * 512], start=(kc == 0), stop=(kc == 7))
                    return ins
                sc.add("pe", fmm, reads=XNT + [("w", slot)], writes=[("B", bi)])
                if mc < 4:
                    for a_ in range(2):
                        hd_ = 2 * mc + a_
                        sc.add("act", lambda e, hd_=hd_, a_=a_, bi=bi: e.activation(
                            out=aqT[64 * a_:64 * a_ + 64, hd_ * 512:(hd_ + 1) * 512],
                            in_=B[bi][64 * a_:64 * a_ + 64, :], func=AF.Copy, scale=0.125),
                            reads=[("B", bi)], writes=[("aqT", hd_)])
                else:
                    j = mc - 4
                    off = j * NRING * 128 + s0 * 128
                    sc.add("dve", lambda e, off=off, bi=bi: e.tensor_copy(out=kring[:, off:off + 512], in_=B[bi][:, :]),
                           reads=[("B", bi)], writes=[("kr", j, s0 + c) for c in range(4)])
        for tc, bi in tm_group(4, strided=True):
            slot_v = s0 + tc
            dst = ap(vring, slot_v * VW, [[192, 4], [128, 2], [1, 64]])
            src = ap(B[bi], 0, [[128, 4], [64, 2], [1, 64]])
            sc.add("act", lambda e, dst=dst, src=src: e.copy(out=dst, in_=src), reads=[("B", bi)],
                   writes=[("vr", slot_v)])
            odst = ap(vring, slot_v * VW + 64, [[192, 4], [1, 64]])
            osrc_t = onesb if kind == "own" else flagsb
            osrc = ap(osrc_t, 0, [[0, 4], [1, 64]])
            sc.add("pool", lambda e, odst=odst, osrc=osrc: e.tensor_copy(out=odst, in_=osrc), reads=[],
                   writes=[("vr1", slot_v)])

    ybf = [sil_bf[k // 2][:, (k % 2) * 512:(k % 2) * 512 + 512] for k in range(4)]

    def retention_stage(t):
        n0 = 4 * t
        cb = ap(caus, 0, [[0, 4], [1, 128]])
        pend = []

        def sc_op(tc):
            def fsc(e, tc=tc):
                ins = None
                for h in range(4):
                    ins = e.matmul(TBf[0][:, h * 128:(h + 1) * 128],
                                   lhsT=U[:, (20 + h) * 512 + tc * 128:(20 + h) * 512 + (tc + 1) * 128],
                                   rhs=U[:, (16 + h) * 512 + tc * 128:(16 + h) * 512 + (tc + 1) * 128],
                                   start=True, stop=True)
                return ins
            sc.add("pe", fsc, reads=[("U", 16 + h) for h in range(8)], writes=[("T", 0)])
            sb_ = scb[tc % 2]
            sc.add("dve", lambda e, sb_=sb_: e.tensor_tensor(
                out=sb_[:, :], in0=TBf[0][:, :], in1=cb, op=ALU.mult),
                reads=[("T", 0)], writes=[("scb", tc % 2)])

        def out_op(tc):
            sb_ = scb[tc % 2]
            ri = tc % 2
            sl = (n0 + tc) % 6

            def fo(e, tc=tc, ri=ri, sb_=sb_, sl=sl):
                ins = None
                for h in range(4):
                    e.matmul(TBf[1][:, h * 128:(h + 1) * 128], lhsT=sb_[:, h * 128:(h + 1) * 128],
                             rhs=U[:, (8 + tc) * 512 + h * 128:(8 + tc) * 512 + (h + 1) * 128],
                             start=True, stop=False)
                    ins = e.matmul(TBf[1][:, h * 128:(h + 1) * 128],
                                   lhsT=U[:, (16 + h) * 512 + tc * 128:(16 + h) * 512 + (tc + 1) * 128],
                                   rhs=Sb[:, sl * 512 + h * 128: sl * 512 + (h + 1) * 128],
                                   start=False, stop=True)
                return ins
            sc.add("pe", fo, reads=[("scb", tc % 2), ("U", 8 + tc), ("Sb", sl)] + [("U", 16 + h) for h in range(4)],
                   writes=[("T", 1)])
            sc.add("act", lambda e: e.copy(out=retf[:, :], in_=TBf[1][:, :]), reads=[("T", 1)],
                   writes=["retf"])
            sc.add("act", lambda e: e.activation(out=sqf[:, :], in_=TBf[1][:, :], func=AF.Square),
                   reads=[("T", 1)], writes=["sqf"])
            sc.add("dve", lambda e: e.tensor_reduce(out=st[:, 0:4], in_=ap(retf, 0, [[128, 4], [1, 128]]),
                                                    axis=AX.X, op=ALU.add), reads=["retf"], writes=["st_a"])
            sc.add("dve", lambda e: e.tensor_reduce(out=st[:, 4:8], in_=ap(sqf, 0, [[128, 4], [1, 128]]),
                                                    axis=AX.X, op=ALU.add), reads=["sqf"], writes=["st_b"])
            sc.add("dve", lambda e: e.tensor_single_scalar(out=st[:, 8:16], in_=st[:, 0:8], scalar=1.0 / 128,
                                                           op=ALU.mult), reads=["st_a", "st_b"], writes=["st_c"])
            sc.add("dve", lambda e: e.tensor_tensor(out=st[:, 16:20], in0=st[:, 8:12], in1=st[:, 8:12], op=ALU.mult),
                   reads=["st_c"], writes=["st_d"])
            sc.add("dve", lambda e: e.tensor_tensor(out=st[:, 20:24], in0=st[:, 12:16], in1=st[:, 16:20],
                                                    op=ALU.subtract), reads=["st_c", "st_d"], writes=["st_e"])
            sc.add("dve", lambda e: e.tensor_single_scalar(out=st[:, 24:28], in_=st[:, 20:24], scalar=EPS, op=ALU.add),
                   reads=["st_e"], writes=["st_f"])
            sc.add("act", lambda e: e.activation(out=st[:, 24:28], in_=st[:, 24:28], func=AF.Sqrt),
                   reads=["st_f"], writes=["st_f"])
            sc.add("dve", lambda e: e.reciprocal(out=st[:, 24:28], in_=st[:, 24:28]),
                   reads=["st_f"], writes=["st_f"])
            mean_b = ap(st, 8, [[1, 4], [0, 128]])
            rstd_b = ap(st, 24, [[1, 4], [0, 128]])
            for h in range(4):
                sc.add("dve", lambda e, h=h: e.tensor_scalar(
                    out=retf[:, h * 128:(h + 1) * 128], in0=retf[:, h * 128:(h + 1) * 128],
                    scalar1=st[:, 8 + h:9 + h], scalar2=st[:, 24 + h:25 + h], op0=ALU.subtract, op1=ALU.mult),
                    reads=["retf", "st_c", "st_f"], writes=["retf"])
            yb = ybf[tc]
            ykeys = [("ybf", tc), ("sil", tc // 2)]
            sc.add("pool", lambda e, tc=tc, yb=yb: e.tensor_tensor(out=yb, in0=retf[:, :],
                                                                   in1=U[:, (12 + tc) * 512:(13 + tc) * 512], op=ALU.mult),
                   reads=["retf", ("U", 12 + tc)], writes=ykeys)
            base = (tc % 2) * 512
            sbt = sil_bf[tc // 2]
            pend.append(lambda tc=tc, sbt=sbt, base=base, ykeys=ykeys: transpose4(
                lambda h: sbt[:, base + h * 128: base + (h + 1) * 128], ykeys,
                ap(Vb, tc * 128, [[512, 4], [1, 128]]), [("V", h) for h in range(4)], "dve"))

        return sc_op, out_op, pend

    SBANKS = [(("B", 0), B[0]), (("B", 1), B[1]), (("R", 1), R[1])]
    TBf = [TB[0].bitcast(F32), TB[1].bitcast(F32)]

    def attention_stage(t, ret):
        ret_sc, ret_out, pend = ret
        units = [("n", 0, g) for g in range(4)] + [("n", -1, g) for g in range(4)] + [("f", d) for d in (-2, -3, -4)]
        NU = len(units)
        stream = [(hd, u) for hd in range(8) for u in range(NU)]
        hooks = {}

        def krcols(j, d, g):
            return j * NRING * 128 + ((t + d) % 5) * 512 + g

        def s_op(k):
            hd, u = stream[k]
            j = hd // 2
            un = units[u]
            sbk = k % 3
            skey, sbank = SBANKS[sbk]
            if un[0] == "n":
                _, d, g = un
                kc0 = krcols(j, d, g)
                rk = [("kr", j, ((t + d) % 5) * 4 + c) for c in range(4)]
                sc.add("pe", lambda e, kc0=kc0, sbank=sbank, hd=hd: e.matmul(
                    sbank[:, :], lhsT=ap(kring, kc0, [[4, 128]]), rhs=aqT[:, hd * 512:(hd + 1) * 512],
                    start=True, stop=True), reads=rk + [("aqT", hd)], writes=[skey])
                mk = masks[:, ((d + 1) * 4 + g) * 512:((d + 1) * 4 + g + 1) * 512]
            else:
                _, d = un
                rk = [("kr", j, ((t + d) % 5) * 4 + c) for c in range(4)]
                kcs = [krcols(j, d, g) for g in range(4)]

                def fs(e, kcs=kcs, sbank=sbank, hd=hd):
                    ins = None
                    for g in range(4):
                        ins = e.matmul(sbank[:, g * 128:(g + 1) * 128], lhsT=ap(kring, kcs[g], [[4, 128]]),
                                       rhs=ap(aqT, hd * 512 + g, [[4, 128]]), start=True, stop=True)
                    return ins
                sc.add("pe", fs, reads=rk + [("aqT", hd)], writes=[skey])
                fk = 0 if d == -4 else 1
                mk = ap(fmasks, fk * 128, [[0, 4], [1, 128]])
            ebk = k % 4
            sc.add("act", lambda e, ebk=ebk, sbank=sbank: e.activation(out=eb[ebk][:, :], in_=sbank[:, :], func=AF.Exp),
                   reads=[skey], writes=[("eb", ebk)])
            sc.add("dve", lambda e, ebk=ebk, mk=mk: e.tensor_tensor(
                out=eb[ebk][:, :], in0=eb[ebk][:, :], in1=mk, op=ALU.mult),
                reads=[("eb", ebk)], writes=[("eb", ebk)])

        def pv_op(k):
            hd, u = stream[k]
            j, a = hd // 2, hd % 2
            ob = 2 + hd % 2
            un = units[u]
            sbk = k % 4
            if un[0] == "n":
                _, d, g = un
                slot = ((t + d) % 5) * 4 + g
                vcol = slot * VW + j * 192 + 64 * a
                sc.add("pe", lambda e, vcol=vcol, sbk=sbk, u=u, ob=ob: e.matmul(
                    B[ob][:, :], lhsT=vring[:, vcol:vcol + 128], rhs=eb[sbk][:, :], start=(u == 0), stop=False),
                    reads=[("vr", slot), ("vr1", slot), ("eb", sbk)], writes=[("B", ob)])
            else:
                _, d = un
                slots = [((t + d) % 5) * 4 + g for g in range(4)]

                def fp(e, slots=slots, sbk=sbk, u=u, j=j, a=a, ob=ob):
                    ins = None
                    for g in range(4):
                        vcol = slots[g] * VW + j * 192 + 64 * a
                        ins = e.matmul(ap(B[ob], g, [[4, 128]]), lhsT=vring[:, vcol:vcol + 128],
                                       rhs=eb[sbk][:, g * 128:(g + 1) * 128], start=False,
                                       stop=(u == NU - 1 and g == 3))
                    return ins
                sc.add("pe", fp, reads=[("vr", sl_) for sl_ in slots] + [("vr1", sl_) for sl_ in slots] + [("eb", sbk)],
                       writes=[("B", ob)])

        def recip_fn(hd, piece):
            if piece != 0:
                return
            a = hd % 2
            ob = 2 + hd % 2
            d0_ = 64 if a == 0 else 0
            sc.add("act", lambda e, d0_=d0_, ob=ob: e.activation(
                out=lnd[d0_:d0_ + 64, :], in_=B[ob][d0_:d0_ + 64, :], func=AF.Ln),
                reads=[("B", ob)], writes=["lnd"])
            sc.add("act", lambda e, d0_=d0_: e.activation(
                out=Rb[d0_:d0_ + 64, :], in_=lnd[d0_:d0_ + 64, :], func=AF.Exp, scale=-1.0),
                reads=["lnd"], writes=[("Rb", p_) for p_ in range(4)])

        def post_fn(hd):
            j, a = hd // 2, hd % 2
            ob = 2 + hd % 2
            n0_ = 0 if a == 0 else 64
            ri = 0
            if a == 0:
                sc.add("pe", lambda e, ri=ri: e.matmul(R[ri][0:64, :], lhsT=ident[64:128, 64:128], rhs=Rb[64:128, :],
                                                       start=True, stop=True), reads=[("Rb", p_) for p_ in range(4)],
                       writes=[("R", ri)])
            else:
                sc.add("pe", lambda e, ri=ri: e.matmul(R[ri][:, :], lhsT=shif[0:64, :], rhs=Rb[0:64, :],
                                                       start=True, stop=True), reads=[("Rb", p_) for p_ in range(4)],
                       writes=[("R", ri)])
            sc.add("act", lambda e, ri=ri, n0_=n0_: e.copy(out=rsb[n0_:n0_ + 64, :], in_=R[ri][n0_:n0_ + 64, :]),
                   reads=[("R", ri)], writes=["rsb2"])
            pg = 4 + j
            sc.add("dve", lambda e, n0_=n0_, ob=ob, pg=pg: e.tensor_tensor(
                out=Vb[n0_:n0_ + 64, pg * 512:(pg + 1) * 512], in0=B[ob][n0_:n0_ + 64, :], in1=rsb[n0_:n0_ + 64, :],
                op=ALU.mult), reads=[("B", ob), "rsb2"], writes=[("V", pg)])

        nstream_ = len(stream)
        for hd in range(8):
            kend = hd * NU + NU - 1
            for piece in range(4):
                hooks.setdefault(kend + 2 + piece, []).append(lambda hd=hd, piece=piece: recip_fn(hd, piece))
            hooks.setdefault(kend + 8, []).append(lambda hd=hd: post_fn(hd))
            if hd < 4:
                k0 = hd * 2 * NU
                hooks.setdefault(k0 + 3, []).append(lambda hd=hd: ret_sc(hd))
                hooks.setdefault(k0 + 7, []).append(lambda hd=hd: ret_out(hd))
                hooks.setdefault(k0 + 2 * NU + 1, []).append(lambda hd=hd: pend[hd]())
        for k in range(3):
            s_op(k)
        for k in range(nstream_):
            if k + 3 < nstream_:
                s_op(k + 3)
            pv_op(k)
            for f_ in hooks.pop(k, []):
                f_()
        for k in sorted(hooks):
            for f_ in hooks[k]:
                f_()

    def outproj_stage(par):
        x = xt[par]
        vk = [("V", i) for i in range(8)]
        for cg in range(2):
            s0 = next_unit(UIDX[("OUT", cg, 0)])
            s1 = next_unit(UIDX[("OUT", cg, 1)])
            for tc in range(4):
                def fmm(e, tc=tc, s0=s0, s1=s1):
                    ins = None
                    for kc in range(8):
                        s = s0 if kc < 4 else s1
                        ins = e.matmul(B[tc][:, :], lhsT=Vb[:, kc * 512 + tc * 128: kc * 512 + (tc + 1) * 128],
                                       rhs=wslice(s, (kc % 4) * 512, 512), start=(kc == 0), stop=(kc == 7))
                    return ins
                sc.add("pe", fmm, reads=vk + [("w", s0), ("w", s1)], writes=[("B", tc)])
            for tc in range(4):
                xs = x[:, tc * 1024 + cg * 512: tc * 1024 + (cg + 1) * 512]
                sc.add("dve", lambda e, tc=tc, xs=xs: e.tensor_tensor(out=xs, in0=B[tc][:, :], in1=xs, op=ALU.add),
                       reads=[("B", tc), ("xt", par, tc)], writes=[("xt", par, tc)])

    def load_x(t):
        par = t % 2
        src = xin[t * T:(t + 1) * T, :].rearrange("(tc p) d -> p tc d", p=128)
        dst = xt[par][:, :].rearrange("p (tc d) -> p tc d", tc=4)
        sc.add("sp", lambda e: e.dma_start(out=dst, in_=src), writes=xkeys(par), dma=("xt", par))

    def load_tab(t):
        srcc = cosd[t * T:(t + 1) * T, :].rearrange("(tc p) d -> p tc d", p=128)
        srcs = ssd[t * T:(t + 1) * T, :].rearrange("(tc p) d -> p tc d", p=128)
        sc.add("sp", lambda e: e.dma_start(out=tab[:, 0:512].rearrange("p (tc d) -> p tc d", tc=4), in_=srcc),
               writes=["tabc"], dma="tabc")
        sc.add("sp", lambda e: e.dma_start(out=tab[:, 512:1024].rearrange("p (tc d) -> p tc d", tc=4), in_=srcs),
               writes=["tabs"], dma="tabs")

    load_x(0)
    if ntile > 1:
        load_x(1)
    nxt = {}
    defer_x = []
    store_q = []

    def hoist_chain(tn):
        nxt["gs"] = nxt.pop("pre") if "pre" in nxt else load_gain(0)
        norm_stage(tn % 2, nxt["gs"], chain_only=True)

    hoist_chain(0)
    norm_transposes()
    for t in range(ntile):
        par = t % 2
        kind = tile_kind(t)
        load_tab(t)
        def fh_():
            if store_q:
                store_q.pop(0)()
            nxt["gs2"] = load_gain(1)
        ffn_stage(par, 0, early_hook=fh_)
        gs2 = nxt["gs2"]
        if t == 0:
            dump("h1", xt[par][:, 0:1024], xkeys(par))
        norm_stage(par, gs2)
        last = (t + 1 == ntile)
        if kind != "own":
            inproj_stage(t, kind, hook_after_rv=(None if last else (lambda: hoist_chain(t + 1))))
            if not last:
                if t + 2 < ntile:
                    load_x(t + 2)
                norm_transposes()
            continue
        inproj_stage(t, kind)
        if defer_x:
            load_x(defer_x.pop(0))
        gs3 = load_gain(2)
        pend = retention_stage(t)
        attention_stage(t, pend)
        outproj_stage(par)
        norm_stage(par, gs3)
        gs4 = load_gain(3)
        ffn_stage(par, 1, gu_hook=(None if last else (lambda: hoist_chain(t + 1))),
                  mid_hook=(None if last else norm_transposes),
                  early_hook=(None if last else (lambda: nxt.__setitem__("pre", load_gain(0)))))
        norm_stage(par, gs4, final=True)
        ot = t - nh
        dst = yout[ot * T:(ot + 1) * T, :].rearrange("(tc p) d -> p tc d", p=128)
        src = xt[par][:, :].rearrange("p (tc d) -> p tc d", tc=4)
        store_q.append(lambda dst=dst, src=src, par=par: sc.add(
            "sp", lambda e: e.dma_start(out=dst, in_=src), reads=xkeys(par), dma=("xt_out", par)))
        if last:
            store_q.pop(0)()
        if t + 2 < ntile:
            defer_x.append(t + 2)
    sc.add("sp", lambda e: e.nop(), writes=xkeys(0) + xkeys(1))

    with nc.allow_low_precision("bf16 matmul operands by design; fp32 accumulation"):
        with nc.Block() as block:
            sc.emit(nc, block)
    return nc


_CACHE = {}


def _prep_core(x, b, s, half, common):
    own = x[b, s * half:(s + 1) * half]
    if s == 0:
        halo = np.zeros_like(own)
        flag = 0.0
    else:
        halo = x[b, (s - 1) * half:s * half]
        flag = 1.0
    xin = np.ascontiguousarray(np.concatenate([halo, own], axis=0))
    pos = np.arange((s - 1) * half, (s + 1) * half)
    cos, ssn = _rope_tables(pos)
    m = dict(common)
    m.update(xin=xin, cosd=cos, ssd=ssn, flags=np.full((128, 64), flag, np.float32))
    return m


def run(x, params, nh, no, dbg=()):
    key = (nh, no, tuple(dbg))
    if key not in _CACHE:
        _CACHE[key] = build(nh, no, dbg)
    nc = _CACHE[key]
    cst = _consts()
    f32 = lambda a: np.ascontiguousarray(np.asarray(a, np.float32))
    gains = np.stack([np.broadcast_to(f32(params[k]).reshape(1, D), (128, D))
                      for k in ("norm_ffn1", "norm_mix", "norm_ffn2", "norm_final")]).copy()
    common = dict(
        w_g1=f32(params["ffn1_w_gate"][0]), w_u1=f32(params["ffn1_w_up"][0]), w_d1=f32(params["ffn1_w_down"][0]),
        w_g2=f32(params["ffn2_w_gate"][0]), w_u2=f32(params["ffn2_w_up"][0]), w_d2=f32(params["ffn2_w_down"][0]),
        w_in=f32(params["w_in"][0]), w_out=f32(params["w_out"][0]), gains=gains,
        rgain=np.broadcast_to(f32(params["ret_norm_gain"]).reshape(1, 512), (128, 512)).copy(),
        small=cst["small"], g128=cst["g128"], caus=cst["caus"], ident=cst["ident"], shif=cst["shif"],
        masks=cst["masks"], fmasks=cst["fmasks"],
    )
    half = no * T
    nb = x.shape[0]
    in_maps = []
    for c in range(8):
        b, s = (c // 2) % nb, c % 2
        in_maps.append(_prep_core(x, b, s, half, common))
    res = run_bass_kernel_spmd(nc, in_maps, core_ids=list(range(8)))
    return res


def kernel(x, norm_ffn1, ffn1_w_gate, ffn1_w_up, ffn1_w_down, norm_mix, w_in, ret_norm_gain, w_out,
           norm_ffn2, ffn2_w_gate, ffn2_w_up, ffn2_w_down, norm_final):
    x = np.asarray(x, np.float32)
    params = dict(norm_ffn1=norm_ffn1, ffn1_w_gate=ffn1_w_gate, ffn1_w_up=ffn1_w_up, ffn1_w_down=ffn1_w_down,
                  norm_mix=norm_mix, w_in=w_in, ret_norm_gain=ret_norm_gain, w_out=w_out, norm_ffn2=norm_ffn2,
                  ffn2_w_gate=ffn2_w_gate, ffn2_w_up=ffn2_w_up, ffn2_w_down=ffn2_w_down, norm_final=norm_final)
    params = {k: np.asarray(v) for k, v in params.items()}
    res = run(x, params, 8, 8)
    out = np.empty((4, 8192, D), np.float32)
    for c in range(8):
        b, s = c // 2, c % 2
        out[b, s * 4096:(s + 1) * 4096] = np.asarray(res.results[c]["yout"], np.float32)
    return out
```
